# Optimizing a Trainium2 kernel written in Bass

```python
import math
import jax, jax.numpy as jnp
from jax import lax
import numpy as np

D_MODEL = 1024
BATCH = 4
SEQ = 4096
DEPTH = 2

GRID_W = 64
CTX_LEN = 256
N_EVEN = (DEPTH + 1) // 2
N_ODD = DEPTH // 2

S5_WIDTH = D_MODEL // 2
S5_GROUP = 16
S5_GROUPS = S5_WIDTH // S5_GROUP
S5_STATE = 64
SGU_WIDTH = D_MODEL // 2
SGU_HEADS = 8
SGU_HEAD_DIM = SGU_WIDTH // SGU_HEADS
CHUNK = 128
EVEN_IN = 2 * S5_WIDTH + 3 * SGU_WIDTH
EVEN_SPLITS = [S5_WIDTH, 2 * S5_WIDTH, 2 * S5_WIDTH + SGU_WIDTH, 2 * S5_WIDTH + 2 * SGU_WIDTH]
EVEN_OUT = S5_WIDTH + SGU_WIDTH
CONV_WIDTH = D_MODEL
CONV_K = 31
ODD_IN = 3 * CONV_WIDTH
DN_ALPHA = (2 * DEPTH) ** 0.25
DN_BETA = (8 * DEPTH) ** -0.25
LN_EPS = 1e-5

kernel_name = "hybrid_s5_sgu_conformer_prefix_dit"


def layer_norm(x, g, b):
    xf = x.astype(jnp.float32)
    mu = jnp.mean(xf, axis=-1, keepdims=True)
    var = jnp.mean(jnp.square(xf - mu), axis=-1, keepdims=True)
    y = (xf - mu) * lax.rsqrt(var + LN_EPS) * g.astype(jnp.float32) + b.astype(jnp.float32)
    return y.astype(x.dtype)


def adaln(cond, mod_w, mod_b):
    m = jax.nn.silu(cond) @ mod_w + mod_b
    return jnp.split(m, 3, axis=-1)


def modulate(x, shift, scale):
    return x * (1 + scale) + shift


def s5_discretise(lam_re, lam_im, log_dt, b_re, b_im):
    lam = lax.complex(lam_re.astype(jnp.float32), lam_im.astype(jnp.float32))
    dt = jnp.exp(log_dt.astype(jnp.float32))[:, None]
    lam_bar = jnp.exp(lam * dt)
    b = lax.complex(b_re.astype(jnp.float32), b_im.astype(jnp.float32))
    b_bar = ((lam_bar - 1) / lam)[..., None] * b
    return lam_bar, b_bar


def s5_scan(lam_bar, bu):
    a = jnp.broadcast_to(lam_bar, bu.shape)

    def combine(e1, e2):
        a1, b1 = e1
        a2, b2 = e2
        return a1 * a2, a2 * b1 + b2

    _, h = lax.associative_scan(combine, (a, bu), axis=1)
    return h


def s5_mixer(u_lat, u_ctx, ctx_out, lam_re, lam_im, log_dt, b_re, b_im, c_re, c_im, d_skip):
    bsz, n, _ = u_lat.shape
    n_c = u_ctx.shape[1]
    ul = u_lat.astype(jnp.float32).reshape(bsz, n, S5_GROUPS, S5_GROUP)
    uc = u_ctx.astype(jnp.float32).reshape(bsz, n_c, S5_GROUPS, S5_GROUP)
    d = d_skip.astype(jnp.float32).reshape(S5_GROUPS, S5_GROUP)
    y_lat = d * ul
    y_ctx = d * uc if ctx_out else None
    for dirn in range(2):
        lam_bar, b_bar = s5_discretise(lam_re[dirn], lam_im[dirn], log_dt[dirn], b_re[dirn], b_im[dirn])
        c_mat = lax.complex(c_re[dirn].astype(jnp.float32), c_im[dirn].astype(jnp.float32))
        bu_l = jnp.einsum('blgh,gph->blgp', ul, b_bar)
        bu_c = jnp.einsum('blgh,gph->blgp', uc, b_bar)
        if dirn == 1:
            bu_l, bu_c = bu_l[:, ::-1], bu_c[:, ::-1]
        h_c = s5_scan(lam_bar, bu_c)
        h_l = s5_scan(lam_bar, bu_l.at[:, 0].add(lam_bar * h_c[:, -1]))
        if dirn == 1:
            h_l, h_c = h_l[:, ::-1], h_c[:, ::-1]
        y_lat = y_lat + jnp.einsum('blgp,ghp->blgh', h_l, c_mat).real
        if ctx_out:
            y_ctx = y_ctx + jnp.einsum('blgp,ghp->blgh', h_c, c_mat).real
    y_lat = y_lat.reshape(bsz, n, S5_WIDTH).astype(u_lat.dtype)
    if not ctx_out:
        return y_lat, None
    return y_lat, y_ctx.reshape(bsz, n_c, S5_WIDTH).astype(u_ctx.dtype)


def s5_glu(y, glu_w, glu_b):
    y = jax.nn.gelu(y)
    return y * jax.nn.sigmoid(y @ glu_w + glu_b)


def sgu(u, v, ln_g, ln_b, w_s, b_s):
    bsz, n, _ = u.shape
    u = jax.nn.gelu(u)
    v = layer_norm(jax.nn.gelu(v), ln_g, ln_b)
    vc = v.reshape(bsz, n // CHUNK, CHUNK, SGU_HEADS, SGU_HEAD_DIM)
    s = jnp.einsum('hqk,bckhd->bcqhd', w_s, vc) + b_s.T[:, :, None]
    return u * s.reshape(bsz, n, SGU_WIDTH)


def depthwise2d(x, k):
    return lax.conv_general_dilated(x, k[:, :, None, :], (1, 1), 'SAME',
                                    dimension_numbers=('NHWC', 'HWIO', 'NHWC'),
                                    feature_group_count=x.shape[-1])


def conformer_conv(a, g, dw_w, dw_b, ln_g, ln_b, on_grid):
    h = a * jax.nn.sigmoid(g)
    bsz, n, cw = h.shape
    k = dw_w.astype(h.dtype)
    if on_grid:
        rows = n // GRID_W
        h4 = h.reshape(bsz, rows, GRID_W, cw)
        half = cw // 2
        along_row = depthwise2d(h4[..., :half], k[None, :, :half])
        along_col = depthwise2d(h4[..., half:], k[:, None, half:])
        h = jnp.concatenate([along_row, along_col], axis=-1).reshape(bsz, n, cw)
    else:
        h = depthwise2d(h[:, None], k[None])[:, 0]
    h = layer_norm(h + dw_b, ln_g, ln_b)
    return jax.nn.silu(h)


def even_layer(x, xc, c, c_ctx, need_ctx, mod_w, mod_b, norm_g, norm_b, w_in, w_out,
               lam_re, lam_im, log_dt, b_re, b_im, c_re, c_im, d_skip, glu_w, glu_b,
               sgu_ln_g, sgu_ln_b, sgu_w, sgu_b):
    shift, scale, gate = adaln(c, mod_w, mod_b)
    h = modulate(x, shift[:, None], scale[:, None])
    ua, za, ub, vb, zb = jnp.split(h @ w_in, EVEN_SPLITS, axis=-1)
    shift_c, scale_c, gate_c = adaln(c_ctx, mod_w, mod_b)
    hc = modulate(xc, shift_c, scale_c)
    if need_ctx:
        ua_c, za_c, ub_c, vb_c, zb_c = jnp.split(hc @ w_in, EVEN_SPLITS, axis=-1)
    else:
        ua_c = hc @ w_in[:, :S5_WIDTH]
    s_lat, s_ctx = s5_mixer(ua, ua_c, need_ctx, lam_re, lam_im, log_dt, b_re, b_im, c_re, c_im, d_skip)
    y = jnp.concatenate([s5_glu(s_lat, glu_w, glu_b) * jax.nn.silu(za),
                         sgu(ub, vb, sgu_ln_g, sgu_ln_b, sgu_w, sgu_b) * jax.nn.silu(zb)], axis=-1) @ w_out
    x_new = layer_norm(DN_ALPHA * x + gate[:, None] * y, norm_g, norm_b)
    if not need_ctx:
        return x_new, None
    yc = jnp.concatenate([s5_glu(s_ctx, glu_w, glu_b) * jax.nn.silu(za_c),
                          sgu(ub_c, vb_c, sgu_ln_g, sgu_ln_b, sgu_w, sgu_b) * jax.nn.silu(zb_c)], axis=-1) @ w_out
    xc_new = layer_norm(DN_ALPHA * xc + gate_c * yc, norm_g, norm_b)
    return x_new, xc_new


def odd_layer(x, xc, c, c_ctx, need_ctx, mod_w, mod_b, norm_g, norm_b, w_in, w_out,
              dw_w, dw_b, ln_g, ln_b):
    def branch(h, on_grid):
        a, g, z = jnp.split(h @ w_in, 3, axis=-1)
        return (conformer_conv(a, g, dw_w, dw_b, ln_g, ln_b, on_grid) * jax.nn.silu(z)) @ w_out

    shift, scale, gate = adaln(c, mod_w, mod_b)
    y = branch(modulate(x, shift[:, None], scale[:, None]), True)
    x_new = layer_norm(DN_ALPHA * x + gate[:, None] * y, norm_g, norm_b)
    if not need_ctx:
        return x_new, None
    shift_c, scale_c, gate_c = adaln(c_ctx, mod_w, mod_b)
    yc = branch(modulate(xc, shift_c, scale_c), False)
    xc_new = layer_norm(DN_ALPHA * xc + gate_c * yc, norm_g, norm_b)
    return x_new, xc_new


def setup_inputs(seed: int = 0) -> dict:
    key = jax.random.key(seed)
    ks = iter(jax.random.split(key, 40))
    nrm = lambda shape: jax.random.normal(next(ks), shape, jnp.float32)
    D = D_MODEL
    G, P, H = S5_GROUPS, S5_STATE, S5_GROUP
    n_idx = jnp.arange(P, dtype=jnp.float32)
    return {
        "x": nrm((BATCH, SEQ, D)),
        "c": nrm((BATCH, D)),
        "ctx": nrm((BATCH, CTX_LEN, D)),
        "c_ctx": nrm((D,)),
        "mod_w": nrm((DEPTH, D, 3 * D)) * (0.5 * D ** -0.5),
        "mod_b": nrm((DEPTH, 3 * D)) * 0.01,
        "norm_g": 1.0 + 0.01 * nrm((DEPTH, D)),
        "norm_b": 0.01 * nrm((DEPTH, D)),
        "ev_w_in": nrm((N_EVEN, D, EVEN_IN)) * D ** -0.5,
        "ev_w_out": nrm((N_EVEN, EVEN_OUT, D)) * (EVEN_OUT ** -0.5 * DN_BETA),
        "s5_lam_re": -0.5 + 0.01 * nrm((N_EVEN, 2, G, P)),
        "s5_lam_im": math.pi * n_idx + 0.01 * nrm((N_EVEN, 2, G, P)),
        "s5_log_dt": jax.random.uniform(next(ks), (N_EVEN, 2, G), jnp.float32, math.log(1e-3), math.log(1e-1)),
        "s5_b_re": nrm((N_EVEN, 2, G, P, H)) * (2 * H) ** -0.5,
        "s5_b_im": nrm((N_EVEN, 2, G, P, H)) * (2 * H) ** -0.5,
        "s5_c_re": nrm((N_EVEN, 2, G, H, P)) * (2 * P) ** -0.5,
        "s5_c_im": nrm((N_EVEN, 2, G, H, P)) * (2 * P) ** -0.5,
        "s5_d": nrm((N_EVEN, S5_WIDTH)),
        "glu_w": nrm((N_EVEN, S5_WIDTH, S5_WIDTH)) * S5_WIDTH ** -0.5,
        "glu_b": 0.01 * nrm((N_EVEN, S5_WIDTH)),
        "sgu_ln_g": 1.0 + 0.01 * nrm((N_EVEN, SGU_WIDTH)),
        "sgu_ln_b": 0.01 * nrm((N_EVEN, SGU_WIDTH)),
        "sgu_w": nrm((N_EVEN, SGU_HEADS, CHUNK, CHUNK)) * CHUNK ** -0.5,
        "sgu_b": 1.0 + 0.01 * nrm((N_EVEN, SGU_HEADS, CHUNK)),
        "od_w_in": nrm((N_ODD, D, ODD_IN)) * D ** -0.5,
        "od_w_out": nrm((N_ODD, CONV_WIDTH, D)) * (CONV_WIDTH ** -0.5 * DN_BETA),
        "dw_w": nrm((N_ODD, CONV_K, CONV_WIDTH)) * CONV_K ** -0.5,
        "dw_b": 0.01 * nrm((N_ODD, CONV_WIDTH)),
        "conv_ln_g": 1.0 + 0.01 * nrm((N_ODD, CONV_WIDTH)),
        "conv_ln_b": 0.01 * nrm((N_ODD, CONV_WIDTH)),
    }


def reference(x, c, ctx, c_ctx, mod_w, mod_b, norm_g, norm_b, ev_w_in, ev_w_out,
              s5_lam_re, s5_lam_im, s5_log_dt, s5_b_re, s5_b_im, s5_c_re, s5_c_im, s5_d,
              glu_w, glu_b, sgu_ln_g, sgu_ln_b, sgu_w, sgu_b,
              od_w_in, od_w_out, dw_w, dw_b, conv_ln_g, conv_ln_b):
    xc = ctx
    for l in range(DEPTH):
        need_ctx = any(j % 2 == 0 for j in range(l + 1, DEPTH))
        i = l // 2
        if l % 2 == 0:
            x, xc = even_layer(x, xc, c, c_ctx, need_ctx, mod_w[l], mod_b[l], norm_g[l], norm_b[l],
                               ev_w_in[i], ev_w_out[i], s5_lam_re[i], s5_lam_im[i], s5_log_dt[i],
                               s5_b_re[i], s5_b_im[i], s5_c_re[i], s5_c_im[i], s5_d[i],
                               glu_w[i], glu_b[i], sgu_ln_g[i], sgu_ln_b[i], sgu_w[i], sgu_b[i])
        else:
            x, xc = odd_layer(x, xc, c, c_ctx, need_ctx, mod_w[l], mod_b[l], norm_g[l], norm_b[l],
                              od_w_in[i], od_w_out[i], dw_w[i], dw_b[i], conv_ln_g[i], conv_ln_b[i])
    return x
```

```python
import math
import contextlib
import numpy as np
import concourse.bass as bass
import concourse.mybir as mybir
from concourse.bass_utils import run_bass_kernel_spmd

F32 = mybir.dt.float32
BF16 = mybir.dt.bfloat16
ALU = mybir.AluOpType
AF = mybir.ActivationFunctionType

NDMASEM = 12
W = 32
T = 8
G = 32
J_OWN = 256
J_CTX = 32


class Prog:
    ENGS = ["sync", "scalar", "vector", "gpsimd", "tensor"]

    def __init__(self, nc, dma_queues=("sync", "gpsimd")):
        self.nc = nc
        self.ops = {e: [] for e in self.ENGS}
        self.cnt = {}
        self.last_w = {}
        self.readers = {}
        self.seen = {e: {} for e in self.ENGS}
        self.dma_i = {q: 0 for q in dma_queues}
        self.semnames = list(self.ENGS)
        for q in dma_queues:
            for i in range(NDMASEM):
                self.semnames.append(f"{q}_d{i}")
        for s in self.semnames:
            self.cnt[s] = 0
        self.nops = 0

    def op(self, eng, fn, reads=(), writes=(), dma=False):
        reads = [reads] if isinstance(reads, str) else list(reads)
        writes = [writes] if isinstance(writes, str) else list(writes)
        deps = {}

        def add(sv):
            if sv is not None and sv[1] > deps.get(sv[0], 0):
                deps[sv[0]] = sv[1]

        for k in reads:
            add(self.last_w.get(k))
        for k in writes:
            add(self.last_w.get(k))
            for r in self.readers.get(k, ()):
                add(r)
        if dma:
            i = self.dma_i[eng]
            self.dma_i[eng] += 1
            sem = f"{eng}_d{i % NDMASEM}"
            inc = 16
            if self.cnt[sem] > 0:
                deps[sem] = max(deps.get(sem, 0), self.cnt[sem])
        else:
            sem = eng
            inc = 1
        waits = []
        for s, v in deps.items():
            if s == "tensor" and eng == "tensor":
                continue
            if v > self.seen[eng].get(s, 0):
                waits.append((s, v))
                self.seen[eng][s] = v
        self.cnt[sem] += inc
        val = self.cnt[sem]
        self.ops[eng].append((waits, fn, sem, inc))
        for k in writes:
            self.last_w[k] = (sem, val)
            self.readers[k] = []
        for k in reads:
            self.readers.setdefault(k, []).append((sem, val))
        self.nops += 1

    def cc(self, fn, sem, reads, writes):
        deps = {}
        for kx in reads:
            sv = self.last_w.get(kx)
            if sv is not None and sv[1] > deps.get(sv[0], 0):
                deps[sv[0]] = sv[1]
        waits = []
        for s_, v in deps.items():
            if v > self.seen["gpsimd"].get(s_, 0):
                waits.append((s_, v))
                self.seen["gpsimd"][s_] = v
        self.ops["gpsimd"].append((waits, fn, sem, 1))
        self.cnt[sem] = 1
        for kx in writes:
            self.last_w[kx] = (sem, 1)
            self.readers[kx] = []
        self.nops += 1

    barrier_skip = ()

    def barrier(self):
        skip = set(self.barrier_skip)
        for eng in self.ENGS:
            waits = [(s, v) for s, v in self.cnt.items() if s not in skip and v > self.seen[eng].get(s, 0)]
            for s, v in waits:
                self.seen[eng][s] = v
            if waits:
                self.ops[eng].append((waits, None, None, 0))
        self.last_w = {k_: sv for k_, sv in self.last_w.items() if sv[0] in skip}
        self.readers = {}

    def emit(self, sems, block):
        prog = self

        def replay(engname, eng):
            for waits, fn, sem, inc in prog.ops[engname]:
                for s, v in waits:
                    eng.wait_ge(sems[s], v)
                if fn is not None:
                    fn(eng).then_inc(sems[sem], inc)

        @block.sync
        def _(e):
            replay("sync", e)

        @block.scalar
        def _(e):
            replay("scalar", e)

        @block.vector
        def _(e):
            replay("vector", e)

        @block.gpsimd
        def _(e):
            replay("gpsimd", e)

        @block.tensor
        def _(e):
            replay("tensor", e)


class KB:
    def __init__(self, nc, es, f32_elems, bf_elems):
        self.nc = nc
        self.es = es
        self.P = Prog(nc)
        self.sems = {s: es.enter_context(nc.semaphore(s)) for s in self.P.semnames}
        self.af = es.enter_context(nc.sbuf_tensor("arena_f", [128, f32_elems], F32))
        self.ab = es.enter_context(nc.sbuf_tensor("arena_b", [128, bf_elems], BF16))
        self.pf = 0
        self.pb = 0
        self.capf = f32_elems
        self.ptop = f32_elems
        self.capb = bf_elems
        self.ps = [es.enter_context(nc.psum_tensor(f"ps{i}", [128, 512], F32)) for i in range(8)]
        self.uid = 0

    def boot_clear(self):
        for s in self.sems.values():
            self.nc.gpsimd.sem_clear(s)
        self.nc.all_engine_barrier()

    def f32t(self, n):
        self.ptop -= n
        assert self.ptop >= self.pf, ("f32 arena overflow (top)", self.ptop, self.pf)
        return self.af[:, self.ptop:self.ptop + n]

    def mark(self):
        return (self.pf, self.pb)

    def release(self, m):
        self.pf, self.pb = m

    def f32(self, n):
        a = self.af[:, self.pf:self.pf + n]
        self.pf += n
        assert self.pf <= self.ptop, ("f32 arena overflow", self.pf, self.ptop)
        return a

    def bf(self, n):
        n = (n + 1) // 2 * 2
        a = self.ab[:, self.pb:self.pb + n]
        self.pb += n
        assert self.pb <= self.capb, ("bf16 arena overflow", self.pb)
        return a

    def key(self, pfx="k"):
        self.uid += 1
        return f"{pfx}{self.uid}"

    def v(self, fn, r=(), w=()):
        self.P.op("vector", fn, r, w)

    def g(self, fn, r=(), w=()):
        self.P.op("gpsimd", fn, r, w)

    def a(self, fn, r=(), w=()):
        self.P.op("scalar", fn, r, w)

    def t(self, fn, r=(), w=()):
        self.P.op("tensor", fn, r, w)

    def cc(self, fn, sem, r, w):
        self.P.cc(fn, sem, r, w)

    def dma(self, out, in_, r=(), w=(), q="sync"):
        self.P.op(q, lambda e: e.dma_start(out=out, in_=in_), r, w, dma=True)

    def tt(self, eng, out, a, b, op, r, w):
        self.P.op(eng, lambda e: e.tensor_tensor(out=out, in0=a, in1=b, op=op), r, w)

    def cmul(self, ore, oim, are, aim, bre, bim, t1, t2, r, w, kt, engs=("vector", "gpsimd")):
        e0, e1 = engs
        k = [kt + "_t%d" % i for i in range(4)]
        self.tt(e0, t1[0], are, bre, ALU.mult, r, [k[0]])
        self.tt(e0, t2[0], aim, bim, ALU.mult, r, [k[1]])
        self.tt(e0, ore, t1[0], t2[0], ALU.subtract, [k[0], k[1]], w)
        self.tt(e1, t1[1], are, bim, ALU.mult, r, [k[2]])
        self.tt(e1, t2[1], aim, bre, ALU.mult, r, [k[3]])
        self.tt(e1, oim, t1[1], t2[1], ALU.add, [k[2], k[3]], w)


def r3(ap, a, b):
    return ap.rearrange("p (a b) -> p a b", a=a, b=b)


def r4(ap, a, b, c):
    return ap.rearrange("p (a b c) -> p a b c", a=a, b=b, c=c)


def bc_last(ap2, n):
    return ap2.unsqueeze(2).broadcast_to([ap2.shape[0], ap2.shape[1], n])


def s5_pre_init(kb, din, need_out=True):
    P = kb.P
    out = {}
    B8 = kb.bf(G * 2 * 128)
    out["B8"] = r4(B8, G, 2, 128)
    if need_out:
        C8b = kb.bf(2 * G * 128)
        out["C8"] = r4(C8b, 2, G, 128)
        Toep = kb.bf(G * 128)
        out["Toep"] = r3(Toep, G, 128)
    tabs = {}
    for nm in ["Tpre", "Tpim", "Ture", "Tuim"]:
        tabs[nm] = r3(kb.f32(G * W), G, W)
    out.update(tabs)
    rmask = kb.f32(W)
    out["rmask"] = rmask
    ptop0 = kb.ptop
    sm = {}
    for nm in ["lamre", "lamim", "logdt", "dT"]:
        sm[nm] = kb.f32t(G)
        kb.dma(sm[nm], din[nm], w=[nm])
    big = {}
    for nm in ["bre", "bim", "cre", "cim"]:
        big[nm] = r3(kb.f32t(G * 16), G, 16)
        kb.dma(big[nm], din[nm].rearrange("p (g h) -> p g h", g=G), w=[nm])
    ident = kb.f32t(128)
    kb.dma(ident, din["ident"], w=["ident"])
    maskF = kb.f32t(128)
    maskB = kb.f32t(128)
    kb.dma(maskF, din["maskF"], w=["maskF"])
    kb.dma(maskB, din["maskB"], w=["maskB"])
    S = lambda: kb.f32t(G)
    dt, are, th, mag, c, s, cc, ss, cs = [S() for _ in range(9)]
    kb.a(lambda e: e.activation(out=dt, in_=sm["logdt"], func=AF.Exp), ["logdt"], ["dt"])
    kb.tt("vector", are, sm["lamre"], dt, ALU.mult, ["lamre", "dt"], ["are"])
    kb.tt("vector", th, sm["lamim"], dt, ALU.mult, ["lamim", "dt"], ["th"])
    kb.a(lambda e: e.activation(out=mag, in_=are, func=AF.Exp), ["are"], ["mag"])
    NSQ = 6
    halfpi = kb.f32t(1)
    kb.v(lambda e: e.memset(halfpi, math.pi / 2), [], ["halfpi"])
    kb.a(lambda e: e.activation(out=s, in_=th, func=AF.Sin, scale=1.0 / 2 ** NSQ), ["th"], ["s"])
    kb.a(lambda e: e.activation(out=c, in_=th, func=AF.Sin, scale=1.0 / 2 ** NSQ, bias=halfpi), ["th", "halfpi"], ["c"])
    for _ in range(NSQ):
        kb.tt("vector", cc, c, c, ALU.mult, ["c"], ["cc"])
        kb.tt("vector", ss, s, s, ALU.mult, ["s"], ["ss"])
        kb.tt("vector", cs, c, s, ALU.mult, ["c", "s"], ["cs"])
        kb.tt("vector", c, cc, ss, ALU.subtract, ["cc", "ss"], ["c"])
        kb.v(lambda e: e.tensor_scalar(out=s, in0=cs, scalar1=2.0, scalar2=None, op0=ALU.mult), ["cs"], ["s"])
    pwre = r3(kb.f32t(9 * G), 9, G)
    pwim = r3(kb.f32t(9 * G), 9, G)
    kb.v(lambda e: e.memset(pwre[:, 0, :], 1.0), [], ["pw0"])
    kb.v(lambda e: e.memset(pwim[:, 0, :], 0.0), [], ["pw0"])
    kb.tt("vector", pwre[:, 1, :], mag, c, ALU.mult, ["mag", "c"], ["pw1"])
    kb.tt("vector", pwim[:, 1, :], mag, s, ALU.mult, ["mag", "s"], ["pw1"])
    t1 = [S(), S()]
    t2 = [S(), S()]
    for k in range(2, 9):
        kb.cmul(pwre[:, k, :], pwim[:, k, :], pwre[:, k - 1, :], pwim[:, k - 1, :], pwre[:, 1, :], pwim[:, 1, :],
                t1, t2, ["pw%d" % (k - 1), "pw1"], ["pw%d" % k], "pwt")
    nre, den, rden, qre, qim, u1, u2 = [S() for _ in range(7)]
    kb.v(lambda e: e.tensor_scalar(out=nre, in0=pwre[:, 1, :], scalar1=-1.0, scalar2=None, op0=ALU.add), ["pw1"], ["nre"])
    nim = pwim[:, 1, :]
    kb.tt("vector", u1, sm["lamre"], sm["lamre"], ALU.mult, ["lamre"], ["u1"])
    kb.tt("vector", u2, sm["lamim"], sm["lamim"], ALU.mult, ["lamim"], ["u2"])
    kb.tt("vector", den, u1, u2, ALU.add, ["u1", "u2"], ["den"])
    kb.v(lambda e: e.reciprocal(out=rden, in_=den), ["den"], ["rden"])
    kb.tt("vector", u1, nre, sm["lamre"], ALU.mult, ["nre", "lamre"], ["u1"])
    kb.tt("vector", u2, nim, sm["lamim"], ALU.mult, ["pw1", "lamim"], ["u2"])
    kb.tt("vector", qre, u1, u2, ALU.add, ["u1", "u2"], ["qre0"])
    kb.tt("vector", qre, qre, rden, ALU.mult, ["qre0", "rden"], ["qre"])
    kb.tt("vector", u1, nim, sm["lamre"], ALU.mult, ["pw1", "lamre"], ["u1"])
    kb.tt("vector", u2, nre, sm["lamim"], ALU.mult, ["nre", "lamim"], ["u2"])
    kb.tt("vector", qim, u1, u2, ALU.subtract, ["u1", "u2"], ["qim0"])
    kb.tt("vector", qim, qim, rden, ALU.mult, ["qim0", "rden"], ["qim"])
    L = lambda: r3(kb.f32t(G * 16), G, 16)
    bbre, bbim = L(), L()
    mL = kb.ptop
    lt1, lt2 = [L(), L()], [L(), L()]
    kb.cmul(bbre, bbim, bc_last(qre, 16), bc_last(qim, 16), big["bre"], big["bim"], lt1, lt2,
            ["qre", "qim", "bre", "bim"], ["bbar"], "bbt")
    kb.P.barrier()
    kb.ptop = mL
    l8re, l8im = pwre[:, 8, :], pwim[:, 8, :]
    i8re, i8im = S(), S()
    kb.tt("vector", u1, l8re, l8re, ALU.mult, ["pw8"], ["u1"])
    kb.tt("vector", u2, l8im, l8im, ALU.mult, ["pw8"], ["u2"])
    kb.tt("vector", den, u1, u2, ALU.add, ["u1", "u2"], ["den"])
    kb.v(lambda e: e.reciprocal(out=rden, in_=den), ["den"], ["rden"])
    kb.tt("vector", i8re, l8re, rden, ALU.mult, ["pw8", "rden"], ["i8"])
    kb.tt("vector", u1, l8im, rden, ALU.mult, ["pw8", "rden"], ["u1"])
    kb.v(lambda e: e.tensor_scalar(out=i8im, in0=u1, scalar1=-1.0, scalar2=None, op0=ALU.mult), ["u1"], ["i8"])
    mW = kb.ptop
    for (tre, tim, bre_, bim_, nm, dep) in [(tabs["Ture"], tabs["Tuim"], l8re, l8im, "Tu", "pw8"),
                                           (tabs["Tpre"], tabs["Tpim"], i8re, i8im, "Tp", "i8")]:
        kb.v(lambda e, tre=tre, bre_=bre_: e.tensor_copy(out=tre[:, :, 0], in_=bre_), [dep], [nm])
        kb.v(lambda e, tim=tim, bim_=bim_: e.tensor_copy(out=tim[:, :, 0], in_=bim_), [dep], [nm])
        n = 1
        if nm == "Tu":
            wt1 = [r3(kb.f32t(G * W // 2), G, W // 2) for _ in range(2)]
            wt2 = [r3(kb.f32t(G * W // 2), G, W // 2) for _ in range(2)]
        else:
            kb.P.barrier()
        while n < W:
            kb.cmul(tre[:, :, n:2 * n], tim[:, :, n:2 * n], tre[:, :, 0:n], tim[:, :, 0:n],
                    bc_last(tre[:, :, n - 1], n), bc_last(tim[:, :, n - 1], n),
                    [x[:, :, 0:n] for x in wt1], [x[:, :, 0:n] for x in wt2], [nm], [nm], nm + "t")
            n *= 2
    kb.P.barrier()
    kb.ptop = mW
    kb.v(lambda e: e.memset(rmask, 1.0), [], ["rmask"])
    kb.v(lambda e: e.memset(rmask[:, 0:1], 0.0), ["rmask"], ["rmask"])
    PWbre, PWbim, PWcre, PWcim = [r3(kb.f32t(G * 8), G, 8) for _ in range(4)]
    ci = 0
    for sidx in range(8):
        for (ps_, eb, ec) in [(slice(0, 64), 7 - sidx, sidx + 1), (slice(64, 128), sidx, 8 - sidx)]:
            for (dst, srcp, ex) in [(PWbre, pwre, eb), (PWbim, pwim, eb), (PWcre, pwre, ec), (PWcim, pwim, ec)]:
                o = dst[ps_, :, sidx]
                i_ = srcp[ps_, ex, :]
                eng = kb.v if ci % 2 == 0 else kb.g
                ci += 1
                eng(lambda e, o=o, i_=i_: e.tensor_copy(out=o, in_=i_), ["pw%d" % ex], ["PW"])
    kb.P.barrier()
    out["lamWre"] = tabs["Ture"][:, :, W - 1]
    out["lamWim"] = tabs["Tuim"][:, :, W - 1]
    ctx = dict(out=out, need_out=need_out, ptop0=ptop0, PW=(PWbre, PWbim, PWcre, PWcim), bb=(bbre, bbim), big=big, sm=sm,
               i8=(i8re, i8im), ident=ident, maskF=maskF, maskB=maskB)
    return out, ctx


def s5_pre_batches(kb, ctx):
    out = ctx["out"]; need_out = ctx["need_out"]
    PWbre, PWbim, PWcre, PWcim = ctx["PW"]
    bbre, bbim = ctx["bb"]; big = ctx["big"]; sm = ctx["sm"]
    i8re, i8im = ctx["i8"]; ident = ctx["ident"]; maskF = ctx["maskF"]; maskB = ctx["maskB"]
    GH = 2
    A4 = lambda: r4(kb.f32t(GH * 128), GH, 8, 16)
    BLre, BLim = A4(), A4()
    if need_out:
        BLsre = r4(kb.bf(GH * 128), GH, 8, 16)
        BLsim = r4(kb.bf(GH * 128), GH, 8, 16)
    T = [A4() for _ in range(4)]
    f3 = lambda x: x.rearrange("p g s h -> p g (s h)")
    for gh in range(0, G, GH):
        gs = slice(gh, gh + GH)
        bs = lambda x: x[:, gs, :].unsqueeze(3).broadcast_to([128, GH, 8, 16])
        bh = lambda x: x[:, gs, :].unsqueeze(2).broadcast_to([128, GH, 8, 16])
        kb.cmul(BLre, BLim, bs(PWbre), bs(PWbim), bh(bbre), bh(bbim), [T[0], T[1]], [T[2], T[3]], ["PW", "bbar"], ["BL"], "bT")
        for gi in range(GH):
            g = gh + gi
            for ri, src in enumerate([BLre, BLim]):
                pi = 4 + (gi * 2 + ri) % 2
                pst = kb.ps[pi]
                kps = "ps%d" % pi
                srcg = src[:, gi, :, :].rearrange("p s h -> p (s h)")
                kb.t(lambda e, pst=pst, srcg=srcg: e.transpose(pst[:, 0:128], srcg, ident), ["BL", "ident"], [kps])
                dst = out["B8"][:, g, ri, :]
                kb.a(lambda e, pst=pst, dst=dst: e.activation(out=dst, in_=pst[:, 0:128], func=AF.Copy), [kps], ["B8"])
        if need_out:
            kb.cmul(f3(BLsre), f3(BLsim), f3(BLre), f3(BLim), bc_last(i8re[:, gs], 128), bc_last(i8im[:, gs], 128),
                    [f3(T[0]), f3(T[1])], [f3(T[2]), f3(T[3])], ["BL", "i8"], ["BLs"], "bT")
            yield
            C8re, C8im = BLre, BLim
            kb.cmul(C8re, C8im, bs(PWcre), bs(PWcim), bh(big["cre"]), bh(big["cim"]), [T[0], T[1]], [T[2], T[3]],
                    ["PW", "cre", "cim"], ["BL"], "bT")
            kb.v(lambda e: e.tensor_scalar(out=C8im, in0=C8im, scalar1=-1.0, scalar2=None, op0=ALU.mult), ["BL"], ["BL"])
            d0 = out["C8"][:, 0, gs, :]
            d1 = out["C8"][:, 1, gs, :]
            kb.v(lambda e, d0=d0: e.tensor_copy(out=d0, in_=f3(C8re)), ["BL"], ["C8b"])
            kb.g(lambda e, d1=d1: e.tensor_copy(out=d1, in_=f3(C8im)), ["BL"], ["C8b"])
            yield
            for gi in range(GH):
                g = gh + gi
                pss = []
                for d in range(2):
                    pi = 6 + d
                    pst = kb.ps[pi]
                    kps = "ps%d" % pi
                    rows = slice(64 * d, 64 * d + 64)
                    lre = BLsre[rows, gi, :, :].rearrange("p s h -> p (s h)")
                    lim = BLsim[rows, gi, :, :].rearrange("p s h -> p (s h)")
                    rre = out["C8"][rows, 0, g, :]
                    rim = out["C8"][rows, 1, g, :]
                    kb.t(lambda e, pst=pst, lre=lre, rre=rre: e.matmul(pst[:, 0:128], lhsT=lre, rhs=rre, start=True, stop=False), ["BLs", "C8b"], [kps])
                    kb.t(lambda e, pst=pst, lim=lim, rim=rim: e.matmul(pst[:, 0:128], lhsT=lim, rhs=rim, start=False, stop=True), ["BLs", "C8b"], [kps])
                    pss.append((pst, kps))
                tA = T[0][:, gi, :, :].rearrange("p s h -> p (s h)")
                tB = T[1][:, gi, :, :].rearrange("p s h -> p (s h)")
                kA = "bT_t0"
                kBt = "bT_t2"
                kb.v(lambda e, tA=tA, p0=pss[0][0]: e.tensor_tensor(out=tA, in0=p0[:, 0:128], in1=maskF, op=ALU.mult), [pss[0][1], "maskF"], [kA])
                kb.v(lambda e, tB=tB, p1=pss[1][0]: e.tensor_tensor(out=tB, in0=p1[:, 0:128], in1=maskB, op=ALU.mult), [pss[1][1], "maskB"], [kBt])
                kb.g(lambda e, tA=tA, tB=tB: e.tensor_tensor(out=tA, in0=tA, in1=tB, op=ALU.add), [kA, kBt], [kA])
                dcol = sm["dT"][:, g:g + 1]
                dst = out["Toep"][:, g, :]
                kb.v(lambda e, tA=tA, dcol=dcol, dst=dst: e.scalar_tensor_tensor(out=dst, in0=ident, scalar=dcol, in1=tA, op0=ALU.mult, op1=ALU.add),
                     [kA, "dT", "ident"], ["Toep"])
        yield
    return


def s5_precompute(kb, din, need_out=True):
    out, ctx = s5_pre_init(kb, din, need_out)
    for _ in s5_pre_batches(kb, ctx):
        pass
    kb.P.barrier()
    kb.ptop = ctx["ptop0"]
    return out


def interleave(gens):
    gens = list(gens)
    while gens:
        for g_ in list(gens):
            try:
                next(g_)
            except StopIteration:
                gens.remove(g_)


def _s5_gq(kb, cst, X, J, gb, GB, bufs, tag, sfx, psb):
    NW = J // W
    m1a, m2a, m3a, m4a, qre, qim = bufs
    v4 = lambda x: r4(x[:, 0:GB * J], GB, NW, W)
    v3 = lambda x: r3(x[:, 0:GB * J], GB, J)
    gs = slice(gb, gb + GB)
    tb = lambda nm: cst[nm][:, gs, :].unsqueeze(2).broadcast_to([128, GB, NW, W])
    K = lambda n: n + sfx
    assert GB * J <= 512
    for ri in range(2):
        pi = psb + ri
        pst = kb.ps[pi]
        kps = "ps%d" % pi
        for gi in range(GB):
            g = gb + gi
            lhs = cst["B8"][:, g, ri, :]
            rhs = X[:, g, :]
            kb.t(lambda e, pst=pst, lhs=lhs, rhs=rhs, gi=gi: e.matmul(pst[:, gi * J:(gi + 1) * J], lhsT=lhs, rhs=rhs, start=True, stop=True),
                 ["B8", tag + "X"], [kps])
        dst = v3(m1a if ri == 0 else m2a)
        kd = K("m1") if ri == 0 else K("m2")
        src = r3(pst[:, 0:GB * J], GB, J)
        kb.a(lambda e, src=src, dst=dst: e.activation(out=dst[0:64, :, :], in_=src[0:64, :, :], func=AF.Copy),
             [kps], [kd + "_f%d" % i for i in range(GB)])
        kb.a(lambda e, src=src, dst=dst: e.activation(out=dst[64:128, :, ::-1], in_=src[64:128, :, :], func=AF.Copy),
             [kps], [kd + "_b%d" % i for i in range(GB)])
        yield
    K1 = [K("m1") + "_f%d" % i for i in range(GB)] + [K("m1") + "_b%d" % i for i in range(GB)]
    K2 = [K("m2") + "_f%d" % i for i in range(GB)] + [K("m2") + "_b%d" % i for i in range(GB)]
    kb.tt("vector", v4(m3a), v4(m1a), tb("Tpre"), ALU.mult, K1 + ["Tp"], [K("m3")])
    kb.tt("gpsimd", v4(m4a), v4(m2a), tb("Tpim"), ALU.mult, K2 + ["Tp"], [K("m4")])
    yield
    kb.tt("vector", v4(qre), v4(m3a), v4(m4a), ALU.subtract, [K("m3"), K("m4")], [K("qre")])
    yield
    kb.tt("vector", v4(m3a), v4(m1a), tb("Tpim"), ALU.mult, K1 + ["Tp"], [K("m3")])
    kb.tt("gpsimd", v4(m4a), v4(m2a), tb("Tpre"), ALU.mult, K2 + ["Tp"], [K("m4")])
    yield
    kb.tt("vector", v4(qim), v4(m3a), v4(m4a), ALU.add, [K("m3"), K("m4")], [K("qim")])
    yield
    return


def s5_pass1(kb, cst, X, J, Qend_re, Qend_im, tag, QD=None, GB=2):
    NW = J // W
    assert G % (2 * GB) == 0 and GB * J <= 512
    m0 = kb.mark()
    sets = [[kb.f32(GB * J) for _ in range(6)] for _ in range(2)]
    v4 = lambda x: r4(x, GB, NW, W)

    def stream(si):
        bufs = sets[si]
        sfx = "_s%d" % si
        for gb in range(si * GB, G, 2 * GB):
            gs = slice(gb, gb + GB)
            yield from _s5_gq(kb, cst, X, J, gb, GB, bufs, tag, sfx, 2 * si)
            if QD is not None:
                kb.dma(QD[gb // GB, 0], bufs[4], r=["qre" + sfx], w=["QD%d" % (gb // GB)])
                kb.dma(QD[gb // GB, 1], bufs[5], r=["qim" + sfx], w=["QD%d" % (gb // GB)])
            o1 = Qend_re[:, gs, :]
            o2 = Qend_im[:, gs, :]
            kb.v(lambda e, o1=o1: e.tensor_reduce(out=o1, in_=v4(bufs[4]), axis=mybir.AxisListType.X, op=ALU.add), ["qre" + sfx], [tag + "Qend"])
            yield
            kb.v(lambda e, o2=o2: e.tensor_reduce(out=o2, in_=v4(bufs[5]), axis=mybir.AxisListType.X, op=ALU.add), ["qim" + sfx], [tag + "Qend"])
            yield
    interleave([stream(0), stream(1)])
    kb.P.barrier()
    kb.release(m0)


def s5_carries(kb, cst, Qend_re, Qend_im, init_re, init_im, car_re, car_im, NW, tag):
    m0 = kb.mark()
    ct = [r3(kb.f32(G), G, 1) for _ in range(6)]
    kb.v(lambda e: e.tensor_copy(out=car_re[:, :, 0], in_=init_re), [tag + "init"], [tag + "car"])
    kb.v(lambda e: e.tensor_copy(out=car_im[:, :, 0], in_=init_im), [tag + "init"], [tag + "car"])
    lw_re = cst["lamWre"].unsqueeze(2)
    lw_im = cst["lamWim"].unsqueeze(2)
    for w in range(NW):
        cr, ci = car_re[:, :, w:w + 1], car_im[:, :, w:w + 1]
        nr, ni = car_re[:, :, w + 1:w + 2], car_im[:, :, w + 1:w + 2]
        kb.tt("vector", ct[0], Qend_re[:, :, w:w + 1], cr, ALU.add, [tag + "Qend", tag + "car"], ["ct0"])
        kb.tt("gpsimd", ct[1], Qend_im[:, :, w:w + 1], ci, ALU.add, [tag + "Qend", tag + "car"], ["ct1"])
        kb.cmul(nr, ni, ct[0], ct[1], lw_re, lw_im, [ct[2], ct[3]], [ct[4], ct[5]], ["ct0", "ct1", "Tu"], [tag + "car"], "ctt")
    kb.P.barrier()
    kb.release(m0)


def s5_pass2(kb, cst, X, J, car_re, car_im, SPre, SPim, tag, g0=0, g1=G, QD=None, car_tag=None, pre_barrier=True):
    NW = J // W
    GB = 2
    ctag = tag if car_tag is None else car_tag
    m0 = kb.mark()
    sets = [[kb.f32(GB * J) for _ in range(6)] for _ in range(2)]
    mska = kb.f32(GB * J)
    v4 = lambda x: r4(x, GB, NW, W)
    v3 = lambda x: r3(x, GB, J)
    mk = cst["rmask"].unsqueeze(1).unsqueeze(1).broadcast_to([128, GB, NW, W])
    kb.v(lambda e: e.tensor_copy(out=v4(mska), in_=mk), ["rmask"], ["mska"])

    def stream(si):
        bufs = sets[si]
        m1a, m2a, m3a, m4a, qre, qim = bufs
        sfx = "_s%d" % si
        K = lambda n: n + sfx

        def load(gb):
            kb.dma(qre, QD[gb // GB, 0], w=[K("qre")])
            kb.dma(qim, QD[gb // GB, 1], w=[K("qim")])
        first = g0 + si * GB
        load(first)
        for gb in range(first, g1, 2 * GB):
            gs = slice(gb, gb + GB)
            gsl = slice(gb - g0, gb - g0 + GB)
            tb = lambda nm: cst[nm][:, gs, :].unsqueeze(2).broadcast_to([128, GB, NW, W])
            q4r, q4i = v4(qre), v4(qim)
            cr = car_re[:, gs, 0:NW].unsqueeze(3)
            ci = car_im[:, gs, 0:NW].unsqueeze(3)
            kb.tt("vector", q4r[:, :, :, 0:1], q4r[:, :, :, 0:1], cr, ALU.add, [K("qre"), ctag + "car"], [K("qre")])
            kb.tt("vector", q4i[:, :, :, 0:1], q4i[:, :, :, 0:1], ci, ALU.add, [K("qim"), ctag + "car"], [K("qim")])
            yield
            kb.v(lambda e: e.tensor_tensor_scan(out=m1a, data0=mska, data1=qre, initial=0.0, op0=ALU.mult, op1=ALU.add), ["mska", K("qre")], [K("m1")])
            yield
            kb.v(lambda e: e.tensor_tensor_scan(out=m2a, data0=mska, data1=qim, initial=0.0, op0=ALU.mult, op1=ALU.add), ["mska", K("qim")], [K("m2")])
            yield
            if gb + 2 * GB < g1:
                load(gb + 2 * GB)
            cre4, cim4 = v4(m1a), v4(m2a)
            kb.tt("vector", v4(m3a), cre4, tb("Ture"), ALU.mult, [K("m1"), "Tu"], [K("m3")])
            kb.tt("gpsimd", v4(m4a), cim4, tb("Tuim"), ALU.mult, [K("m2"), "Tu"], [K("m4")])
            yield
            x3, y3 = v3(m3a), v3(m4a)
            kb.tt("vector", SPre[0:64, gsl, 1:J], x3[0:64, :, 0:J - 1], y3[0:64, :, 0:J - 1], ALU.subtract, [K("m3"), K("m4")], [tag + "SP"])
            kb.tt("gpsimd", SPre[64:128, gsl, 0:J - 1], x3[64:128, :, J - 2::-1], y3[64:128, :, J - 2::-1], ALU.subtract, [K("m3"), K("m4")], [tag + "SP"])
            yield
            kb.tt("vector", v4(m3a), cre4, tb("Tuim"), ALU.mult, [K("m1"), "Tu"], [K("m3")])
            kb.tt("gpsimd", v4(m4a), cim4, tb("Ture"), ALU.mult, [K("m2"), "Tu"], [K("m4")])
            yield
            kb.tt("vector", SPim[0:64, gsl, 1:J], x3[0:64, :, 0:J - 1], y3[0:64, :, 0:J - 1], ALU.add, [K("m3"), K("m4")], [tag + "SP"])
            kb.tt("vector", SPim[64:128, gsl, 0:J - 1], x3[64:128, :, J - 2::-1], y3[64:128, :, J - 2::-1], ALU.add, [K("m3"), K("m4")], [tag + "SP"])
            yield
    interleave([stream(0), stream(1)])
    i_re, i_im = car_re[:, g0:g1, 0], car_im[:, g0:g1, 0]
    kb.v(lambda e: e.tensor_copy(out=SPre[0:64, :, 0], in_=i_re[0:64, :]), [ctag + "car"], [tag + "SP"])
    kb.v(lambda e: e.tensor_copy(out=SPim[0:64, :, 0], in_=i_im[0:64, :]), [ctag + "car"], [tag + "SP"])
    kb.v(lambda e: e.tensor_copy(out=SPre[64:128, :, J - 1], in_=i_re[64:128, :]), [ctag + "car"], [tag + "SP"])
    kb.v(lambda e: e.tensor_copy(out=SPim[64:128, :, J - 1], in_=i_im[64:128, :]), [ctag + "car"], [tag + "SP"])
    kb.P.barrier()
    kb.release(m0)


def s5_relayout(kb, selF, uaT, X, J, tag):
    for g in range(G):
        pt, gl = g // 8, g % 8
        pst = kb.ps[g % 8]
        kps = "ps%d" % (g % 8)
        for s in range(8):
            rhs = uaT[:, pt, s::8]
            lhs = selF[:, gl, 16 * (7 - s):16 * (7 - s) + 128]
            kb.t(lambda e, pst=pst, lhs=lhs, rhs=rhs, s=s: e.matmul(pst[:, 0:J], lhsT=lhs, rhs=rhs, start=(s == 0), stop=(s == 7)),
                 ["selF", tag + "uaT"] + ([kps] if s else []), [kps])
        dst = X[:, g, :]
        if g % 2 == 0:
            kb.a(lambda e, pst=pst, dst=dst: e.activation(out=dst, in_=pst[:, 0:J], func=AF.Copy), [kps], [tag + "X"])
        else:
            kb.v(lambda e, pst=pst, dst=dst: e.tensor_copy(out=dst, in_=pst[:, 0:J]), [kps], [tag + "X"])


def s5_output(kb, cst, X, SPre, SPim, J, consume, tag, g0=0, g1=G):
    for g in range(g0, g1):
        pst = kb.ps[g % 8]
        kps = "ps%d" % (g % 8)
        ops = [(cst["Toep"][:, g, :], X[:, g, :]), (cst["C8"][:, 0, g, :], SPre[:, g - g0, :]), (cst["C8"][:, 1, g, :], SPim[:, g - g0, :])]
        for i, (lhs, rhs) in enumerate(ops):
            kb.t(lambda e, pst=pst, lhs=lhs, rhs=rhs, i=i: e.matmul(pst[:, 0:J], lhsT=lhs, rhs=rhs, start=(i == 0), stop=(i == 2)),
                 ["Toep", "C8b", tag + "X", tag + "SP"] + ([kps] if i else []), [kps])
        consume(g, pst, kps)


DN_ALPHA = 4 ** 0.25
DEBUG_F = False
LN_EPS = 1e-5
NTOK = 2048
F32_ARENA = 17200
BF_ARENA = 52000


def MM(kb, pst_ap, lhs, rhs, start, stop, r, w):
    kb.t(lambda e: e.matmul(pst_ap, lhsT=lhs, rhs=rhs, start=start, stop=stop), r, w)


def ACT(kb, out, in_, func, r, w, **kw):
    kb.a(lambda e: e.activation(out=out, in_=in_, func=func, **kw), r, w)


def adaln_loads(kb, D, sfx, wb, csb, csrep, gate_rep):
    cs = r3(kb.f32t(16), 8, 2)
    kb.dma(cs.rearrange("p a b -> p (a b)"), D["cT"], w=["cs" + sfx])
    modbT = kb.f32t(24)
    kb.dma(modbT, D["modbT"], w=["modbT" + sfx])
    gb = gate_rep
    kb.dma(gb, D["modb_gate_rep"], w=["gate_rep" + sfx])
    wv = D["modw"].rearrange("(k p) c -> p k c", p=128)
    for blk in range(6):
        kb.dma(wb[blk], wv[:, :, blk * 512:(blk + 1) * 512], w=["adw%d" % blk + sfx], q="gpsimd")
    return dict(cs=cs, modbT=modbT, gb=gb, wb=wb, csb=csb, csrep=csrep, sfx=sfx)


def adaln_compute(kb, A, modT, gate_rep):
    sfx = A["sfx"]
    cs, modbT, gb, wb, csb, csrep = A["cs"], A["modbT"], A["gb"], A["wb"], A["csb"], A["csrep"]
    ACT(kb, csb, cs, AF.Silu, ["cs" + sfx], ["csb" + sfx])
    kb.v(lambda e: e.tensor_copy(out=csrep, in_=csb[:, :, 0:1].broadcast_to([128, 8, 128])), ["csb" + sfx], ["csrep" + sfx])
    for blk in range(6):
        w = wb[blk]
        kw = "adw%d" % blk + sfx
        for cb in range(4):
            pst = kb.ps[cb]
            kps = "ps%d" % cb
            for k in range(8):
                MM(kb, pst[:, 0:2], w[:, k, cb * 128:(cb + 1) * 128], csb[:, k, :], k == 0, k == 7, [kw, "csb" + sfx], [kps])
            col = blk * 4 + cb
            o = modT[:, col, :]
            b_ = modbT[:, col:col + 1].broadcast_to([128, 2])
            kb.v(lambda e, o=o, pst=pst, b_=b_: e.tensor_tensor(out=o, in0=pst[:, 0:2], in1=b_, op=ALU.add), [kps, "modbT" + sfx], ["modT" + sfx])
        if blk >= 4:
            half = blk - 4
            pst = kb.ps[4 + half]
            kps = "ps%d" % (4 + half)
            for k in range(8):
                MM(kb, pst[:, 0:512], csrep[:, k, :], w[:, k, :], k == 0, k == 7, [kw, "csrep" + sfx], [kps])
            o = gate_rep[:, half * 512:(half + 1) * 512]
            b_ = gb[:, half * 512:(half + 1) * 512]
            kb.v(lambda e, o=o, pst=pst, b_=b_: e.tensor_tensor(out=o, in0=pst[:, 0:512], in1=b_, op=ALU.add), [kps, "gate_rep" + sfx], ["gate_rep" + sfx])
    sc = modT[:, 8:16, :]
    kb.v(lambda e: e.tensor_scalar(out=sc, in0=sc, scalar1=1.0, scalar2=None, op0=ALU.add), ["modT" + sfx], ["modT" + sfx])


def adaln(kb, D):
    modT = r3(kb.f32(48), 24, 2)
    gate_rep = kb.f32(1024)
    m0 = kb.mark()
    pt0 = kb.ptop
    wb = [r3(kb.bf(8 * 512), 8, 512) for _ in range(6)]
    A = adaln_loads(kb, D, "", wb, r3(kb.bf(16), 8, 2), r3(kb.bf(8 * 128), 8, 128), gate_rep)
    adaln_compute(kb, A, modT, gate_rep)
    kb.P.barrier()
    kb.release(m0)
    kb.ptop = pt0
    return modT, gate_rep


def make_hmod(kb, xsrc, modT, ci, hm, xt, ident):
    kb.dma(xt, xsrc.rearrange("(s p) d -> p s d", p=128), w=["xt"])
    for s in range(4):
        for k in range(8):
            pi = (s * 8 + k) % 8
            pst = kb.ps[pi]
            kps = "ps%d" % pi
            src = xt[:, s, k * 128:(k + 1) * 128]
            kb.t(lambda e, pst=pst, src=src: e.transpose(pst[:, 0:128], src, ident), ["xt", "ident"], [kps])
            o = hm[:, k, s * 128:(s + 1) * 128]
            ACT(kb, o, pst[:, 0:128], AF.Identity, [kps, "modT"], ["hm"], scale=modT[:, 8 + k, ci:ci + 1], bias=modT[:, k, ci:ci + 1])


def resid_ln(kb, y_ps_list, xtile, gate_rep, g_rep, b_rep, dst_dram, tmp, tag, sfx=""):
    t1, t2, st, mv, rs = tmp
    K = lambda n: "rl%s_%s" % (sfx, n)
    for h in range(2):
        pst, kps = y_ps_list[h]
        sl = slice(h * 512, (h + 1) * 512)
        kb.v(lambda e, pst=pst, sl=sl: e.scalar_tensor_tensor(out=t2[:, sl], in0=xtile[:, sl], scalar=DN_ALPHA, in1=pst[:, 0:512], op0=ALU.mult, op1=ALU.add),
             [kps, tag + "x" + sfx, K("t10"), K("t11")], [K("t2%d" % h)])
        kb.v(lambda e, sl=sl, h=h: e.bn_stats(out=st[:, h * 6:(h + 1) * 6], in_=t2[:, sl]), [K("t2%d" % h)], [K("st%d" % h)])
    kb.v(lambda e: e.bn_aggr(out=mv, in_=st), [K("st0"), K("st1")], [K("mv")])
    kb.v(lambda e: e.tensor_scalar(out=rs, in0=mv[:, 1:2], scalar1=LN_EPS, scalar2=None, op0=ALU.add), [K("mv")], [K("rs")])
    ACT(kb, rs, rs, AF.Sqrt, [K("rs")], [K("rs")])
    kb.v(lambda e: e.reciprocal(out=rs, in_=rs), [K("rs")], [K("rs")])
    nb = st[:, 0:1]
    kb.v(lambda e: e.scalar_tensor_tensor(out=nb, in0=mv[:, 0:1], scalar=-1.0, in1=rs, op0=ALU.mult, op1=ALU.mult), [K("mv"), K("rs"), K("st0"), K("st1")], [K("nb")])
    ACT(kb, t2, t2, AF.Identity, [K("t20"), K("t21"), K("nb"), K("rs")], [K("t20"), K("t21")], scale=rs, bias=nb)
    kb.v(lambda e: e.tensor_tensor(out=t2, in0=t2, in1=g_rep, op=ALU.mult), [K("t20"), K("t21"), "g_rep"], [K("t20"), K("t21")])
    kb.g(lambda e: e.tensor_tensor(out=t1, in0=t2, in1=b_rep, op=ALU.add), [K("t20"), K("t21"), "b_rep"], [K("t10"), K("t11")])
    kb.dma(dst_dram, t1, r=[K("t10"), K("t11")])


def build12(stage):
    assert stage == "F"
    S1 = stage in (1, "F")
    S2 = stage in (2, "F")
    nc = bass.Bass("TRN2", target_bir_lowering=False)
    D = {}

    def inp(name, shape):
        D[name] = nc.dram_tensor(name, shape, F32, kind="ExternalInput").ap()
    for nm in ["lamre", "lamim", "logdt", "dT"]:
        inp(nm, [128, 32])
    for nm in ["bre", "bim", "cre", "cim"]:
        inp(nm, [128, 512])
    inp("selF", [128, 8 * 240]); inp("ident", [128, 128]); inp("maskF", [128, 128]); inp("maskB", [128, 128])
    inp("x", [NTOK, 1024]); inp("cT", [128, 16]); inp("modw", [1024, 3072]); inp("modbT", [128, 24]); inp("modb_gate_rep", [128, 1024])
    inp("w_in", [1024, 2560])
    if S1:
        inp("ctx", [256, 1024])
    if stage == 1:
        FIN = nc.dram_tensor("FIN", [128, 64], F32, kind="ExternalOutput").ap()
        CFIN = nc.dram_tensor("CFIN", [128, 64], F32, kind="ExternalOutput").ap()
    if stage == 2:
        inp("init", [128, 64])
    if S2:
        inp("g_rep", [128, 1024]); inp("b_rep", [128, 1024])
        inp("glu_w", [512, 512]); inp("glu_bT", [128, 4]); inp("lng_rep", [128, 512]); inp("lnb_rep", [128, 512])
        inp("wsT", [128, 1024]); inp("bs_tm", [128, 8]); inp("w_out", [1024, 1024])
        X1 = nc.dram_tensor("X1", [NTOK, 1024], F32, kind=("ExternalOutput" if (stage == 2 or DEBUG_F) else "Internal")).ap()
        HM = nc.dram_tensor("HM", [4, 128, 8 * 512], BF16, kind="Internal").ap()
    if stage == "F":
        inp("mhalf", [128, 2])
        D1 = {}
        for nm, shp in [("cT", [128, 16]), ("modw", [1024, 3072]), ("modbT", [128, 24]), ("modb_gate_rep", [128, 1024]),
                        ("w_in", [1024, 3072]), ("w_out", [1024, 1024]), ("g_rep", [128, 1024]), ("b_rep", [128, 1024]),
                        ("dwT", [128, 8 * 31]), ("dwbT", [128, 8]), ("clngT", [128, 8]), ("clnbT", [128, 8])]:
            D1[nm] = nc.dram_tensor(nm + "_1", shp, F32, kind="ExternalInput").ap()
        D1["ident"] = D["ident"]
        D1["mhalf"] = D["mhalf"]
        OUT = nc.dram_tensor("OUT", [NTOK, 1024], F32, kind="ExternalOutput").ap()
        HM3 = nc.dram_tensor("HM3", [4, 128, 8 * 512], BF16, kind="Internal").ap()
        CV3 = nc.dram_tensor("CV3", [8, 128, NTOK], F32, kind="Internal").ap()
        DBGI = nc.dram_tensor("DBGI", [128, 64], F32, kind="ExternalOutput").ap() if DEBUG_F else None
        DBG2 = nc.dram_tensor("DBG2", [128, 64 * 3 + 1024 + 1024], F32, kind="ExternalOutput").ap() if DEBUG_F else None
        QD = nc.dram_tensor("QD", [16, 2, 128, 2 * J_OWN], F32, kind="Internal").ap()
        CCI1 = nc.dram_tensor("cci1", [128, 64], F32, kind="Internal").ap()
        CCO1 = nc.dram_tensor("cco1", [256, 64], F32, kind="Internal").ap()
        CCI2 = nc.dram_tensor("cci2", [128, 8192], BF16, kind="Internal").ap()
        CCO2 = nc.dram_tensor("cco2", [256, 8192], BF16, kind="Internal").ap()
    with contextlib.ExitStack() as es:
        kb = KB(nc, es, F32_ARENA, BF_ARENA)
        for nm in ["cc0", "cc1"]:
            kb.sems[nm] = es.enter_context(nc.semaphore(nm))
            kb.P.cnt[nm] = 0
        kb.boot_clear()
        block = es.enter_context(nc.Block())
        ygT = r3(kb.ab[:, BF_ARENA - 4 * NTOK:BF_ARENA], 4, NTOK) if S2 else None
        ident = kb.f32(128)
        kb.dma(ident, D["ident"], w=["ident"])
        modT = r3(kb.f32(48), 24, 2); gate_rep = kb.f32(1024)
        modT1 = r3(kb.f32(48), 24, 2); gate_rep1 = kb.f32(1024)
        mPers = kb.mark()
        ad = []
        for li, DD in enumerate([D, D1]):
            wb = [r3(kb.ab[:, (6 * li + i) * 4096:(6 * li + i + 1) * 4096], 8, 512) for i in range(6)]
            base = 49152 + li * 1040
            ad.append(adaln_loads(kb, DD, "_a%d" % li, wb, r3(kb.ab[:, base:base + 16], 8, 2), r3(kb.ab[:, base + 16:base + 1040], 8, 128), [gate_rep, gate_rep1][li]))
        kb.P.barrier_skip = tuple("gpsimd_d%d" % i for i in range(NDMASEM))
        cst, pctx = s5_pre_init(kb, D, need_out=S2)
        kb.P.barrier_skip = ()
        adaln_compute(kb, ad[0], modT, gate_rep)
        adaln_compute(kb, ad[1], modT1, gate_rep1)
        kb.P.barrier()
        selF = r3(kb.bf(8 * 240), 8, 240)
        kb.dma(selF.rearrange("p a b -> p (a b)"), D["selF"], w=["selF"], q="gpsimd")
        X = r3(kb.bf(G * J_OWN), G, J_OWN)
        init = kb.f32(64)
        fin = kb.f32(64)
        NWO = J_OWN // W
        Qre = r3(kb.f32(G * NWO), G, NWO); Qim = r3(kb.f32(G * NWO), G, NWO)
        car_re = r3(kb.f32(G * (NWO + 1)), G, NWO + 1); car_im = r3(kb.f32(G * (NWO + 1)), G, NWO + 1)
        QCre = r3(kb.f32(G), G, 1); QCim = r3(kb.f32(G), G, 1)
        cC_re = r3(kb.f32(G * 2), G, 2); cC_im = r3(kb.f32(G * 2), G, 2)
        if S1:
            Xc = r3(kb.ab[:, BF_ARENA - G * 64:BF_ARENA], G, 64)
            cfin = kb.f32(64)
        mP2 = kb.mark()
        hm = r3(kb.bf(8 * 512), 8, 512)
        wua = r3(kb.bf(8 * 512), 8, 512)
        kb.dma(wua, D["w_in"].rearrange("(k p) c -> p k c", p=128)[:, :, 0:512], w=["wua"], q="gpsimd")
        NTC = NTOK + 256
        uaT = r3(kb.bf(4 * NTC), 4, NTC)
        xt = r3(kb.f32(1 * 1024), 1, 1024)

        def ua_proj(dst, ncol0, n=512):
            for cb in range(4):
                pst = kb.ps[cb]
                kps = "ps%d" % cb
                for k in range(8):
                    MM(kb, pst[:, 0:n], wua[:, k, cb * 128:(cb + 1) * 128], hm[:, k, 0:n], k == 0, k == 7, ["wua", "hm"], [kps])
                o = dst[:, cb, ncol0:ncol0 + n]
                ACT(kb, o, pst[:, 0:n], AF.Copy, [kps], ["LuaT"])

        def hmod_sub(src_rows, s, ci, buf):
            kx = "xt%d" % buf
            kb.dma(xt[:, buf, :], src_rows, w=[kx])
            for k in range(8):
                pst = kb.ps[k % 4]
                kps = "ps%d" % (k % 4)
                src = xt[:, buf, k * 128:(k + 1) * 128]
                kb.t(lambda e, pst=pst, src=src: e.transpose(pst[:, 0:128], src, ident), [kx, "ident"], [kps])
                o = hm[:, k, s * 128:(s + 1) * 128]
                ACT(kb, o, pst[:, 0:128], AF.Identity, [kps, "modT"], ["hm"], scale=modT[:, 8 + k, ci:ci + 1], bias=modT[:, k, ci:ci + 1])

        def p2_stream():
            for tt_ in range(4):
                for s in range(4):
                    r0 = tt_ * 512 + s * 128
                    hmod_sub(D["x"][r0:r0 + 128, :], s, 0, 0)
                    yield
                if S2:
                    kb.dma(HM[tt_], hm.rearrange("p a b -> p (a b)"), r=["hm"], w=["HM%d" % tt_])
                ua_proj(uaT, tt_ * 512)
                yield
            for s in range(2):
                hmod_sub(D["ctx"][s * 128:(s + 1) * 128, :], s, 1, 0)
            yield
            ua_proj(uaT, NTOK, 256)
            yield
            JT = NTC // 8
            for g in range(G):
                pt, gl = g // 8, g % 8
                pst = kb.ps[g % 4]
                kps = "ps%d" % (g % 4)
                for s in range(8):
                    MM(kb, pst[:, 0:JT], selF[:, gl, 16 * (7 - s):16 * (7 - s) + 128], uaT[:, pt, s::8], s == 0, s == 7, ["selF", "LuaT"], [kps])
                ACT(kb, X[:, g, :], pst[:, 0:J_OWN], AF.Copy, [kps], ["LX"])
                ACT(kb, Xc[:, g, 0:32], pst[:, J_OWN:JT], AF.Copy, [kps], ["CX"])
                yield
        interleave([s5_pre_batches(kb, pctx), p2_stream()])
        kb.P.barrier()
        kb.ptop = pctx["ptop0"]
        if S1:
            kb.P.barrier()
            kb.release(mP2)
            kb.v(lambda e: e.memset(init, 0.0), [], ["Cinit"])
            s5_pass1(kb, cst, Xc[:, :, 0:32], 32, QCre, QCim, "C", GB=8)
            s5_carries(kb, cst, QCre, QCim, init[:, 0:32], init[:, 32:64], cC_re, cC_im, 1, "C")
            kb.v(lambda e: e.tensor_copy(out=cfin[:, 0:32], in_=cC_re[:, :, 1]), [], ["Cfin"])
            kb.v(lambda e: e.tensor_copy(out=cfin[:, 32:64], in_=cC_im[:, :, 1]), [], ["Cfin"])
            kb.v(lambda e: e.tensor_copy(out=init, in_=cfin), ["Cfin"], ["Linit"])
            s5_pass1(kb, cst, X, J_OWN, Qre, Qim, "L", QD=QD)
            s5_carries(kb, cst, Qre, Qim, init[:, 0:32], init[:, 32:64], car_re, car_im, NWO, "L")
            kb.v(lambda e: e.tensor_copy(out=fin[:, 0:32], in_=car_re[:, :, NWO]), [], ["Lfin"])
            kb.v(lambda e: e.tensor_copy(out=fin[:, 32:64], in_=car_im[:, :, NWO]), [], ["Lfin"])
            if stage == 1:
                kb.dma(FIN, fin, r=["Lfin"])
                kb.P.barrier()
                print("stage1 nops", kb.P.nops, "arena", kb.pf, kb.pb)
                kb.P.emit(kb.sems, block)
                return nc
            if DEBUG_F:
                mdd = kb.mark()
                dd = kb.f32(64 * 3 + 2048)
                kb.v(lambda e: e.tensor_copy(out=dd[:, 0:64], in_=cfin), [], ["dd"])
                kb.v(lambda e: e.tensor_copy(out=dd[:, 64:128], in_=fin), [], ["dd"])
                kb.v(lambda e: e.tensor_copy(out=dd[:, 128:160], in_=cst["lamWre"]), [], ["dd"])
                kb.v(lambda e: e.tensor_copy(out=dd[:, 160:192], in_=cst["lamWim"]), [], ["dd"])
                kb.v(lambda e: e.tensor_copy(out=r3(dd[:, 192:192 + 1024], 32, 32), in_=Xc[:, :, 0:32]), [], ["dd"])
                kb.v(lambda e: e.tensor_copy(out=r3(dd[:, 1216:1216 + 1024], 32, 32), in_=X[:, :, 0:32]), [], ["dd"])
                kb.dma(DBG2, dd, r=["dd"])
                kb.P.barrier()
                kb.release(mdd)
            mh = kb.f32(2)
            kb.dma(mh, D["mhalf"], w=["mh"])
            kb.dma(CCI1, fin, r=["Lfin"], w=["cci1"])
            kb.P.barrier()
            kb.cc(lambda e: e.collective_compute("AllGather", ALU.bypass, replica_groups=[[0, 1], [2, 3], [4, 5], [6, 7]], ins=[CCI1], outs=[CCO1]),
                  "cc0", [], ["cco1"])
            kb.P.barrier()
            g01 = r3(kb.f32(128), 2, 64)
            kb.dma(g01, CCO1.rearrange("(r p) c -> p r c", p=128), r=["cco1"], w=["g01"])
            dtmp = kb.f32(64)
            kb.v(lambda e: e.tensor_tensor(out=dtmp[0:64, :], in0=g01[0:64, 0, :], in1=cfin[0:64, :], op=ALU.subtract), ["g01", "Cfin"], ["dtmp"])
            kb.v(lambda e: e.scalar_tensor_tensor(out=init[0:64, :], in0=dtmp[0:64, :], scalar=mh[0:64, 0:1], in1=cfin[0:64, :], op0=ALU.mult, op1=ALU.add),
                 ["dtmp", "mh", "Cfin"], ["Linit"])
            kb.v(lambda e: e.tensor_tensor(out=dtmp[64:128, :], in0=cfin[64:128, :], in1=g01[64:128, 1, :], op=ALU.subtract), ["g01", "Cfin"], ["dtmp"])
            kb.v(lambda e: e.scalar_tensor_tensor(out=init[64:128, :], in0=dtmp[64:128, :], scalar=mh[64:128, 0:1], in1=g01[64:128, 1, :], op0=ALU.mult, op1=ALU.add),
                 ["dtmp", "mh", "g01"], ["Linit"])
            if DEBUG_F:
                kb.dma(DBGI, init, r=["Linit"])
            kb.P.barrier()
            s5_carries(kb, cst, Qre, Qim, init[:, 0:32], init[:, 32:64], car_re, car_im, NWO, "L")
        kb.P.barrier()
        if stage == 2:
            kb.release(mP2)
            kb.dma(init, D["init"], w=["Linit"])
        b8flat = cst["B8"].rearrange("p g r m -> p (g r m)")
        SPsets = [(r3(kb.bf(16 * J_OWN), 16, J_OWN), r3(kb.bf(16 * J_OWN), 16, J_OWN)),
                  (r3(b8flat[:, 0:16 * J_OWN], 16, J_OWN), r3(b8flat[:, 16 * J_OWN:32 * J_OWN], 16, J_OWN))]
        Yg = r3(kb.bf(16 * J_OWN), 16, J_OWN)

        def s5_half_out(half):
            g0, g1 = 16 * half, 16 * half + 16
            SPre, SPim = SPsets[half]
            tg = "L%d" % half

            def consume(g, pst, kps):
                o = Yg[:, g - g0, :]
                ACT(kb, o, pst[:, 0:J_OWN], AF.Gelu, [kps], ["Yg"])
            s5_output(kb, cst, X, SPre, SPim, J_OWN, consume, tg, g0, g1)
            for ptl in range(2):
                pt = 2 * half + ptl
                for t in range(8):
                    pi = t % 8
                    pst = kb.ps[pi]
                    kps = "ps%d" % pi
                    for gl in range(8):
                        lhs = selF[:, t, 16 * (7 - gl):16 * (7 - gl) + 128]
                        rhs = Yg[:, ptl * 8 + gl, :]
                        MM(kb, pst[:, 0:J_OWN], lhs, rhs, gl == 0, gl == 7, ["selF", "Yg"], [kps])
                    o = ygT[:, pt, t::8]
                    ACT(kb, o, pst[:, 0:J_OWN], AF.Copy, [kps], ["ygT"])
        s5_pass2(kb, cst, X, J_OWN, car_re, car_im, SPsets[0][0], SPsets[0][1], "L0", 0, 16, QD=QD, car_tag="L")
        s5_half_out(0)
        s5_pass2(kb, cst, X, J_OWN, car_re, car_im, SPsets[1][0], SPsets[1][1], "L1", 16, 32, QD=QD, car_tag="L", pre_barrier=False)
        s5_half_out(1)
        kb.P.barrier()
        kb.pb = 0
        kb.pf = mPers[0]
        kb.ptop = kb.capf
        wv = D["w_in"].rearrange("(k p) c -> p k c", p=128)
        wout = r3(kb.bf(8 * 1024), 8, 1024)
        gluw = r3(kb.bf(4 * 512), 4, 512)
        wsT = r3(kb.bf(8 * 128), 8, 128)
        wsec = [r3(kb.bf(8 * 512), 8, 512) for _ in range(4)]
        hms = [r3(kb.bf(8 * 512), 8, 512)]
        act = r3(kb.bf(8 * 512), 8, 512)
        vtm4 = r3(kb.bf(4 * 512), 4, 512)
        utm4 = r3(kb.bf(4 * 512), 4, 512)
        ztm4 = r3(kb.bf(4 * 512), 4, 512)
        g_rep = kb.f32(1024); kb.dma(g_rep, D["g_rep"], w=["g_rep"])
        b_rep = kb.f32(1024); kb.dma(b_rep, D["b_rep"], w=["b_rep"])
        lng = kb.f32(512); kb.dma(lng, D["lng_rep"], w=["lng"])
        lnb = kb.f32(512); kb.dma(lnb, D["lnb_rep"], w=["lnb"])
        glub = kb.f32(4); kb.dma(glub, D["glu_bT"], w=["glub"])
        bstm = kb.f32(8); kb.dma(bstm, D["bs_tm"], w=["bstm"])
        sza = r3(kb.f32(4 * 512), 4, 512)
        sigs = [kb.f32(512) for _ in range(2)]
        gv4 = r3(kb.f32(4 * 512), 4, 512)
        rsets = [(kb.f32(1024), kb.f32(1024), kb.f32(12), kb.f32(2), kb.f32(1)) for _ in range(2)]
        xtiles = [kb.f32(1024) for _ in range(2)]
        sm2 = [(kb.f32(6), kb.f32(2), kb.f32(1)) for _ in range(4)]

        def load_w(i, c0):
            kb.dma(wsec[i], wv[:, :, c0:c0 + 512], w=["wsec%d" % i], q="gpsimd")
            return wsec[i], "wsec%d" % i
        wvv, kwv = load_w(2, 1536)
        wza, kza = load_w(0, 512)
        kb.dma(gluw, D["glu_w"].rearrange("(k p) c -> p k c", p=128), w=["gluw"], q="gpsimd")
        wu, kwu = load_w(1, 1024)
        kb.dma(wsT.rearrange("p a b -> p (a b)"), D["wsT"], w=["wsT"], q="gpsimd")
        wz, kwz = load_w(3, 2048)
        kb.dma(wout, D["w_out"].rearrange("(k p) c -> p k c", p=128), w=["wout"], q="gpsimd")
        kb.v(lambda e: e.tensor_tensor(out=wout, in0=wout, in1=gate_rep.unsqueeze(1).broadcast_to([128, 8, 1024]), op=ALU.mult), ["wout"], ["wout"])
        for tt_ in range(4):
            tsl = slice(tt_ * 512, (tt_ + 1) * 512)
            hm = hms[0]
            khm = "hm0"
            assert kb.pb <= BF_ARENA - 4 * NTOK, kb.pb
            if tt_ == 0:
                kb.dma(hm.rearrange("p a b -> p (a b)"), HM[0], w=[khm])
            EV = [0, 2, 4, 6]
            for s in range(4):
                ssl = slice(s * 128, (s + 1) * 128)
                pst = kb.ps[EV[s]]; kps = "ps%d" % EV[s]
                for k in range(8):
                    MM(kb, pst[:, 0:512], hm[:, k, ssl], wvv[:, k, :], k == 0, k == 7, [kwv, khm], [kps])
                ACT(kb, gv4[:, s, :], pst[:, 0:512], AF.Gelu, [kps], ["gv%d" % s])
            for s in range(4):
                st2, mv2, rs2 = sm2[s]
                gv = gv4[:, s, :]
                kg = "gv%d" % s
                ks = "sm%d" % s
                kb.v(lambda e, st2=st2, gv=gv: e.bn_stats(out=st2, in_=gv), [kg], [ks])
                kb.v(lambda e, st2=st2, mv2=mv2: e.bn_aggr(out=mv2, in_=st2), [ks], [ks])
                kb.v(lambda e, mv2=mv2, rs2=rs2: e.tensor_scalar(out=rs2, in0=mv2[:, 1:2], scalar1=LN_EPS, scalar2=None, op0=ALU.add), [ks], [ks])
                ACT(kb, rs2, rs2, AF.Sqrt, [ks], [ks])
                kb.v(lambda e, rs2=rs2: e.reciprocal(out=rs2, in_=rs2), [ks], [ks])
                kb.v(lambda e, gv=gv, mv2=mv2, rs2=rs2: e.tensor_scalar(out=gv, in0=gv, scalar1=mv2[:, 0:1], scalar2=rs2, op0=ALU.subtract, op1=ALU.mult), [kg, ks], [kg])
                kb.g(lambda e, gv=gv: e.tensor_tensor(out=gv, in0=gv, in1=lng, op=ALU.mult), [kg, "lng"], [kg])
                o = vtm4[:, s, :]
                kb.g(lambda e, gv=gv, o=o: e.tensor_tensor(out=o, in0=gv, in1=lnb, op=ALU.add), [kg, "lnb"], ["vtm%d" % s])
            for cb in range(4):
                pst = kb.ps[EV[cb]]; kps = "ps%d" % EV[cb]
                for k in range(8):
                    MM(kb, pst[:, 0:512], wza[:, k, cb * 128:(cb + 1) * 128], hm[:, k, :], k == 0, k == 7, [kza, khm], [kps])
                ACT(kb, sza[:, cb, :], pst[:, 0:512], AF.Silu, [kps], ["sza%d" % cb])
            for cb in range(4):
                pst = kb.ps[EV[cb]]; kps = "ps%d" % EV[cb]
                sig = sigs[cb % 2]; ksig = "sig%d" % (cb % 2)
                for k in range(4):
                    MM(kb, pst[:, 0:512], gluw[:, k, cb * 128:(cb + 1) * 128], ygT[:, k, tsl], k == 0, k == 3, ["gluw", "ygT"], [kps])
                ACT(kb, sig, pst[:, 0:512], AF.Sigmoid, [kps, "glub"], [ksig], bias=glub[:, cb:cb + 1])
                yv = ygT[:, cb, tsl]
                kb.v(lambda e, yv=yv, sig=sig: e.tensor_tensor(out=sig, in0=sig, in1=yv, op=ALU.mult), [ksig, "ygT"], [ksig])
                o = act[:, cb, :]
                zz = sza[:, cb, :]
                kb.g(lambda e, o=o, zz=zz, sig=sig: e.tensor_tensor(out=o, in0=sig, in1=zz, op=ALU.mult), [ksig, "sza%d" % cb], ["actA%d" % cb])
            if tt_ == 3 and stage == "F":
                wa_pref = r3(kb.ab[:, BF_ARENA - 8 * 1024:BF_ARENA], 8, 1024)
                kb.dma(wa_pref, D1["w_in"].rearrange("(k p) c -> p k c", p=128)[:, :, 0:1024], w=["ygT", "wa"], q="gpsimd")
            for s in range(4):
                ssl = slice(s * 128, (s + 1) * 128)
                pst = kb.ps[2 * s + 1]; kps = "ps%d" % (2 * s + 1)
                for k in range(8):
                    MM(kb, pst[:, 0:512], hm[:, k, ssl], wu[:, k, :], k == 0, k == 7, [kwu, khm], [kps])
                ACT(kb, utm4[:, s, :], pst[:, 0:512], AF.Gelu, [kps], ["utm%d" % s])
            for s in range(4):
                ssl = slice(s * 128, (s + 1) * 128)
                pst = kb.ps[2 * s]; kps = "ps%d" % (2 * s)
                for h in range(8):
                    MM(kb, pst[:, 64 * h:64 * h + 64], wsT[:, h, :], vtm4[:, s, 64 * h:64 * h + 64], True, True, ["wsT", "vtm%d" % s], [kps])
                sgo = gv4[:, s, :]
                kg = "gv%d" % s
                kb.v(lambda e, pst=pst, sgo=sgo: e.tensor_tensor(out=r3(sgo, 8, 64), in0=r3(pst[:, 0:512], 8, 64), in1=bc_last(bstm, 64), op=ALU.add),
                     [kps, "bstm", "vtm%d" % s], [kg])
                us = utm4[:, s, :]
                kb.v(lambda e, sgo=sgo, us=us: e.tensor_tensor(out=sgo, in0=sgo, in1=us, op=ALU.mult), [kg, "utm%d" % s], [kg])
                pz = kb.ps[2 * s + 1]; kpz = "ps%d" % (2 * s + 1)
                for k in range(8):
                    MM(kb, pz[:, 0:512], hm[:, k, ssl], wz[:, k, :], k == 0, k == 7, [kwz, khm], [kpz])
                zt = ztm4[:, s, :]
                ACT(kb, zt, pz[:, 0:512], AF.Silu, [kpz], ["ztm%d" % s])
                kb.g(lambda e, sgo=sgo, zt=zt: e.tensor_tensor(out=sgo, in0=sgo, in1=zt, op=ALU.mult), [kg, "ztm%d" % s], [kg])
            if tt_ < 3:
                kb.dma(hm.rearrange("p a b -> p (a b)"), HM[tt_ + 1], w=[khm])
            for s in range(4):
                ssl = slice(s * 128, (s + 1) * 128)
                pst = kb.ps[2 * s]; kps = "ps%d" % (2 * s)
                for cb in range(4):
                    src = gv4[:, s, cb * 128:(cb + 1) * 128]
                    kb.t(lambda e, pst=pst, src=src, cb=cb: e.transpose(pst[:, cb * 128:(cb + 1) * 128], src, ident), ["gv%d" % s, "ident"], [kps])
                o = act[:, 4:8, ssl]
                ACT(kb, o, r3(pst[:, 0:512], 4, 128), AF.Copy, [kps], ["actB%d" % s])
            actk = ["actA%d" % i for i in range(4)] + ["actB%d" % i for i in range(4)]
            kb.dma(xtiles[0], D["x"][tt_ * 512:tt_ * 512 + 128, :], w=["Ox0"])
            for s in range(4):
                ssl = slice(s * 128, (s + 1) * 128)
                row0 = tt_ * 512 + s * 128
                bi = s % 2
                xtile = xtiles[bi]
                if s < 3:
                    kb.dma(xtiles[1 - bi], D["x"][row0 + 128:row0 + 256, :], w=["Ox%d" % (1 - bi)])
                ys = []
                for h in range(2):
                    pst = kb.ps[(2 * s + 1 + 2 * h) % 8]; kps = "ps%d" % ((2 * s + 1 + 2 * h) % 8)
                    for cb in range(8):
                        MM(kb, pst[:, 0:512], act[:, cb, ssl], wout[:, cb, h * 512:(h + 1) * 512], cb == 0, cb == 7, actk + ["wout"], [kps])
                    ys.append((pst, kps))
                resid_ln(kb, ys, xtile, gate_rep, g_rep, b_rep, X1[row0:row0 + 128, :], rsets[bi], "O", sfx=str(bi))
        kb.P.barrier()
        if stage == "F":
            kb.pf = mPers[0]
            kb.pb = 0
            kb.ptop = kb.capf
            layer1(kb, D1, X1, OUT, HM3, CV3, (CCI2, CCO2), ident, modT1, gate_rep1, wa_pref)
        print("stage", stage, "nops", kb.P.nops, "arena", kb.pf, kb.pb)
        kb.P.emit(kb.sems, block)
    return nc


def layer1(kb, D, x1src, OUT, HM, CV, ccb, ident, modT, gate_rep, wa_pref=None):
    CCI2, CCO2 = ccb
    mh = kb.f32(2)
    kb.dma(mh, D["mhalf"], w=["mh"])
    Hrow = r4(kb.bf(4 * 32 * 94), 4, 32, 94)
    Hcol = r4(kb.bf(4 * 64 * 64), 4, 64, 64)
    kb.v(lambda e: e.memset(Hrow, 0.0), [], ["Hrow"])
    mB = kb.mark()
    wv = D["w_in"].rearrange("(k p) c -> p k c", p=128)
    wg = r3(kb.bf(8 * 1024), 8, 1024)
    kb.dma(wg, wv[:, :, 1024:2048], w=["wg"], q="gpsimd")
    if wa_pref is not None:
        wa = wa_pref
    else:
        wa = r3(kb.bf(8 * 1024), 8, 1024)
        kb.dma(wa, wv[:, :, 0:1024], w=["wa"], q="gpsimd")
    hm = r3(kb.bf(8 * 512), 8, 512)
    xt = r3(kb.f32(4 * 1024), 4, 1024)
    sig = kb.f32(512)
    for tw in range(4):
        make_hmod(kb, x1src[tw * 512:(tw + 1) * 512, :], modT, 0, hm, xt, ident)
        kb.dma(HM[tw], hm.rearrange("p a b -> p (a b)"), r=["hm"], w=["HMw%d" % tw])
        for cb in range(8):
            pa = kb.ps[(cb % 4) * 2]; ka = "ps%d" % ((cb % 4) * 2)
            pg = kb.ps[(cb % 4) * 2 + 1]; kg = "ps%d" % ((cb % 4) * 2 + 1)
            for k in range(8):
                MM(kb, pa[:, 0:512], wa[:, k, cb * 128:(cb + 1) * 128], hm[:, k, :], k == 0, k == 7, ["wa", "hm"] + ([ka] if k else []), [ka])
            for k in range(8):
                MM(kb, pg[:, 0:512], wg[:, k, cb * 128:(cb + 1) * 128], hm[:, k, :], k == 0, k == 7, ["wg", "hm"] + ([kg] if k else []), [kg])
            ACT(kb, sig, pg[:, 0:512], AF.Sigmoid, [kg], ["sig"])
            if cb < 4:
                o = Hrow[:, cb, tw * 8:(tw + 1) * 8, 15:79]
                kk = "Hrow"
            else:
                o = Hcol[:, cb - 4, 16 + tw * 8:16 + (tw + 1) * 8, :]
                kk = "Hcol"
            kb.v(lambda e, o=o, pa=pa: e.tensor_tensor(out=o, in0=r3(pa[:, 0:512], 8, 64), in1=r3(sig, 8, 64), op=ALU.mult), [ka, "sig"], [kk])
    ci = CCI2.rearrange("p (s c r) -> p s c r", s=2, c=4)
    kb.dma(ci[:, 0], Hcol[:, :, 16:32, :].rearrange("p c r w -> p c (r w)"), r=["Hcol"], w=["cci2"])
    kb.dma(ci[:, 1], Hcol[:, :, 32:48, :].rearrange("p c r w -> p c (r w)"), r=["Hcol"], w=["cci2"])
    kb.P.barrier()
    kb.release(mB)
    kb.cc(lambda e: e.collective_compute("AllGather", ALU.bypass, replica_groups=[[0, 1], [2, 3], [4, 5], [6, 7]], ins=[CCI2], outs=[CCO2]),
          "cc1", [], ["cco2"])

    def halo_in():
        co = CCO2.rearrange("(r p) (s c x) -> r p s c x", p=128, s=2, c=4)
        kb.dma(Hcol[:, :, 0:16, :].rearrange("p c r w -> p c (r w)"), co[0, :, 1], r=["cco2"], w=["HcolT"])
        kb.dma(Hcol[:, :, 48:64, :].rearrange("p c r w -> p c (r w)"), co[1, :, 0], r=["cco2"], w=["HcolB"])
        top = Hcol[:, :, 0:16, :]
        bot = Hcol[:, :, 48:64, :]
        kb.v(lambda e: e.tensor_scalar(out=top, in0=top, scalar1=mh[:, 0:1], scalar2=None, op0=ALU.mult), ["HcolT", "mh"], ["HcolT"])
        kb.v(lambda e: e.tensor_scalar(out=bot, in0=bot, scalar1=mh[:, 1:2], scalar2=None, op0=ALU.mult), ["HcolB", "mh"], ["HcolB"])
    layer1_tail(kb, D, x1src, 0, OUT, HM, CV, Hrow, Hcol, ident, gate_rep, halo_in)


def layer1_tail(kb, D, xsrc, xoff, OUT, HM, CV, Hrow, Hcol, ident, gate_rep, halo_in=None):
    if True:
        mC = kb.mark()
        identb = kb.bf(128)
        kb.v(lambda e: e.tensor_copy(out=identb, in_=ident), ["ident"], ["identb"])
        dwT = r3(kb.f32(8 * 31), 8, 31)
        kb.dma(dwT.rearrange("p a b -> p (a b)"), D["dwT"], w=["dwT"])
        dwb = kb.f32(8); kb.dma(dwb, D["dwbT"], w=["dwb"])
        dgs = [r3(kb.bf(31 * 128), 31, 128) for _ in range(2)]
        cvo = [kb.f32(512) for _ in range(2)]
        ncv = 0
        for cb in range(8):
            if cb == 4 and halo_in is not None:
                halo_in()
            wk = dwT[:, cb, :].unsqueeze(2).broadcast_to([128, 31, 128])
            ib = identb.unsqueeze(1).broadcast_to([128, 31, 128])
            dg = dgs[cb % 2]
            kb.g(lambda e, wk=wk, dg=dg: e.tensor_tensor(out=dg, in0=ib, in1=wk, op=ALU.mult), ["identb", "dwT"], ["dg%d" % (cb % 2)])
            dgk = ["dg%d" % (cb % 2)]
            for t4 in range(4):
                pst = kb.ps[t4]; kps = "ps%d" % t4
                for k in range(31):
                    if cb < 4:
                        rhs = Hrow[:, cb, t4 * 8:(t4 + 1) * 8, k:k + 64]
                    else:
                        r0 = t4 * 8 + k + 1
                        rhs = Hcol[:, cb - 4, r0:r0 + 8, :]
                    MM(kb, pst[:, 0:512], dg[:, k, :], rhs, k == 0, k == 30, dgk + (["Hrow"] if cb < 4 else ["Hcol", "HcolT", "HcolB"]), [kps])
                c = cvo[ncv % 2]; kc = "cvo%d" % (ncv % 2); ncv += 1
                ACT(kb, c, pst[:, 0:512], AF.Identity, [kps, "dwb"], [kc], bias=dwb[:, cb:cb + 1])
                kb.dma(CV[cb][:, t4 * 512:(t4 + 1) * 512], c, r=[kc], w=["CV"])
        kb.P.barrier()
        kb.release(mC)
        kb.pb = 0
        wout = r3(kb.bf(8 * 1024), 8, 1024)
        kb.dma(wout, D["w_out"].rearrange("(k p) c -> p k c", p=128), w=["wout"], q="gpsimd")
        kb.v(lambda e: e.tensor_tensor(out=wout, in0=wout, in1=gate_rep.unsqueeze(1).broadcast_to([128, 8, 1024]), op=ALU.mult), ["wout"], ["wout"])
        wz = [r3(kb.bf(8 * 512), 8, 512) for _ in range(2)]
        hm = r3(kb.bf(8 * 512), 8, 512)
        act = r3(kb.bf(8 * 512), 8, 512)
        ones = kb.f32(128)
        kb.v(lambda e: e.memset(ones, 1.0), [], ["ones"])
        g_rep = kb.f32(1024); kb.dma(g_rep, D["g_rep"], w=["g_rep"])
        b_rep = kb.f32(1024); kb.dma(b_rep, D["b_rep"], w=["b_rep"])
        clng = kb.f32(8); kb.dma(clng, D["clngT"], w=["clng"])
        clnb = kb.f32(8); kb.dma(clnb, D["clnbT"], w=["clnb"])
        cvt = r3(kb.f32(8 * 512), 8, 512)
        mean = kb.f32(512); rstd = kb.f32(512)
        sqs = [kb.f32(512) for _ in range(2)]
        nrms = [kb.f32(512) for _ in range(2)]
        szs = [kb.f32(512) for _ in range(2)]
        f32b = lambda n: kb.bf(2 * n).bitcast(F32)
        rsets = [(kb.f32(1024), kb.f32(1024), kb.f32(12), kb.f32(2), kb.f32(1)),
                 (f32b(1024), f32b(1024), kb.f32(12), kb.f32(2), kb.f32(1))]
        xtiles = [kb.f32(1024), f32b(1024)]
        wzv = D["w_in"].rearrange("(k p) c -> p k c", p=128)
        for zh in range(2):
            kb.dma(wz[zh], wzv[:, :, 2048 + zh * 512:2048 + (zh + 1) * 512], w=["wz%d" % zh], q="gpsimd")
        def c2_loads(t4):
            kb.dma(hm.rearrange("p a b -> p (a b)"), HM[t4], w=["hm"])
            for cb in range(8):
                kb.dma(cvt[:, cb, :], CV[cb][:, t4 * 512:(t4 + 1) * 512], w=["cvt%d" % cb])
        p1 = kb.ps[6]; p2 = kb.ps[7]

        def c2_stats_mm(cbs):
            for cb in cbs:
                sq = sqs[cb % 2]; ksq = "sq%d" % (cb % 2)
                MM(kb, p1[:, 0:512], ones, cvt[:, cb, :], cb == 0, cb == 7, ["ones", "cvt%d" % cb], ["ps6"])
                ACT(kb, sq, cvt[:, cb, :], AF.Square, ["cvt%d" % cb], [ksq])
                MM(kb, p2[:, 0:512], ones, sq, cb == 0, cb == 7, ["ones", ksq], ["ps7"])
        szb = r3(kb.bf(8 * 512), 8, 512)

        def c2_zproj(cbs):
            for cb in cbs:
                w = wz[cb // 4]; kw = "wz%d" % (cb // 4)
                pst = kb.ps[cb % 2]; kps = "ps%d" % (cb % 2)
                for k in range(8):
                    MM(kb, pst[:, 0:512], w[:, k, (cb % 4) * 128:(cb % 4 + 1) * 128], hm[:, k, :], k == 0, k == 7, [kw, "hm"], [kps])
                ACT(kb, szb[:, cb, :], pst[:, 0:512], AF.Silu, [kps], ["szb%d" % cb])
        c2_loads(0)
        c2_stats_mm(range(8))
        c2_zproj(range(8))
        for t4 in range(4):
            sq = sqs[0]
            kb.v(lambda e: e.tensor_scalar(out=mean, in0=p1[:, 0:512], scalar1=1.0 / 1024, scalar2=None, op0=ALU.mult), ["ps6"], ["mean"])
            kb.v(lambda e, sq=sq: e.tensor_tensor(out=sq, in0=mean, in1=mean, op=ALU.mult), ["mean"], ["sq0"])
            kb.v(lambda e, sq=sq: e.scalar_tensor_tensor(out=rstd, in0=p2[:, 0:512], scalar=1.0 / 1024, in1=sq, op0=ALU.mult, op1=ALU.subtract), ["ps7", "sq0"], ["rstd"])
            kb.v(lambda e: e.tensor_scalar(out=rstd, in0=rstd, scalar1=LN_EPS, scalar2=None, op0=ALU.add), ["rstd"], ["rstd"])
            ACT(kb, rstd, rstd, AF.Sqrt, ["rstd"], ["rstd"])
            kb.v(lambda e: e.reciprocal(out=rstd, in_=rstd), ["rstd"], ["rstd"])
            for cb in range(8):
                nrm = nrms[cb % 2]; kn = "nrm%d" % (cb % 2)
                sz = szb[:, cb, :]; kz = "szb%d" % cb
                kb.v(lambda e, cb=cb, nrm=nrm: e.tensor_tensor(out=nrm, in0=cvt[:, cb, :], in1=mean, op=ALU.subtract), ["cvt%d" % cb, "mean"], [kn])
                kb.v(lambda e, nrm=nrm: e.tensor_tensor(out=nrm, in0=nrm, in1=rstd, op=ALU.mult), [kn, "rstd"], [kn])
                ACT(kb, nrm, nrm, AF.Silu, [kn, "clng", "clnb"], [kn], scale=clng[:, cb:cb + 1], bias=clnb[:, cb:cb + 1])
                o = act[:, cb, :]
                kb.g(lambda e, o=o, nrm=nrm, sz=sz: e.tensor_tensor(out=o, in0=nrm, in1=sz, op=ALU.mult), [kn, kz], ["act%d" % cb])
            actk = ["act%d" % i for i in range(8)]
            if t4 < 3:
                c2_loads(t4 + 1)
            kb.dma(xtiles[0], xsrc[xoff + t4 * 512:xoff + t4 * 512 + 128, :], w=["Ox0"])
            for s in range(4):
                ssl = slice(s * 128, (s + 1) * 128)
                row0 = t4 * 512 + s * 128
                bi = s % 2
                xtile = xtiles[bi]
                if s < 3:
                    kb.dma(xtiles[1 - bi], xsrc[xoff + row0 + 128:xoff + row0 + 256, :], w=["Ox%d" % (1 - bi)])
                ys = []
                for h in range(2):
                    pi = 4 + (2 * s + h) % 2 if False else (4 + h if s % 2 == 0 else 2 + h)
                    pst = kb.ps[pi]; kps = "ps%d" % pi
                    for cb in range(8):
                        MM(kb, pst[:, 0:512], act[:, cb, ssl], wout[:, cb, h * 512:(h + 1) * 512], cb == 0, cb == 7, actk + ["wout"], [kps])
                    ys.append((pst, kps))
                if t4 < 3 and s in (0, 2):
                    c2_zproj(range(0, 4) if s == 0 else range(4, 8))
                if t4 < 3 and s in (1, 3):
                    c2_stats_mm(range(0, 4) if s == 1 else range(4, 8))
                resid_ln(kb, ys, xtile, gate_rep, g_rep, b_rep, OUT[row0:row0 + 128, :], rsets[bi], "O", sfx=str(bi))


def build3():
    nc = bass.Bass("TRN2", target_bir_lowering=False)
    D = {}

    def inp(name, shape):
        D[name] = nc.dram_tensor(name, shape, F32, kind="ExternalInput").ap()
    inp("xw", [4096, 1024]); inp("maskw", [128, 4096]); inp("ident", [128, 128])
    inp("cT", [128, 16]); inp("modw", [1024, 3072]); inp("modbT", [128, 24]); inp("modb_gate_rep", [128, 1024])
    inp("w_in", [1024, 3072]); inp("w_out", [1024, 1024]); inp("g_rep", [128, 1024]); inp("b_rep", [128, 1024])
    inp("dwT", [128, 8 * 31]); inp("dwbT", [128, 8]); inp("clngT", [128, 8]); inp("clnbT", [128, 8])
    OUT = nc.dram_tensor("OUT", [NTOK, 1024], F32, kind="ExternalOutput").ap()
    HM = nc.dram_tensor("HM3", [4, 128, 8 * 512], BF16, kind="Internal").ap()
    CV = nc.dram_tensor("CV3", [8, 128, NTOK], F32, kind="Internal").ap()
    with contextlib.ExitStack() as es:
        kb = KB(nc, es, F32_ARENA, BF_ARENA)
        kb.boot_clear()
        block = es.enter_context(nc.Block())
        ident = kb.f32(128)
        kb.dma(ident, D["ident"], w=["ident"])
        modT, gate_rep = adaln(kb, D)
        Hrow = r4(kb.bf(4 * 32 * 94), 4, 32, 94)
        Hcol = r4(kb.bf(4 * 64 * 64), 4, 64, 64)
        kb.v(lambda e: e.memset(Hrow, 0.0), [], ["Hrow"])
        mB = kb.mark()
        wv = D["w_in"].rearrange("(k p) c -> p k c", p=128)
        wa = r3(kb.bf(8 * 1024), 8, 1024)
        wg = r3(kb.bf(8 * 1024), 8, 1024)
        kb.dma(wa, wv[:, :, 0:1024], w=["wa"], q="gpsimd")
        kb.dma(wg, wv[:, :, 1024:2048], w=["wg"], q="gpsimd")
        hm = r3(kb.bf(8 * 512), 8, 512)
        xt = r3(kb.f32(4 * 1024), 4, 1024)
        msk = kb.f32(512)
        sig = kb.f32(512)
        for tw in range(8):
            own = 2 <= tw < 6
            make_hmod(kb, D["xw"][tw * 512:(tw + 1) * 512, :], modT, 0, hm, xt, ident)
            if own:
                kb.dma(HM[tw - 2], hm.rearrange("p a b -> p (a b)"), r=["hm"], w=["HMw%d" % tw])
            kb.dma(msk, D["maskw"][:, tw * 512:(tw + 1) * 512], w=["msk"])
            for cb in (range(8) if own else range(4, 8)):
                pa = kb.ps[(cb % 4) * 2]; ka = "ps%d" % ((cb % 4) * 2)
                pg = kb.ps[(cb % 4) * 2 + 1]; kg = "ps%d" % ((cb % 4) * 2 + 1)
                for k in range(8):
                    MM(kb, pa[:, 0:512], wa[:, k, cb * 128:(cb + 1) * 128], hm[:, k, :], k == 0, k == 7, ["wa", "hm"] + ([ka] if k else []), [ka])
                for k in range(8):
                    MM(kb, pg[:, 0:512], wg[:, k, cb * 128:(cb + 1) * 128], hm[:, k, :], k == 0, k == 7, ["wg", "hm"] + ([kg] if k else []), [kg])
                ACT(kb, sig, pg[:, 0:512], AF.Sigmoid, [kg], ["sig"])
                kb.v(lambda e, pa=pa: e.tensor_tensor(out=sig, in0=pa[:, 0:512], in1=sig, op=ALU.mult), [ka, "sig"], ["sig"])
                if cb < 4:
                    o = Hrow[:, cb, (tw - 2) * 8:(tw - 1) * 8, 15:79]
                    kb.g(lambda e, o=o: e.tensor_tensor(out=o, in0=r3(sig, 8, 64), in1=r3(msk, 8, 64), op=ALU.mult), ["sig", "msk"], ["Hrow"])
                else:
                    o = Hcol[:, cb - 4, tw * 8:(tw + 1) * 8, :]
                    kb.g(lambda e, o=o: e.tensor_tensor(out=o, in0=r3(sig, 8, 64), in1=r3(msk, 8, 64), op=ALU.mult), ["sig", "msk"], ["Hcol"])
        kb.P.barrier()
        kb.release(mB)
        layer1_tail(kb, D, D["xw"], 1024, OUT, HM, CV, Hrow, Hcol, ident, gate_rep)
        kb.P.barrier()
        print("stage3 nops", kb.P.nops, "arena", kb.pf, kb.pb)
        kb.P.emit(kb.sems, block)
    return nc


def host_consts():
    selF = np.zeros((128, 8, 240), np.float32)
    for gl in range(8):
        for h in range(16):
            selF[16 * gl + h, gl, 112 + h] = 1.0
    ident = np.eye(128, dtype=np.float32)
    si = np.arange(128) // 16
    maskF = (si[None, :] >= si[:, None]).astype(np.float32)
    maskB = (si[None, :] <= si[:, None]).astype(np.float32)
    return dict(selF=selF.reshape(128, -1), ident=ident, maskF=maskF, maskB=maskB)


def s5_host_layout(lam_re, lam_im, log_dt, b_re, b_im, c_re, c_im, d):
    o = {}
    o["lamre"] = lam_re.transpose(0, 2, 1).reshape(128, 32)
    o["lamim"] = lam_im.transpose(0, 2, 1).reshape(128, 32)
    o["logdt"] = np.repeat(log_dt[:, None, :], 64, axis=1).reshape(128, 32)
    o["bre"] = b_re.transpose(0, 2, 1, 3).reshape(128, 512)
    o["bim"] = b_im.transpose(0, 2, 1, 3).reshape(128, 512)
    o["cre"] = c_re.transpose(0, 3, 1, 2).reshape(128, 512)
    o["cim"] = c_im.transpose(0, 3, 1, 2).reshape(128, 512)
    o["dT"] = np.tile(d.reshape(32, 16).T, (8, 1))
    return {k: np.ascontiguousarray(v, dtype=np.float32) for k, v in o.items()}


def colT(v, n):
    return np.ascontiguousarray(np.asarray(v, np.float32).reshape(n, 128).T)


def rep(v):
    return np.ascontiguousarray(np.tile(np.asarray(v, np.float32)[None, :], (128, 1)))


_NC = {}


def _get(stage):
    if stage not in _NC:
        _NC[stage] = build12(stage) if stage in (1, 2, "F") else build3()
    return _NC[stage]


def kernel(x, c, ctx, c_ctx, mod_w, mod_b, norm_g, norm_b, ev_w_in, ev_w_out,
           s5_lam_re, s5_lam_im, s5_log_dt, s5_b_re, s5_b_im, s5_c_re, s5_c_im, s5_d,
           glu_w, glu_b, sgu_ln_g, sgu_ln_b, sgu_w, sgu_b,
           od_w_in, od_w_out, dw_w, dw_b, conv_ln_g, conv_ln_b):
    A = lambda a: np.asarray(a, dtype=np.float32)
    x, c, ctx, c_ctx, mod_w, mod_b = A(x), A(c), A(ctx), A(c_ctx), A(mod_w), A(mod_b)
    cores = list(range(8))
    base = dict(host_consts())
    base.update(s5_host_layout(A(s5_lam_re)[0], A(s5_lam_im)[0], A(s5_log_dt)[0], A(s5_b_re)[0], A(s5_b_im)[0],
                               A(s5_c_re)[0], A(s5_c_im)[0], A(s5_d)[0]))

    def cT_of(b):
        return np.ascontiguousarray(np.stack([c[b].reshape(8, 128).T, c_ctx.reshape(8, 128).T], axis=2).reshape(128, 16))

    def mod_of(l, sfx=""):
        return {"modw" + sfx: np.ascontiguousarray(mod_w[l]), "modbT" + sfx: colT(mod_b[l], 24),
                "modb_gate_rep" + sfx: rep(mod_b[l][2048:3072])}
    shared = dict(base)
    shared.update(mod_of(0))
    shared.update(mod_of(1, "_1"))
    shared.update(w_in=np.ascontiguousarray(A(ev_w_in)[0]), g_rep=rep(A(norm_g)[0]), b_rep=rep(A(norm_b)[0]),
                  glu_w=np.ascontiguousarray(A(glu_w)[0]), glu_bT=colT(A(glu_b)[0], 4),
                  lng_rep=rep(A(sgu_ln_g)[0]), lnb_rep=rep(A(sgu_ln_b)[0]),
                  wsT=np.ascontiguousarray(A(sgu_w)[0].transpose(2, 0, 1).reshape(128, 1024)),
                  bs_tm=np.ascontiguousarray(A(sgu_b)[0].T), w_out=np.ascontiguousarray(A(ev_w_out)[0]),
                  w_in_1=np.ascontiguousarray(A(od_w_in)[0]), w_out_1=np.ascontiguousarray(A(od_w_out)[0]),
                  g_rep_1=rep(A(norm_g)[1]), b_rep_1=rep(A(norm_b)[1]),
                  dwT_1=np.ascontiguousarray(A(dw_w)[0].T.reshape(8, 128, 31).transpose(1, 0, 2).reshape(128, 8 * 31)),
                  dwbT_1=colT(A(dw_b)[0], 8), clngT_1=colT(A(conv_ln_g)[0], 8), clnbT_1=colT(A(conv_ln_b)[0], 8))
    maps = []
    for core in cores:
        b, half = core // 2, core % 2
        m = dict(shared)
        mh = np.zeros((128, 2), np.float32)
        mh[:, 0] = float(half)
        mh[:, 1] = 1.0 - float(half)
        m.update(x=np.ascontiguousarray(x[b, half * 2048:(half + 1) * 2048]), ctx=np.ascontiguousarray(ctx[b]),
                 cT=cT_of(b), cT_1=cT_of(b), mhalf=mh)
        maps.append(m)
    res = run_bass_kernel_spmd(_get("F"), maps, core_ids=cores).results
    out = np.zeros((4, 4096, 1024), np.float32)
    for core in cores:
        b, half = core // 2, core % 2
        out[b, half * 2048:(half + 1) * 2048] = res[core]["OUT"]
    return out
```

```python
import math
import contextlib
import numpy as np
import concourse.bass as bass
import concourse.mybir as mybir
from concourse.bass_utils import run_bass_kernel_spmd

F32 = mybir.dt.float32
BF16 = mybir.dt.bfloat16
ALU = mybir.AluOpType
AF = mybir.ActivationFunctionType

NDMASEM = 12
W = 32
T = 8
G = 32
J_OWN = 256
J_CTX = 32


class Prog:
    ENGS = ["sync", "scalar", "vector", "gpsimd", "tensor"]

    def __init__(self, nc, dma_queues=("sync", "gpsimd")):
        self.nc = nc
        self.ops = {e: [] for e in self.ENGS}
        self.cnt = {}
        self.last_w = {}
        self.readers = {}
        self.seen = {e: {} for e in self.ENGS}
        self.dma_i = {q: 0 for q in dma_queues}
        self.semnames = list(self.ENGS)
        for q in dma_queues:
            for i in range(NDMASEM):
                self.semnames.append(f"{q}_d{i}")
        for s in self.semnames:
            self.cnt[s] = 0
        self.nops = 0

    def op(self, eng, fn, reads=(), writes=(), dma=False):
        reads = [reads] if isinstance(reads, str) else list(reads)
        writes = [writes] if isinstance(writes, str) else list(writes)
        deps = {}

        def add(sv):
            if sv is not None and sv[1] > deps.get(sv[0], 0):
                deps[sv[0]] = sv[1]

        for k in reads:
            add(self.last_w.get(k))
        for k in writes:
            add(self.last_w.get(k))
            for r in self.readers.get(k, ()):
                add(r)
        if dma:
            i = self.dma_i[eng]
            self.dma_i[eng] += 1
            sem = f"{eng}_d{i % NDMASEM}"
            inc = 16
            if self.cnt[sem] > 0:
                deps[sem] = max(deps.get(sem, 0), self.cnt[sem])
        else:
            sem = eng
            inc = 1
        waits = []
        for s, v in deps.items():
            if s == "tensor" and eng == "tensor":
                continue
            if v > self.seen[eng].get(s, 0):
                waits.append((s, v))
                self.seen[eng][s] = v
        self.cnt[sem] += inc
        val = self.cnt[sem]
        self.ops[eng].append((waits, fn, sem, inc))
        for k in writes:
            self.last_w[k] = (sem, val)
            self.readers[k] = []
        for k in reads:
            self.readers.setdefault(k, []).append((sem, val))
        self.nops += 1

    def cc(self, fn, sem, reads, writes):
        deps = {}
        for kx in reads:
            sv = self.last_w.get(kx)
            if sv is not None and sv[1] > deps.get(sv[0], 0):
                deps[sv[0]] = sv[1]
        waits = []
        for s_, v in deps.items():
            if v > self.seen["gpsimd"].get(s_, 0):
                waits.append((s_, v))
                self.seen["gpsimd"][s_] = v
        self.ops["gpsimd"].append((waits, fn, sem, 1))
        self.cnt[sem] = 1
        for kx in writes:
            self.last_w[kx] = (sem, 1)
            self.readers[kx] = []
        self.nops += 1

    barrier_skip = ()

    def barrier(self):
        skip = set(self.barrier_skip)
        for eng in self.ENGS:
            waits = [(s, v) for s, v in self.cnt.items() if s not in skip and v > self.seen[eng].get(s, 0)]
            for s, v in waits:
                self.seen[eng][s] = v
            if waits:
                self.ops[eng].append((waits, None, None, 0))
        self.last_w = {k_: sv for k_, sv in self.last_w.items() if sv[0] in skip}
        self.readers = {}

    def emit(self, sems, block):
        prog = self

        def replay(engname, eng):
            for waits, fn, sem, inc in prog.ops[engname]:
                for s, v in waits:
                    eng.wait_ge(sems[s], v)
                if fn is not None:
                    fn(eng).then_inc(sems[sem], inc)

        @block.sync
        def _(e):
            replay("sync", e)

        @block.scalar
        def _(e):
            replay("scalar", e)

        @block.vector
        def _(e):
            replay("vector", e)

        @block.gpsimd
        def _(e):
            replay("gpsimd", e)

        @block.tensor
        def _(e):
            replay("tensor", e)


class KB:
    def __init__(self, nc, es, f32_elems, bf_elems):
        self.nc = nc
        self.es = es
        self.P = Prog(nc)
        self.sems = {s: es.enter_context(nc.semaphore(s)) for s in self.P.semnames}
        self.af = es.enter_context(nc.sbuf_tensor("arena_f", [128, f32_elems], F32))
        self.ab = es.enter_context(nc.sbuf_tensor("arena_b", [128, bf_elems], BF16))
        self.pf = 0
        self.pb = 0
        self.capf = f32_elems
        self.ptop = f32_elems
        self.capb = bf_elems
        self.ps = [es.enter_context(nc.psum_tensor(f"ps{i}", [128, 512], F32)) for i in range(8)]
        self.uid = 0

    def boot_clear(self):
        for s in self.sems.values():
            self.nc.gpsimd.sem_clear(s)
        self.nc.all_engine_barrier()

    def f32t(self, n):
        self.ptop -= n
        assert self.ptop >= self.pf, ("f32 arena overflow (top)", self.ptop, self.pf)
        return self.af[:, self.ptop:self.ptop + n]

    def mark(self):
        return (self.pf, self.pb)

    def release(self, m):
        self.pf, self.pb = m

    def f32(self, n):
        a = self.af[:, self.pf:self.pf + n]
        self.pf += n
        assert self.pf <= self.ptop, ("f32 arena overflow", self.pf, self.ptop)
        return a

    def bf(self, n):
        n = (n + 1) // 2 * 2
        a = self.ab[:, self.pb:self.pb + n]
        self.pb += n
        assert self.pb <= self.capb, ("bf16 arena overflow", self.pb)
        return a

    def key(self, pfx="k"):
        self.uid += 1
        return f"{pfx}{self.uid}"

    def v(self, fn, r=(), w=()):
        self.P.op("vector", fn, r, w)

    def g(self, fn, r=(), w=()):
        self.P.op("gpsimd", fn, r, w)

    def a(self, fn, r=(), w=()):
        self.P.op("scalar", fn, r, w)

    def t(self, fn, r=(), w=()):
        self.P.op("tensor", fn, r, w)

    def cc(self, fn, sem, r, w):
        self.P.cc(fn, sem, r, w)

    def dma(self, out, in_, r=(), w=(), q="sync"):
        self.P.op(q, lambda e: e.dma_start(out=out, in_=in_), r, w, dma=True)

    def tt(self, eng, out, a, b, op, r, w):
        self.P.op(eng, lambda e: e.tensor_tensor(out=out, in0=a, in1=b, op=op), r, w)

    def cmul(self, ore, oim, are, aim, bre, bim, t1, t2, r, w, kt, engs=("vector", "gpsimd")):
        e0, e1 = engs
        k = [kt + "_t%d" % i for i in range(4)]
        self.tt(e0, t1[0], are, bre, ALU.mult, r, [k[0]])
        self.tt(e0, t2[0], aim, bim, ALU.mult, r, [k[1]])
        self.tt(e0, ore, t1[0], t2[0], ALU.subtract, [k[0], k[1]], w)
        self.tt(e1, t1[1], are, bim, ALU.mult, r, [k[2]])
        self.tt(e1, t2[1], aim, bre, ALU.mult, r, [k[3]])
        self.tt(e1, oim, t1[1], t2[1], ALU.add, [k[2], k[3]], w)


def r3(ap, a, b):
    return ap.rearrange("p (a b) -> p a b", a=a, b=b)


def r4(ap, a, b, c):
    return ap.rearrange("p (a b c) -> p a b c", a=a, b=b, c=c)


def bc_last(ap2, n):
    return ap2.unsqueeze(2).broadcast_to([ap2.shape[0], ap2.shape[1], n])


def s5_pre_init(kb, din, need_out=True):
    P = kb.P
    out = {}
    B8 = kb.bf(G * 2 * 128)
    out["B8"] = r4(B8, G, 2, 128)
    if need_out:
        C8b = kb.bf(2 * G * 128)
        out["C8"] = r4(C8b, 2, G, 128)
        Toep = kb.bf(G * 128)
        out["Toep"] = r3(Toep, G, 128)
    tabs = {}
    for nm in ["Tpre", "Tpim", "Ture", "Tuim"]:
        tabs[nm] = r3(kb.f32(G * W), G, W)
    out.update(tabs)
    rmask = kb.f32(W)
    out["rmask"] = rmask
    ptop0 = kb.ptop
    sm = {}
    for nm in ["lamre", "lamim", "logdt", "dT"]:
        sm[nm] = kb.f32t(G)
        kb.dma(sm[nm], din[nm], w=[nm])
    big = {}
    for nm in ["bre", "bim", "cre", "cim"]:
        big[nm] = r3(kb.f32t(G * 16), G, 16)
        kb.dma(big[nm], din[nm].rearrange("p (g h) -> p g h", g=G), w=[nm])
    ident = kb.f32t(128)
    kb.dma(ident, din["ident"], w=["ident"])
    maskF = kb.f32t(128)
    maskB = kb.f32t(128)
    kb.dma(maskF, din["maskF"], w=["maskF"])
    kb.dma(maskB, din["maskB"], w=["maskB"])
    S = lambda: kb.f32t(G)
    dt, are, th, mag, c, s, cc, ss, cs = [S() for _ in range(9)]
    kb.a(lambda e: e.activation(out=dt, in_=sm["logdt"], func=AF.Exp), ["logdt"], ["dt"])
    kb.tt("vector", are, sm["lamre"], dt, ALU.mult, ["lamre", "dt"], ["are"])
    kb.tt("vector", th, sm["lamim"], dt, ALU.mult, ["lamim", "dt"], ["th"])
    kb.a(lambda e: e.activation(out=mag, in_=are, func=AF.Exp), ["are"], ["mag"])
    NSQ = 6
    halfpi = kb.f32t(1)
    kb.v(lambda e: e.memset(halfpi, math.pi / 2), [], ["halfpi"])
    kb.a(lambda e: e.activation(out=s, in_=th, func=AF.Sin, scale=1.0 / 2 ** NSQ), ["th"], ["s"])
    kb.a(lambda e: e.activation(out=c, in_=th, func=AF.Sin, scale=1.0 / 2 ** NSQ, bias=halfpi), ["th", "halfpi"], ["c"])
    for _ in range(NSQ):
        kb.tt("vector", cc, c, c, ALU.mult, ["c"], ["cc"])
        kb.tt("vector", ss, s, s, ALU.mult, ["s"], ["ss"])
        kb.tt("vector", cs, c, s, ALU.mult, ["c", "s"], ["cs"])
        kb.tt("vector", c, cc, ss, ALU.subtract, ["cc", "ss"], ["c"])
        kb.v(lambda e: e.tensor_scalar(out=s, in0=cs, scalar1=2.0, scalar2=None, op0=ALU.mult), ["cs"], ["s"])
    pwre = r3(kb.f32t(9 * G), 9, G)
    pwim = r3(kb.f32t(9 * G), 9, G)
    kb.v(lambda e: e.memset(pwre[:, 0, :], 1.0), [], ["pw0"])
    kb.v(lambda e: e.memset(pwim[:, 0, :], 0.0), [], ["pw0"])
    kb.tt("vector", pwre[:, 1, :], mag, c, ALU.mult, ["mag", "c"], ["pw1"])
    kb.tt("vector", pwim[:, 1, :], mag, s, ALU.mult, ["mag", "s"], ["pw1"])
    t1 = [S(), S()]
    t2 = [S(), S()]
    for k in range(2, 9):
        kb.cmul(pwre[:, k, :], pwim[:, k, :], pwre[:, k - 1, :], pwim[:, k - 1, :], pwre[:, 1, :], pwim[:, 1, :],
                t1, t2, ["pw%d" % (k - 1), "pw1"], ["pw%d" % k], "pwt")
    nre, den, rden, qre, qim, u1, u2 = [S() for _ in range(7)]
    kb.v(lambda e: e.tensor_scalar(out=nre, in0=pwre[:, 1, :], scalar1=-1.0, scalar2=None, op0=ALU.add), ["pw1"], ["nre"])
    nim = pwim[:, 1, :]
    kb.tt("vector", u1, sm["lamre"], sm["lamre"], ALU.mult, ["lamre"], ["u1"])
    kb.tt("vector", u2, sm["lamim"], sm["lamim"], ALU.mult, ["lamim"], ["u2"])
    kb.tt("vector", den, u1, u2, ALU.add, ["u1", "u2"], ["den"])
    kb.v(lambda e: e.reciprocal(out=rden, in_=den), ["den"], ["rden"])
    kb.tt("vector", u1, nre, sm["lamre"], ALU.mult, ["nre", "lamre"], ["u1"])
    kb.tt("vector", u2, nim, sm["lamim"], ALU.mult, ["pw1", "lamim"], ["u2"])
    kb.tt("vector", qre, u1, u2, ALU.add, ["u1", "u2"], ["qre0"])
    kb.tt("vector", qre, qre, rden, ALU.mult, ["qre0", "rden"], ["qre"])
    kb.tt("vector", u1, nim, sm["lamre"], ALU.mult, ["pw1", "lamre"], ["u1"])
    kb.tt("vector", u2, nre, sm["lamim"], ALU.mult, ["nre", "lamim"], ["u2"])
    kb.tt("vector", qim, u1, u2, ALU.subtract, ["u1", "u2"], ["qim0"])
    kb.tt("vector", qim, qim, rden, ALU.mult, ["qim0", "rden"], ["qim"])
    L = lambda: r3(kb.f32t(G * 16), G, 16)
    bbre, bbim = L(), L()
    mL = kb.ptop
    lt1, lt2 = [L(), L()], [L(), L()]
    kb.cmul(bbre, bbim, bc_last(qre, 16), bc_last(qim, 16), big["bre"], big["bim"], lt1, lt2,
            ["qre", "qim", "bre", "bim"], ["bbar"], "bbt")
    kb.P.barrier()
    kb.ptop = mL
    l8re, l8im = pwre[:, 8, :], pwim[:, 8, :]
    i8re, i8im = S(), S()
    kb.tt("vector", u1, l8re, l8re, ALU.mult, ["pw8"], ["u1"])
    kb.tt("vector", u2, l8im, l8im, ALU.mult, ["pw8"], ["u2"])
    kb.tt("vector", den, u1, u2, ALU.add, ["u1", "u2"], ["den"])
    kb.v(lambda e: e.reciprocal(out=rden, in_=den), ["den"], ["rden"])
    kb.tt("vector", i8re, l8re, rden, ALU.mult, ["pw8", "rden"], ["i8"])
    kb.tt("vector", u1, l8im, rden, ALU.mult, ["pw8", "rden"], ["u1"])
    kb.v(lambda e: e.tensor_scalar(out=i8im, in0=u1, scalar1=-1.0, scalar2=None, op0=ALU.mult), ["u1"], ["i8"])
    mW = kb.ptop
    for (tre, tim, bre_, bim_, nm, dep) in [(tabs["Ture"], tabs["Tuim"], l8re, l8im, "Tu", "pw8"),
                                           (tabs["Tpre"], tabs["Tpim"], i8re, i8im, "Tp", "i8")]:
        kb.v(lambda e, tre=tre, bre_=bre_: e.tensor_copy(out=tre[:, :, 0], in_=bre_), [dep], [nm])
        kb.v(lambda e, tim=tim, bim_=bim_: e.tensor_copy(out=tim[:, :, 0], in_=bim_), [dep], [nm])
        n = 1
        if nm == "Tu":
            wt1 = [r3(kb.f32t(G * W // 2), G, W // 2) for _ in range(2)]
            wt2 = [r3(kb.f32t(G * W // 2), G, W // 2) for _ in range(2)]
        else:
            kb.P.barrier()
        while n < W:
            kb.cmul(tre[:, :, n:2 * n], tim[:, :, n:2 * n], tre[:, :, 0:n], tim[:, :, 0:n],
                    bc_last(tre[:, :, n - 1], n), bc_last(tim[:, :, n - 1], n),
                    [x[:, :, 0:n] for x in wt1], [x[:, :, 0:n] for x in wt2], [nm], [nm], nm + "t")
            n *= 2
    kb.P.barrier()
    kb.ptop = mW
    kb.v(lambda e: e.memset(rmask, 1.0), [], ["rmask"])
    kb.v(lambda e: e.memset(rmask[:, 0:1], 0.0), ["rmask"], ["rmask"])
    PWbre, PWbim, PWcre, PWcim = [r3(kb.f32t(G * 8), G, 8) for _ in range(4)]
    ci = 0
    for sidx in range(8):
        for (ps_, eb, ec) in [(slice(0, 64), 7 - sidx, sidx + 1), (slice(64, 128), sidx, 8 - sidx)]:
            for (dst, srcp, ex) in [(PWbre, pwre, eb), (PWbim, pwim, eb), (PWcre, pwre, ec), (PWcim, pwim, ec)]:
                o = dst[ps_, :, sidx]
                i_ = srcp[ps_, ex, :]
                eng = kb.v if ci % 2 == 0 else kb.g
                ci += 1
                eng(lambda e, o=o, i_=i_: e.tensor_copy(out=o, in_=i_), ["pw%d" % ex], ["PW"])
    kb.P.barrier()
    out["lamWre"] = tabs["Ture"][:, :, W - 1]
    out["lamWim"] = tabs["Tuim"][:, :, W - 1]
    ctx = dict(out=out, need_out=need_out, ptop0=ptop0, PW=(PWbre, PWbim, PWcre, PWcim), bb=(bbre, bbim), big=big, sm=sm,
               i8=(i8re, i8im), ident=ident, maskF=maskF, maskB=maskB)
    return out, ctx


def s5_pre_batches(kb, ctx):
    out = ctx["out"]; need_out = ctx["need_out"]
    PWbre, PWbim, PWcre, PWcim = ctx["PW"]
    bbre, bbim = ctx["bb"]; big = ctx["big"]; sm = ctx["sm"]
    i8re, i8im = ctx["i8"]; ident = ctx["ident"]; maskF = ctx["maskF"]; maskB = ctx["maskB"]
    GH = 2
    A4 = lambda: r4(kb.f32t(GH * 128), GH, 8, 16)
    BLre, BLim = A4(), A4()
    if need_out:
        BLsre = r4(kb.bf(GH * 128), GH, 8, 16)
        BLsim = r4(kb.bf(GH * 128), GH, 8, 16)
    T = [A4() for _ in range(4)]
    f3 = lambda x: x.rearrange("p g s h -> p g (s h)")
    for gh in range(0, G, GH):
        gs = slice(gh, gh + GH)
        bs = lambda x: x[:, gs, :].unsqueeze(3).broadcast_to([128, GH, 8, 16])
        bh = lambda x: x[:, gs, :].unsqueeze(2).broadcast_to([128, GH, 8, 16])
        kb.cmul(BLre, BLim, bs(PWbre), bs(PWbim), bh(bbre), bh(bbim), [T[0], T[1]], [T[2], T[3]], ["PW", "bbar"], ["BL"], "bT")
        for gi in range(GH):
            g = gh + gi
            for ri, src in enumerate([BLre, BLim]):
                pi = 4 + (gi * 2 + ri) % 2
                pst = kb.ps[pi]
                kps = "ps%d" % pi
                srcg = src[:, gi, :, :].rearrange("p s h -> p (s h)")
                kb.t(lambda e, pst=pst, srcg=srcg: e.transpose(pst[:, 0:128], srcg, ident), ["BL", "ident"], [kps])
                dst = out["B8"][:, g, ri, :]
                kb.a(lambda e, pst=pst, dst=dst: e.activation(out=dst, in_=pst[:, 0:128], func=AF.Copy), [kps], ["B8"])
        if need_out:
            kb.cmul(f3(BLsre), f3(BLsim), f3(BLre), f3(BLim), bc_last(i8re[:, gs], 128), bc_last(i8im[:, gs], 128),
                    [f3(T[0]), f3(T[1])], [f3(T[2]), f3(T[3])], ["BL", "i8"], ["BLs"], "bT")
            yield
            C8re, C8im = BLre, BLim
            kb.cmul(C8re, C8im, bs(PWcre), bs(PWcim), bh(big["cre"]), bh(big["cim"]), [T[0], T[1]], [T[2], T[3]],
                    ["PW", "cre", "cim"], ["BL"], "bT")
            kb.v(lambda e: e.tensor_scalar(out=C8im, in0=C8im, scalar1=-1.0, scalar2=None, op0=ALU.mult), ["BL"], ["BL"])
            d0 = out["C8"][:, 0, gs, :]
            d1 = out["C8"][:, 1, gs, :]
            kb.v(lambda e, d0=d0: e.tensor_copy(out=d0, in_=f3(C8re)), ["BL"], ["C8b"])
            kb.g(lambda e, d1=d1: e.tensor_copy(out=d1, in_=f3(C8im)), ["BL"], ["C8b"])
            yield
            for gi in range(GH):
                g = gh + gi
                pss = []
                for d in range(2):
                    pi = 6 + d
                    pst = kb.ps[pi]
                    kps = "ps%d" % pi
                    rows = slice(64 * d, 64 * d + 64)
                    lre = BLsre[rows, gi, :, :].rearrange("p s h -> p (s h)")
                    lim = BLsim[rows, gi, :, :].rearrange("p s h -> p (s h)")
                    rre = out["C8"][rows, 0, g, :]
                    rim = out["C8"][rows, 1, g, :]
                    kb.t(lambda e, pst=pst, lre=lre, rre=rre: e.matmul(pst[:, 0:128], lhsT=lre, rhs=rre, start=True, stop=False), ["BLs", "C8b"], [kps])
                    kb.t(lambda e, pst=pst, lim=lim, rim=rim: e.matmul(pst[:, 0:128], lhsT=lim, rhs=rim, start=False, stop=True), ["BLs", "C8b"], [kps])
                    pss.append((pst, kps))
                tA = T[0][:, gi, :, :].rearrange("p s h -> p (s h)")
                tB = T[1][:, gi, :, :].rearrange("p s h -> p (s h)")
                kA = "bT_t0"
                kBt = "bT_t2"
                kb.v(lambda e, tA=tA, p0=pss[0][0]: e.tensor_tensor(out=tA, in0=p0[:, 0:128], in1=maskF, op=ALU.mult), [pss[0][1], "maskF"], [kA])
                kb.v(lambda e, tB=tB, p1=pss[1][0]: e.tensor_tensor(out=tB, in0=p1[:, 0:128], in1=maskB, op=ALU.mult), [pss[1][1], "maskB"], [kBt])
                kb.g(lambda e, tA=tA, tB=tB: e.tensor_tensor(out=tA, in0=tA, in1=tB, op=ALU.add), [kA, kBt], [kA])
                dcol = sm["dT"][:, g:g + 1]
                dst = out["Toep"][:, g, :]
                kb.v(lambda e, tA=tA, dcol=dcol, dst=dst: e.scalar_tensor_tensor(out=dst, in0=ident, scalar=dcol, in1=tA, op0=ALU.mult, op1=ALU.add),
                     [kA, "dT", "ident"], ["Toep"])
        yield
    return


def s5_precompute(kb, din, need_out=True):
    out, ctx = s5_pre_init(kb, din, need_out)
    for _ in s5_pre_batches(kb, ctx):
        pass
    kb.P.barrier()
    kb.ptop = ctx["ptop0"]
    return out


def interleave(gens):
    gens = list(gens)
    while gens:
        for g_ in list(gens):
            try:
                next(g_)
            except StopIteration:
                gens.remove(g_)


def _s5_gq(kb, cst, X, J, gb, GB, bufs, tag, sfx, psb):
    NW = J // W
    m1a, m2a, m3a, m4a, qre, qim = bufs
    v4 = lambda x: r4(x[:, 0:GB * J], GB, NW, W)
    v3 = lambda x: r3(x[:, 0:GB * J], GB, J)
    gs = slice(gb, gb + GB)
    tb = lambda nm: cst[nm][:, gs, :].unsqueeze(2).broadcast_to([128, GB, NW, W])
    K = lambda n: n + sfx
    assert GB * J <= 512
    for ri in range(2):
        pi = psb + ri
        pst = kb.ps[pi]
        kps = "ps%d" % pi
        for gi in range(GB):
            g = gb + gi
            lhs = cst["B8"][:, g, ri, :]
            rhs = X[:, g, :]
            kb.t(lambda e, pst=pst, lhs=lhs, rhs=rhs, gi=gi: e.matmul(pst[:, gi * J:(gi + 1) * J], lhsT=lhs, rhs=rhs, start=True, stop=True),
                 ["B8", tag + "X"], [kps])
        dst = v3(m1a if ri == 0 else m2a)
        kd = K("m1") if ri == 0 else K("m2")
        src = r3(pst[:, 0:GB * J], GB, J)
        kb.a(lambda e, src=src, dst=dst: e.activation(out=dst[0:64, :, :], in_=src[0:64, :, :], func=AF.Copy),
             [kps], [kd + "_f%d" % i for i in range(GB)])
        kb.a(lambda e, src=src, dst=dst: e.activation(out=dst[64:128, :, ::-1], in_=src[64:128, :, :], func=AF.Copy),
             [kps], [kd + "_b%d" % i for i in range(GB)])
        yield
    K1 = [K("m1") + "_f%d" % i for i in range(GB)] + [K("m1") + "_b%d" % i for i in range(GB)]
    K2 = [K("m2") + "_f%d" % i for i in range(GB)] + [K("m2") + "_b%d" % i for i in range(GB)]
    kb.tt("vector", v4(m3a), v4(m1a), tb("Tpre"), ALU.mult, K1 + ["Tp"], [K("m3")])
    kb.tt("gpsimd", v4(m4a), v4(m2a), tb("Tpim"), ALU.mult, K2 + ["Tp"], [K("m4")])
    yield
    kb.tt("vector", v4(qre), v4(m3a), v4(m4a), ALU.subtract, [K("m3"), K("m4")], [K("qre")])
    yield
    kb.tt("vector", v4(m3a), v4(m1a), tb("Tpim"), ALU.mult, K1 + ["Tp"], [K("m3")])
    kb.tt("gpsimd", v4(m4a), v4(m2a), tb("Tpre"), ALU.mult, K2 + ["Tp"], [K("m4")])
    yield
    kb.tt("vector", v4(qim), v4(m3a), v4(m4a), ALU.add, [K("m3"), K("m4")], [K("qim")])
    yield
    return


def s5_pass1(kb, cst, X, J, Qend_re, Qend_im, tag, QD=None, GB=2):
    NW = J // W
    assert G % (2 * GB) == 0 and GB * J <= 512
    m0 = kb.mark()
    sets = [[kb.f32(GB * J) for _ in range(6)] for _ in range(2)]
    v4 = lambda x: r4(x, GB, NW, W)

    def stream(si):
        bufs = sets[si]
        sfx = "_s%d" % si
        for gb in range(si * GB, G, 2 * GB):
            gs = slice(gb, gb + GB)
            yield from _s5_gq(kb, cst, X, J, gb, GB, bufs, tag, sfx, 2 * si)
            if QD is not None:
                kb.dma(QD[gb // GB, 0], bufs[4], r=["qre" + sfx], w=["QD%d" % (gb // GB)])
                kb.dma(QD[gb // GB, 1], bufs[5], r=["qim" + sfx], w=["QD%d" % (gb // GB)])
            o1 = Qend_re[:, gs, :]
            o2 = Qend_im[:, gs, :]
            kb.v(lambda e, o1=o1: e.tensor_reduce(out=o1, in_=v4(bufs[4]), axis=mybir.AxisListType.X, op=ALU.add), ["qre" + sfx], [tag + "Qend"])
            yield
            kb.v(lambda e, o2=o2: e.tensor_reduce(out=o2, in_=v4(bufs[5]), axis=mybir.AxisListType.X, op=ALU.add), ["qim" + sfx], [tag + "Qend"])
            yield
    interleave([stream(0), stream(1)])
    kb.P.barrier()
    kb.release(m0)


def s5_carries(kb, cst, Qend_re, Qend_im, init_re, init_im, car_re, car_im, NW, tag):
    m0 = kb.mark()
    ct = [r3(kb.f32(G), G, 1) for _ in range(6)]
    kb.v(lambda e: e.tensor_copy(out=car_re[:, :, 0], in_=init_re), [tag + "init"], [tag + "car"])
    kb.v(lambda e: e.tensor_copy(out=car_im[:, :, 0], in_=init_im), [tag + "init"], [tag + "car"])
    lw_re = cst["lamWre"].unsqueeze(2)
    lw_im = cst["lamWim"].unsqueeze(2)
    for w in range(NW):
        cr, ci = car_re[:, :, w:w + 1], car_im[:, :, w:w + 1]
        nr, ni = car_re[:, :, w + 1:w + 2], car_im[:, :, w + 1:w + 2]
        kb.tt("vector", ct[0], Qend_re[:, :, w:w + 1], cr, ALU.add, [tag + "Qend", tag + "car"], ["ct0"])
        kb.tt("gpsimd", ct[1], Qend_im[:, :, w:w + 1], ci, ALU.add, [tag + "Qend", tag + "car"], ["ct1"])
        kb.cmul(nr, ni, ct[0], ct[1], lw_re, lw_im, [ct[2], ct[3]], [ct[4], ct[5]], ["ct0", "ct1", "Tu"], [tag + "car"], "ctt")
    kb.P.barrier()
    kb.release(m0)


def s5_pass2(kb, cst, X, J, car_re, car_im, SPre, SPim, tag, g0=0, g1=G, QD=None, car_tag=None, pre_barrier=True):
    NW = J // W
    GB = 2
    ctag = tag if car_tag is None else car_tag
    m0 = kb.mark()
    sets = [[kb.f32(GB * J) for _ in range(6)] for _ in range(2)]
    mska = kb.f32(GB * J)
    v4 = lambda x: r4(x, GB, NW, W)
    v3 = lambda x: r3(x, GB, J)
    mk = cst["rmask"].unsqueeze(1).unsqueeze(1).broadcast_to([128, GB, NW, W])
    kb.v(lambda e: e.tensor_copy(out=v4(mska), in_=mk), ["rmask"], ["mska"])

    def stream(si):
        bufs = sets[si]
        m1a, m2a, m3a, m4a, qre, qim = bufs
        sfx = "_s%d" % si
        K = lambda n: n + sfx

        def load(gb):
            kb.dma(qre, QD[gb // GB, 0], w=[K("qre")])
            kb.dma(qim, QD[gb // GB, 1], w=[K("qim")])
        first = g0 + si * GB
        load(first)
        for gb in range(first, g1, 2 * GB):
            gs = slice(gb, gb + GB)
            gsl = slice(gb - g0, gb - g0 + GB)
            tb = lambda nm: cst[nm][:, gs, :].unsqueeze(2).broadcast_to([128, GB, NW, W])
            q4r, q4i = v4(qre), v4(qim)
            cr = car_re[:, gs, 0:NW].unsqueeze(3)
            ci = car_im[:, gs, 0:NW].unsqueeze(3)
            kb.tt("vector", q4r[:, :, :, 0:1], q4r[:, :, :, 0:1], cr, ALU.add, [K("qre"), ctag + "car"], [K("qre")])
            kb.tt("vector", q4i[:, :, :, 0:1], q4i[:, :, :, 0:1], ci, ALU.add, [K("qim"), ctag + "car"], [K("qim")])
            yield
            kb.v(lambda e: e.tensor_tensor_scan(out=m1a, data0=mska, data1=qre, initial=0.0, op0=ALU.mult, op1=ALU.add), ["mska", K("qre")], [K("m1")])
            yield
            kb.v(lambda e: e.tensor_tensor_scan(out=m2a, data0=mska, data1=qim, initial=0.0, op0=ALU.mult, op1=ALU.add), ["mska", K("qim")], [K("m2")])
            yield
            if gb + 2 * GB < g1:
                load(gb + 2 * GB)
            cre4, cim4 = v4(m1a), v4(m2a)
            kb.tt("vector", v4(m3a), cre4, tb("Ture"), ALU.mult, [K("m1"), "Tu"], [K("m3")])
            kb.tt("gpsimd", v4(m4a), cim4, tb("Tuim"), ALU.mult, [K("m2"), "Tu"], [K("m4")])
            yield
            x3, y3 = v3(m3a), v3(m4a)
            kb.tt("vector", SPre[0:64, gsl, 1:J], x3[0:64, :, 0:J - 1], y3[0:64, :, 0:J - 1], ALU.subtract, [K("m3"), K("m4")], [tag + "SP"])
            kb.tt("gpsimd", SPre[64:128, gsl, 0:J - 1], x3[64:128, :, J - 2::-1], y3[64:128, :, J - 2::-1], ALU.subtract, [K("m3"), K("m4")], [tag + "SP"])
            yield
            kb.tt("vector", v4(m3a), cre4, tb("Tuim"), ALU.mult, [K("m1"), "Tu"], [K("m3")])
            kb.tt("gpsimd", v4(m4a), cim4, tb("Ture"), ALU.mult, [K("m2"), "Tu"], [K("m4")])
            yield
            kb.tt("vector", SPim[0:64, gsl, 1:J], x3[0:64, :, 0:J - 1], y3[0:64, :, 0:J - 1], ALU.add, [K("m3"), K("m4")], [tag + "SP"])
            kb.tt("vector", SPim[64:128, gsl, 0:J - 1], x3[64:128, :, J - 2::-1], y3[64:128, :, J - 2::-1], ALU.add, [K("m3"), K("m4")], [tag + "SP"])
            yield
    interleave([stream(0), stream(1)])
    i_re, i_im = car_re[:, g0:g1, 0], car_im[:, g0:g1, 0]
    kb.v(lambda e: e.tensor_copy(out=SPre[0:64, :, 0], in_=i_re[0:64, :]), [ctag + "car"], [tag + "SP"])
    kb.v(lambda e: e.tensor_copy(out=SPim[0:64, :, 0], in_=i_im[0:64, :]), [ctag + "car"], [tag + "SP"])
    kb.v(lambda e: e.tensor_copy(out=SPre[64:128, :, J - 1], in_=i_re[64:128, :]), [ctag + "car"], [tag + "SP"])
    kb.v(lambda e: e.tensor_copy(out=SPim[64:128, :, J - 1], in_=i_im[64:128, :]), [ctag + "car"], [tag + "SP"])
    kb.P.barrier()
    kb.release(m0)


def s5_relayout(kb, selF, uaT, X, J, tag):
    for g in range(G):
        pt, gl = g // 8, g % 8
        pst = kb.ps[g % 8]
        kps = "ps%d" % (g % 8)
        for s in range(8):
            rhs = uaT[:, pt, s::8]
            lhs = selF[:, gl, 16 * (7 - s):16 * (7 - s) + 128]
            kb.t(lambda e, pst=pst, lhs=lhs, rhs=rhs, s=s: e.matmul(pst[:, 0:J], lhsT=lhs, rhs=rhs, start=(s == 0), stop=(s == 7)),
                 ["selF", tag + "uaT"] + ([kps] if s else []), [kps])
        dst = X[:, g, :]
        if g % 2 == 0:
            kb.a(lambda e, pst=pst, dst=dst: e.activation(out=dst, in_=pst[:, 0:J], func=AF.Copy), [kps], [tag + "X"])
        else:
            kb.v(lambda e, pst=pst, dst=dst: e.tensor_copy(out=dst, in_=pst[:, 0:J]), [kps], [tag + "X"])


def s5_output(kb, cst, X, SPre, SPim, J, consume, tag, g0=0, g1=G):
    for g in range(g0, g1):
        pst = kb.ps[g % 8]
        kps = "ps%d" % (g % 8)
        ops = [(cst["Toep"][:, g, :], X[:, g, :]), (cst["C8"][:, 0, g, :], SPre[:, g - g0, :]), (cst["C8"][:, 1, g, :], SPim[:, g - g0, :])]
        for i, (lhs, rhs) in enumerate(ops):
            kb.t(lambda e, pst=pst, lhs=lhs, rhs=rhs, i=i: e.matmul(pst[:, 0:J], lhsT=lhs, rhs=rhs, start=(i == 0), stop=(i == 2)),
                 ["Toep", "C8b", tag + "X", tag + "SP"] + ([kps] if i else []), [kps])
        consume(g, pst, kps)


DN_ALPHA = 4 ** 0.25
DEBUG_F = False
LN_EPS = 1e-5
NTOK = 2048
F32_ARENA = 17200
BF_ARENA = 52000


def MM(kb, pst_ap, lhs, rhs, start, stop, r, w):
    kb.t(lambda e: e.matmul(pst_ap, lhsT=lhs, rhs=rhs, start=start, stop=stop), r, w)


def ACT(kb, out, in_, func, r, w, **kw):
    kb.a(lambda e: e.activation(out=out, in_=in_, func=func, **kw), r, w)


def adaln_loads(kb, D, sfx, wb, csb, csrep, gate_rep):
    cs = r3(kb.f32t(16), 8, 2)
    kb.dma(cs.rearrange("p a b -> p (a b)"), D["cT"], w=["cs" + sfx])
    modbT = kb.f32t(24)
    kb.dma(modbT, D["modbT"], w=["modbT" + sfx])
    gb = gate_rep
    kb.dma(gb, D["modb_gate_rep"], w=["gate_rep" + sfx])
    wv = D["modw"].rearrange("(k p) c -> p k c", p=128)
    for blk in range(6):
        kb.dma(wb[blk], wv[:, :, blk * 512:(blk + 1) * 512], w=["adw%d" % blk + sfx], q="gpsimd")
    return dict(cs=cs, modbT=modbT, gb=gb, wb=wb, csb=csb, csrep=csrep, sfx=sfx)


def adaln_compute(kb, A, modT, gate_rep):
    sfx = A["sfx"]
    cs, modbT, gb, wb, csb, csrep = A["cs"], A["modbT"], A["gb"], A["wb"], A["csb"], A["csrep"]
    ACT(kb, csb, cs, AF.Silu, ["cs" + sfx], ["csb" + sfx])
    kb.v(lambda e: e.tensor_copy(out=csrep, in_=csb[:, :, 0:1].broadcast_to([128, 8, 128])), ["csb" + sfx], ["csrep" + sfx])
    for blk in range(6):
        w = wb[blk]
        kw = "adw%d" % blk + sfx
        for cb in range(4):
            pst = kb.ps[cb]
            kps = "ps%d" % cb
            for k in range(8):
                MM(kb, pst[:, 0:2], w[:, k, cb * 128:(cb + 1) * 128], csb[:, k, :], k == 0, k == 7, [kw, "csb" + sfx], [kps])
            col = blk * 4 + cb
            o = modT[:, col, :]
            b_ = modbT[:, col:col + 1].broadcast_to([128, 2])
            kb.v(lambda e, o=o, pst=pst, b_=b_: e.tensor_tensor(out=o, in0=pst[:, 0:2], in1=b_, op=ALU.add), [kps, "modbT" + sfx], ["modT" + sfx])
        if blk >= 4:
            half = blk - 4
            pst = kb.ps[4 + half]
            kps = "ps%d" % (4 + half)
            for k in range(8):
                MM(kb, pst[:, 0:512], csrep[:, k, :], w[:, k, :], k == 0, k == 7, [kw, "csrep" + sfx], [kps])
            o = gate_rep[:, half * 512:(half + 1) * 512]
            b_ = gb[:, half * 512:(half + 1) * 512]
            kb.v(lambda e, o=o, pst=pst, b_=b_: e.tensor_tensor(out=o, in0=pst[:, 0:512], in1=b_, op=ALU.add), [kps, "gate_rep" + sfx], ["gate_rep" + sfx])
    sc = modT[:, 8:16, :]
    kb.v(lambda e: e.tensor_scalar(out=sc, in0=sc, scalar1=1.0, scalar2=None, op0=ALU.add), ["modT" + sfx], ["modT" + sfx])


def adaln(kb, D):
    modT = r3(kb.f32(48), 24, 2)
    gate_rep = kb.f32(1024)
    m0 = kb.mark()
    pt0 = kb.ptop
    wb = [r3(kb.bf(8 * 512), 8, 512) for _ in range(6)]
    A = adaln_loads(kb, D, "", wb, r3(kb.bf(16), 8, 2), r3(kb.bf(8 * 128), 8, 128), gate_rep)
    adaln_compute(kb, A, modT, gate_rep)
    kb.P.barrier()
    kb.release(m0)
    kb.ptop = pt0
    return modT, gate_rep


def make_hmod(kb, xsrc, modT, ci, hm, xt, ident):
    kb.dma(xt, xsrc.rearrange("(s p) d -> p s d", p=128), w=["xt"])
    for s in range(4):
        for k in range(8):
            pi = (s * 8 + k) % 8
            pst = kb.ps[pi]
            kps = "ps%d" % pi
            src = xt[:, s, k * 128:(k + 1) * 128]
            kb.t(lambda e, pst=pst, src=src: e.transpose(pst[:, 0:128], src, ident), ["xt", "ident"], [kps])
            o = hm[:, k, s * 128:(s + 1) * 128]
            ACT(kb, o, pst[:, 0:128], AF.Identity, [kps, "modT"], ["hm"], scale=modT[:, 8 + k, ci:ci + 1], bias=modT[:, k, ci:ci + 1])


def resid_ln(kb, y_ps_list, xtile, gate_rep, g_rep, b_rep, dst_dram, tmp, tag, sfx=""):
    t1, t2, st, mv, rs = tmp
    K = lambda n: "rl%s_%s" % (sfx, n)
    for h in range(2):
        pst, kps = y_ps_list[h]
        sl = slice(h * 512, (h + 1) * 512)
        kb.v(lambda e, pst=pst, sl=sl: e.scalar_tensor_tensor(out=t2[:, sl], in0=xtile[:, sl], scalar=DN_ALPHA, in1=pst[:, 0:512], op0=ALU.mult, op1=ALU.add),
             [kps, tag + "x" + sfx, K("t10"), K("t11")], [K("t2%d" % h)])
        kb.v(lambda e, sl=sl, h=h: e.bn_stats(out=st[:, h * 6:(h + 1) * 6], in_=t2[:, sl]), [K("t2%d" % h)], [K("st%d" % h)])
    kb.v(lambda e: e.bn_aggr(out=mv, in_=st), [K("st0"), K("st1")], [K("mv")])
    kb.v(lambda e: e.tensor_scalar(out=rs, in0=mv[:, 1:2], scalar1=LN_EPS, scalar2=None, op0=ALU.add), [K("mv")], [K("rs")])
    ACT(kb, rs, rs, AF.Sqrt, [K("rs")], [K("rs")])
    kb.v(lambda e: e.reciprocal(out=rs, in_=rs), [K("rs")], [K("rs")])
    nb = st[:, 0:1]
    kb.v(lambda e: e.scalar_tensor_tensor(out=nb, in0=mv[:, 0:1], scalar=-1.0, in1=rs, op0=ALU.mult, op1=ALU.mult), [K("mv"), K("rs"), K("st0"), K("st1")], [K("nb")])
    ACT(kb, t2, t2, AF.Identity, [K("t20"), K("t21"), K("nb"), K("rs")], [K("t20"), K("t21")], scale=rs, bias=nb)
    kb.v(lambda e: e.tensor_tensor(out=t2, in0=t2, in1=g_rep, op=ALU.mult), [K("t20"), K("t21"), "g_rep"], [K("t20"), K("t21")])
    kb.g(lambda e: e.tensor_tensor(out=t1, in0=t2, in1=b_rep, op=ALU.add), [K("t20"), K("t21"), "b_rep"], [K("t10"), K("t11")])
    kb.dma(dst_dram, t1, r=[K("t10"), K("t11")])


def build12(stage):
    assert stage == "F"
    S1 = stage in (1, "F")
    S2 = stage in (2, "F")
    nc = bass.Bass("TRN2", target_bir_lowering=False)
    D = {}

    def inp(name, shape):
        D[name] = nc.dram_tensor(name, shape, F32, kind="ExternalInput").ap()
    for nm in ["lamre", "lamim", "logdt", "dT"]:
        inp(nm, [128, 32])
    for nm in ["bre", "bim", "cre", "cim"]:
        inp(nm, [128, 512])
    inp("selF", [128, 8 * 240]); inp("ident", [128, 128]); inp("maskF", [128, 128]); inp("maskB", [128, 128])
    inp("x", [NTOK, 1024]); inp("cT", [128, 16]); inp("modw", [1024, 3072]); inp("modbT", [128, 24]); inp("modb_gate_rep", [128, 1024])
    inp("w_in", [1024, 2560])
    if S1:
        inp("ctx", [256, 1024])
    if stage == 1:
        FIN = nc.dram_tensor("FIN", [128, 64], F32, kind="ExternalOutput").ap()
        CFIN = nc.dram_tensor("CFIN", [128, 64], F32, kind="ExternalOutput").ap()
    if stage == 2:
        inp("init", [128, 64])
    if S2:
        inp("g_rep", [128, 1024]); inp("b_rep", [128, 1024])
        inp("glu_w", [512, 512]); inp("glu_bT", [128, 4]); inp("lng_rep", [128, 512]); inp("lnb_rep", [128, 512])
        inp("wsT", [128, 1024]); inp("bs_tm", [128, 8]); inp("w_out", [1024, 1024])
        X1 = nc.dram_tensor("X1", [NTOK, 1024], F32, kind=("ExternalOutput" if (stage == 2 or DEBUG_F) else "Internal")).ap()
        HM = nc.dram_tensor("HM", [4, 128, 8 * 512], BF16, kind="Internal").ap()
    if stage == "F":
        inp("mhalf", [128, 2])
        D1 = {}
        for nm, shp in [("cT", [128, 16]), ("modw", [1024, 3072]), ("modbT", [128, 24]), ("modb_gate_rep", [128, 1024]),
                        ("w_in", [1024, 3072]), ("w_out", [1024, 1024]), ("g_rep", [128, 1024]), ("b_rep", [128, 1024]),
                        ("dwT", [128, 8 * 31]), ("dwbT", [128, 8]), ("clngT", [128, 8]), ("clnbT", [128, 8])]:
            D1[nm] = nc.dram_tensor(nm + "_1", shp, F32, kind="ExternalInput").ap()
        D1["ident"] = D["ident"]
        D1["mhalf"] = D["mhalf"]
        OUT = nc.dram_tensor("OUT", [NTOK, 1024], F32, kind="ExternalOutput").ap()
        HM3 = nc.dram_tensor("HM3", [4, 128, 8 * 512], BF16, kind="Internal").ap()
        CV3 = nc.dram_tensor("CV3", [8, 128, NTOK], F32, kind="Internal").ap()
        DBGI = nc.dram_tensor("DBGI", [128, 64], F32, kind="ExternalOutput").ap() if DEBUG_F else None
        DBG2 = nc.dram_tensor("DBG2", [128, 64 * 3 + 1024 + 1024], F32, kind="ExternalOutput").ap() if DEBUG_F else None
        QD = nc.dram_tensor("QD", [16, 2, 128, 2 * J_OWN], F32, kind="Internal").ap()
        CCI1 = nc.dram_tensor("cci1", [128, 64], F32, kind="Internal").ap()
        CCO1 = nc.dram_tensor("cco1", [256, 64], F32, kind="Internal").ap()
        CCI2 = nc.dram_tensor("cci2", [128, 8192], BF16, kind="Internal").ap()
        CCO2 = nc.dram_tensor("cco2", [256, 8192], BF16, kind="Internal").ap()
    with contextlib.ExitStack() as es:
        kb = KB(nc, es, F32_ARENA, BF_ARENA)
        for nm in ["cc0", "cc1"]:
            kb.sems[nm] = es.enter_context(nc.semaphore(nm))
            kb.P.cnt[nm] = 0
        kb.boot_clear()
        block = es.enter_context(nc.Block())
        ygT = r3(kb.ab[:, BF_ARENA - 4 * NTOK:BF_ARENA], 4, NTOK) if S2 else None
        ident = kb.f32(128)
        kb.dma(ident, D["ident"], w=["ident"])
        modT = r3(kb.f32(48), 24, 2); gate_rep = kb.f32(1024)
        modT1 = r3(kb.f32(48), 24, 2); gate_rep1 = kb.f32(1024)
        mPers = kb.mark()
        ad = []
        for li, DD in enumerate([D, D1]):
            wb = [r3(kb.ab[:, (6 * li + i) * 4096:(6 * li + i + 1) * 4096], 8, 512) for i in range(6)]
            base = 49152 + li * 1040
            ad.append(adaln_loads(kb, DD, "_a%d" % li, wb, r3(kb.ab[:, base:base + 16], 8, 2), r3(kb.ab[:, base + 16:base + 1040], 8, 128), [gate_rep, gate_rep1][li]))
        kb.P.barrier_skip = tuple("gpsimd_d%d" % i for i in range(NDMASEM))
        cst, pctx = s5_pre_init(kb, D, need_out=S2)
        kb.P.barrier_skip = ()
        adaln_compute(kb, ad[0], modT, gate_rep)
        adaln_compute(kb, ad[1], modT1, gate_rep1)
        kb.P.barrier()
        selF = r3(kb.bf(8 * 240), 8, 240)
        kb.dma(selF.rearrange("p a b -> p (a b)"), D["selF"], w=["selF"], q="gpsimd")
        X = r3(kb.bf(G * J_OWN), G, J_OWN)
        init = kb.f32(64)
        fin = kb.f32(64)
        NWO = J_OWN // W
        Qre = r3(kb.f32(G * NWO), G, NWO); Qim = r3(kb.f32(G * NWO), G, NWO)
        car_re = r3(kb.f32(G * (NWO + 1)), G, NWO + 1); car_im = r3(kb.f32(G * (NWO + 1)), G, NWO + 1)
        QCre = r3(kb.f32(G), G, 1); QCim = r3(kb.f32(G), G, 1)
        cC_re = r3(kb.f32(G * 2), G, 2); cC_im = r3(kb.f32(G * 2), G, 2)
        if S1:
            Xc = r3(kb.ab[:, BF_ARENA - G * 64:BF_ARENA], G, 64)
            cfin = kb.f32(64)
        mP2 = kb.mark()
        hm = r3(kb.bf(8 * 512), 8, 512)
        wua = r3(kb.bf(8 * 512), 8, 512)
        kb.dma(wua, D["w_in"].rearrange("(k p) c -> p k c", p=128)[:, :, 0:512], w=["wua"], q="gpsimd")
        NTC = NTOK + 256
        uaT = r3(kb.bf(4 * NTC), 4, NTC)
        xt = r3(kb.f32(1 * 1024), 1, 1024)

        def ua_proj(dst, ncol0, n=512):
            for cb in range(4):
                pst = kb.ps[cb]
                kps = "ps%d" % cb
                for k in range(8):
                    MM(kb, pst[:, 0:n], wua[:, k, cb * 128:(cb + 1) * 128], hm[:, k, 0:n], k == 0, k == 7, ["wua", "hm"], [kps])
                o = dst[:, cb, ncol0:ncol0 + n]
                ACT(kb, o, pst[:, 0:n], AF.Copy, [kps], ["LuaT"])

        def hmod_sub(src_rows, s, ci, buf):
            kx = "xt%d" % buf
            kb.dma(xt[:, buf, :], src_rows, w=[kx])
            for k in range(8):
                pst = kb.ps[k % 4]
                kps = "ps%d" % (k % 4)
                src = xt[:, buf, k * 128:(k + 1) * 128]
                kb.t(lambda e, pst=pst, src=src: e.transpose(pst[:, 0:128], src, ident), [kx, "ident"], [kps])
                o = hm[:, k, s * 128:(s + 1) * 128]
                ACT(kb, o, pst[:, 0:128], AF.Identity, [kps, "modT"], ["hm"], scale=modT[:, 8 + k, ci:ci + 1], bias=modT[:, k, ci:ci + 1])

        def p2_stream():
            for tt_ in range(4):
                for s in range(4):
                    r0 = tt_ * 512 + s * 128
                    hmod_sub(D["x"][r0:r0 + 128, :], s, 0, 0)
                    yield
                if S2:
                    kb.dma(HM[tt_], hm.rearrange("p a b -> p (a b)"), r=["hm"], w=["HM%d" % tt_])
                ua_proj(uaT, tt_ * 512)
                yield
            for s in range(2):
                hmod_sub(D["ctx"][s * 128:(s + 1) * 128, :], s, 1, 0)
            yield
            ua_proj(uaT, NTOK, 256)
            yield
            JT = NTC // 8
            for g in range(G):
                pt, gl = g // 8, g % 8
                pst = kb.ps[g % 4]
                kps = "ps%d" % (g % 4)
                for s in range(8):
                    MM(kb, pst[:, 0:JT], selF[:, gl, 16 * (7 - s):16 * (7 - s) + 128], uaT[:, pt, s::8], s == 0, s == 7, ["selF", "LuaT"], [kps])
                ACT(kb, X[:, g, :], pst[:, 0:J_OWN], AF.Copy, [kps], ["LX"])
                ACT(kb, Xc[:, g, 0:32], pst[:, J_OWN:JT], AF.Copy, [kps], ["CX"])
                yield
        interleave([s5_pre_batches(kb, pctx), p2_stream()])
        kb.P.barrier()
        kb.ptop = pctx["ptop0"]
        if S1:
            kb.P.barrier()
            kb.release(mP2)
            kb.v(lambda e: e.memset(init, 0.0), [], ["Cinit"])
            s5_pass1(kb, cst, Xc[:, :, 0:32], 32, QCre, QCim, "C", GB=8)
            s5_carries(kb, cst, QCre, QCim, init[:, 0:32], init[:, 32:64], cC_re, cC_im, 1, "C")
            kb.v(lambda e: e.tensor_copy(out=cfin[:, 0:32], in_=cC_re[:, :, 1]), [], ["Cfin"])
            kb.v(lambda e: e.tensor_copy(out=cfin[:, 32:64], in_=cC_im[:, :, 1]), [], ["Cfin"])
            kb.v(lambda e: e.tensor_copy(out=init, in_=cfin), ["Cfin"], ["Linit"])
            s5_pass1(kb, cst, X, J_OWN, Qre, Qim, "L", QD=QD)
            s5_carries(kb, cst, Qre, Qim, init[:, 0:32], init[:, 32:64], car_re, car_im, NWO, "L")
            kb.v(lambda e: e.tensor_copy(out=fin[:, 0:32], in_=car_re[:, :, NWO]), [], ["Lfin"])
            kb.v(lambda e: e.tensor_copy(out=fin[:, 32:64], in_=car_im[:, :, NWO]), [], ["Lfin"])
            if stage == 1:
                kb.dma(FIN, fin, r=["Lfin"])
                kb.P.barrier()
                print("stage1 nops", kb.P.nops, "arena", kb.pf, kb.pb)
                kb.P.emit(kb.sems, block)
                return nc
            if DEBUG_F:
                mdd = kb.mark()
                dd = kb.f32(64 * 3 + 2048)
                kb.v(lambda e: e.tensor_copy(out=dd[:, 0:64], in_=cfin), [], ["dd"])
                kb.v(lambda e: e.tensor_copy(out=dd[:, 64:128], in_=fin), [], ["dd"])
                kb.v(lambda e: e.tensor_copy(out=dd[:, 128:160], in_=cst["lamWre"]), [], ["dd"])
                kb.v(lambda e: e.tensor_copy(out=dd[:, 160:192], in_=cst["lamWim"]), [], ["dd"])
                kb.v(lambda e: e.tensor_copy(out=r3(dd[:, 192:192 + 1024], 32, 32), in_=Xc[:, :, 0:32]), [], ["dd"])
                kb.v(lambda e: e.tensor_copy(out=r3(dd[:, 1216:1216 + 1024], 32, 32), in_=X[:, :, 0:32]), [], ["dd"])
                kb.dma(DBG2, dd, r=["dd"])
                kb.P.barrier()
                kb.release(mdd)
            mh = kb.f32(2)
            kb.dma(mh, D["mhalf"], w=["mh"])
            kb.dma(CCI1, fin, r=["Lfin"], w=["cci1"])
            kb.P.barrier()
            kb.cc(lambda e: e.collective_compute("AllGather", ALU.bypass, replica_groups=[[0, 1], [2, 3], [4, 5], [6, 7]], ins=[CCI1], outs=[CCO1]),
                  "cc0", [], ["cco1"])
            kb.P.barrier()
            g01 = r3(kb.f32(128), 2, 64)
            kb.dma(g01, CCO1.rearrange("(r p) c -> p r c", p=128), r=["cco1"], w=["g01"])
            dtmp = kb.f32(64)
            kb.v(lambda e: e.tensor_tensor(out=dtmp[0:64, :], in0=g01[0:64, 0, :], in1=cfin[0:64, :], op=ALU.subtract), ["g01", "Cfin"], ["dtmp"])
            kb.v(lambda e: e.scalar_tensor_tensor(out=init[0:64, :], in0=dtmp[0:64, :], scalar=mh[0:64, 0:1], in1=cfin[0:64, :], op0=ALU.mult, op1=ALU.add),
                 ["dtmp", "mh", "Cfin"], ["Linit"])
            kb.v(lambda e: e.tensor_tensor(out=dtmp[64:128, :], in0=cfin[64:128, :], in1=g01[64:128, 1, :], op=ALU.subtract), ["g01", "Cfin"], ["dtmp"])
            kb.v(lambda e: e.scalar_tensor_tensor(out=init[64:128, :], in0=dtmp[64:128, :], scalar=mh[64:128, 0:1], in1=g01[64:128, 1, :], op0=ALU.mult, op1=ALU.add),
                 ["dtmp", "mh", "g01"], ["Linit"])
            if DEBUG_F:
                kb.dma(DBGI, init, r=["Linit"])
            kb.P.barrier()
            s5_carries(kb, cst, Qre, Qim, init[:, 0:32], init[:, 32:64], car_re, car_im, NWO, "L")
        kb.P.barrier()
        if stage == 2:
            kb.release(mP2)
            kb.dma(init, D["init"], w=["Linit"])
        b8flat = cst["B8"].rearrange("p g r m -> p (g r m)")
        SPsets = [(r3(kb.bf(16 * J_OWN), 16, J_OWN), r3(kb.bf(16 * J_OWN), 16, J_OWN)),
                  (r3(b8flat[:, 0:16 * J_OWN], 16, J_OWN), r3(b8flat[:, 16 * J_OWN:32 * J_OWN], 16, J_OWN))]
        Yg = r3(kb.bf(16 * J_OWN), 16, J_OWN)

        def s5_half_out(half):
            g0, g1 = 16 * half, 16 * half + 16
            SPre, SPim = SPsets[half]
            tg = "L%d" % half

            def consume(g, pst, kps):
                o = Yg[:, g - g0, :]
                ACT(kb, o, pst[:, 0:J_OWN], AF.Gelu, [kps], ["Yg"])
            s5_output(kb, cst, X, SPre, SPim, J_OWN, consume, tg, g0, g1)
            for ptl in range(2):
                pt = 2 * half + ptl
                for t in range(8):
                    pi = t % 8
                    pst = kb.ps[pi]
                    kps = "ps%d" % pi
                    for gl in range(8):
                        lhs = selF[:, t, 16 * (7 - gl):16 * (7 - gl) + 128]
                        rhs = Yg[:, ptl * 8 + gl, :]
                        MM(kb, pst[:, 0:J_OWN], lhs, rhs, gl == 0, gl == 7, ["selF", "Yg"], [kps])
                    o = ygT[:, pt, t::8]
                    ACT(kb, o, pst[:, 0:J_OWN], AF.Copy, [kps], ["ygT"])
        s5_pass2(kb, cst, X, J_OWN, car_re, car_im, SPsets[0][0], SPsets[0][1], "L0", 0, 16, QD=QD, car_tag="L")
        s5_half_out(0)
        s5_pass2(kb, cst, X, J_OWN, car_re, car_im, SPsets[1][0], SPsets[1][1], "L1", 16, 32, QD=QD, car_tag="L", pre_barrier=False)
        s5_half_out(1)
        kb.P.barrier()
        kb.pb = 0
        kb.pf = mPers[0]
        kb.ptop = kb.capf
        wv = D["w_in"].rearrange("(k p) c -> p k c", p=128)
        wout = r3(kb.bf(8 * 1024), 8, 1024)
        gluw = r3(kb.bf(4 * 512), 4, 512)
        wsT = r3(kb.bf(8 * 128), 8, 128)
        wsec = [r3(kb.bf(8 * 512), 8, 512) for _ in range(4)]
        hms = [r3(kb.bf(8 * 512), 8, 512)]
        act = r3(kb.bf(8 * 512), 8, 512)
        vtm4 = r3(kb.bf(4 * 512), 4, 512)
        utm4 = r3(kb.bf(4 * 512), 4, 512)
        ztm4 = r3(kb.bf(4 * 512), 4, 512)
        g_rep = kb.f32(1024); kb.dma(g_rep, D["g_rep"], w=["g_rep"])
        b_rep = kb.f32(1024); kb.dma(b_rep, D["b_rep"], w=["b_rep"])
        lng = kb.f32(512); kb.dma(lng, D["lng_rep"], w=["lng"])
        lnb = kb.f32(512); kb.dma(lnb, D["lnb_rep"], w=["lnb"])
        glub = kb.f32(4); kb.dma(glub, D["glu_bT"], w=["glub"])
        bstm = kb.f32(8); kb.dma(bstm, D["bs_tm"], w=["bstm"])
        sza = r3(kb.f32(4 * 512), 4, 512)
        sigs = [kb.f32(512) for _ in range(2)]
        gv4 = r3(kb.f32(4 * 512), 4, 512)
        rsets = [(kb.f32(1024), kb.f32(1024), kb.f32(12), kb.f32(2), kb.f32(1)) for _ in range(2)]
        xtiles = [kb.f32(1024) for _ in range(2)]
        sm2 = [(kb.f32(6), kb.f32(2), kb.f32(1)) for _ in range(4)]

        def load_w(i, c0):
            kb.dma(wsec[i], wv[:, :, c0:c0 + 512], w=["wsec%d" % i], q="gpsimd")
            return wsec[i], "wsec%d" % i
        wvv, kwv = load_w(2, 1536)
        wza, kza = load_w(0, 512)
        kb.dma(gluw, D["glu_w"].rearrange("(k p) c -> p k c", p=128), w=["gluw"], q="gpsimd")
        wu, kwu = load_w(1, 1024)
        kb.dma(wsT.rearrange("p a b -> p (a b)"), D["wsT"], w=["wsT"], q="gpsimd")
        wz, kwz = load_w(3, 2048)
        kb.dma(wout, D["w_out"].rearrange("(k p) c -> p k c", p=128), w=["wout"], q="gpsimd")
        kb.v(lambda e: e.tensor_tensor(out=wout, in0=wout, in1=gate_rep.unsqueeze(1).broadcast_to([128, 8, 1024]), op=ALU.mult), ["wout"], ["wout"])
        for tt_ in range(4):
            tsl = slice(tt_ * 512, (tt_ + 1) * 512)
            hm = hms[0]
            khm = "hm0"
            assert kb.pb <= BF_ARENA - 4 * NTOK, kb.pb
            if tt_ == 0:
                kb.dma(hm.rearrange("p a b -> p (a b)"), HM[0], w=[khm])
            EV = [0, 2, 4, 6]
            for s in range(4):
                ssl = slice(s * 128, (s + 1) * 128)
                pst = kb.ps[EV[s]]; kps = "ps%d" % EV[s]
                for k in range(8):
                    MM(kb, pst[:, 0:512], hm[:, k, ssl], wvv[:, k, :], k == 0, k == 7, [kwv, khm], [kps])
                ACT(kb, gv4[:, s, :], pst[:, 0:512], AF.Gelu, [kps], ["gv%d" % s])
            for s in range(4):
                st2, mv2, rs2 = sm2[s]
                gv = gv4[:, s, :]
                kg = "gv%d" % s
                ks = "sm%d" % s
                kb.v(lambda e, st2=st2, gv=gv: e.bn_stats(out=st2, in_=gv), [kg], [ks])
                kb.v(lambda e, st2=st2, mv2=mv2: e.bn_aggr(out=mv2, in_=st2), [ks], [ks])
                kb.v(lambda e, mv2=mv2, rs2=rs2: e.tensor_scalar(out=rs2, in0=mv2[:, 1:2], scalar1=LN_EPS, scalar2=None, op0=ALU.add), [ks], [ks])
                ACT(kb, rs2, rs2, AF.Sqrt, [ks], [ks])
                kb.v(lambda e, rs2=rs2: e.reciprocal(out=rs2, in_=rs2), [ks], [ks])
                kb.v(lambda e, gv=gv, mv2=mv2, rs2=rs2: e.tensor_scalar(out=gv, in0=gv, scalar1=mv2[:, 0:1], scalar2=rs2, op0=ALU.subtract, op1=ALU.mult), [kg, ks], [kg])
                kb.g(lambda e, gv=gv: e.tensor_tensor(out=gv, in0=gv, in1=lng, op=ALU.mult), [kg, "lng"], [kg])
                o = vtm4[:, s, :]
                kb.g(lambda e, gv=gv, o=o: e.tensor_tensor(out=o, in0=gv, in1=lnb, op=ALU.add), [kg, "lnb"], ["vtm%d" % s])
            for cb in range(4):
                pst = kb.ps[EV[cb]]; kps = "ps%d" % EV[cb]
                for k in range(8):
                    MM(kb, pst[:, 0:512], wza[:, k, cb * 128:(cb + 1) * 128], hm[:, k, :], k == 0, k == 7, [kza, khm], [kps])
                ACT(kb, sza[:, cb, :], pst[:, 0:512], AF.Silu, [kps], ["sza%d" % cb])
            for cb in range(4):
                pst = kb.ps[EV[cb]]; kps = "ps%d" % EV[cb]
                sig = sigs[cb % 2]; ksig = "sig%d" % (cb % 2)
                for k in range(4):
                    MM(kb, pst[:, 0:512], gluw[:, k, cb * 128:(cb + 1) * 128], ygT[:, k, tsl], k == 0, k == 3, ["gluw", "ygT"], [kps])
                ACT(kb, sig, pst[:, 0:512], AF.Sigmoid, [kps, "glub"], [ksig], bias=glub[:, cb:cb + 1])
                yv = ygT[:, cb, tsl]
                kb.v(lambda e, yv=yv, sig=sig: e.tensor_tensor(out=sig, in0=sig, in1=yv, op=ALU.mult), [ksig, "ygT"], [ksig])
                o = act[:, cb, :]
                zz = sza[:, cb, :]
                kb.g(lambda e, o=o, zz=zz, sig=sig: e.tensor_tensor(out=o, in0=sig, in1=zz, op=ALU.mult), [ksig, "sza%d" % cb], ["actA%d" % cb])
            if tt_ == 3 and stage == "F":
                wa_pref = r3(kb.ab[:, BF_ARENA - 8 * 1024:BF_ARENA], 8, 1024)
                kb.dma(wa_pref, D1["w_in"].rearrange("(k p) c -> p k c", p=128)[:, :, 0:1024], w=["ygT", "wa"], q="gpsimd")
            for s in range(4):
                ssl = slice(s * 128, (s + 1) * 128)
                pst = kb.ps[2 * s + 1]; kps = "ps%d" % (2 * s + 1)
                for k in range(8):
                    MM(kb, pst[:, 0:512], hm[:, k, ssl], wu[:, k, :], k == 0, k == 7, [kwu, khm], [kps])
                ACT(kb, utm4[:, s, :], pst[:, 0:512], AF.Gelu, [kps], ["utm%d" % s])
            for s in range(4):
                ssl = slice(s * 128, (s + 1) * 128)
                pst = kb.ps[2 * s]; kps = "ps%d" % (2 * s)
                for h in range(8):
                    MM(kb, pst[:, 64 * h:64 * h + 64], wsT[:, h, :], vtm4[:, s, 64 * h:64 * h + 64], True, True, ["wsT", "vtm%d" % s], [kps])
                sgo = gv4[:, s, :]
                kg = "gv%d" % s
                kb.v(lambda e, pst=pst, sgo=sgo: e.tensor_tensor(out=r3(sgo, 8, 64), in0=r3(pst[:, 0:512], 8, 64), in1=bc_last(bstm, 64), op=ALU.add),
                     [kps, "bstm", "vtm%d" % s], [kg])
                us = utm4[:, s, :]
                kb.v(lambda e, sgo=sgo, us=us: e.tensor_tensor(out=sgo, in0=sgo, in1=us, op=ALU.mult), [kg, "utm%d" % s], [kg])
                pz = kb.ps[2 * s + 1]; kpz = "ps%d" % (2 * s + 1)
                for k in range(8):
                    MM(kb, pz[:, 0:512], hm[:, k, ssl], wz[:, k, :], k == 0, k == 7, [kwz, khm], [kpz])
                zt = ztm4[:, s, :]
                ACT(kb, zt, pz[:, 0:512], AF.Silu, [kpz], ["ztm%d" % s])
                kb.g(lambda e, sgo=sgo, zt=zt: e.tensor_tensor(out=sgo, in0=sgo, in1=zt, op=ALU.mult), [kg, "ztm%d" % s], [kg])
            if tt_ < 3:
                kb.dma(hm.rearrange("p a b -> p (a b)"), HM[tt_ + 1], w=[khm])
            for s in range(4):
                ssl = slice(s * 128, (s + 1) * 128)
                pst = kb.ps[2 * s]; kps = "ps%d" % (2 * s)
                for cb in range(4):
                    src = gv4[:, s, cb * 128:(cb + 1) * 128]
                    kb.t(lambda e, pst=pst, src=src, cb=cb: e.transpose(pst[:, cb * 128:(cb + 1) * 128], src, ident), ["gv%d" % s, "ident"], [kps])
                o = act[:, 4:8, ssl]
                ACT(kb, o, r3(pst[:, 0:512], 4, 128), AF.Copy, [kps], ["actB%d" % s])
            actk = ["actA%d" % i for i in range(4)] + ["actB%d" % i for i in range(4)]
            kb.dma(xtiles[0], D["x"][tt_ * 512:tt_ * 512 + 128, :], w=["Ox0"])
            for s in range(4):
                ssl = slice(s * 128, (s + 1) * 128)
                row0 = tt_ * 512 + s * 128
                bi = s % 2
                xtile = xtiles[bi]
                if s < 3:
                    kb.dma(xtiles[1 - bi], D["x"][row0 + 128:row0 + 256, :], w=["Ox%d" % (1 - bi)])
                ys = []
                for h in range(2):
                    pst = kb.ps[(2 * s + 1 + 2 * h) % 8]; kps = "ps%d" % ((2 * s + 1 + 2 * h) % 8)
                    for cb in range(8):
                        MM(kb, pst[:, 0:512], act[:, cb, ssl], wout[:, cb, h * 512:(h + 1) * 512], cb == 0, cb == 7, actk + ["wout"], [kps])
                    ys.append((pst, kps))
                resid_ln(kb, ys, xtile, gate_rep, g_rep, b_rep, X1[row0:row0 + 128, :], rsets[bi], "O", sfx=str(bi))
        kb.P.barrier()
        if stage == "F":
            kb.pf = mPers[0]
            kb.pb = 0
            kb.ptop = kb.capf
            layer1(kb, D1, X1, OUT, HM3, CV3, (CCI2, CCO2), ident, modT1, gate_rep1, wa_pref)
        print("stage", stage, "nops", kb.P.nops, "arena", kb.pf, kb.pb)
        kb.P.emit(kb.sems, block)
    return nc


def layer1(kb, D, x1src, OUT, HM, CV, ccb, ident, modT, gate_rep, wa_pref=None):
    CCI2, CCO2 = ccb
    mh = kb.f32(2)
    kb.dma(mh, D["mhalf"], w=["mh"])
    Hrow = r4(kb.bf(4 * 32 * 94), 4, 32, 94)
    Hcol = r4(kb.bf(4 * 64 * 64), 4, 64, 64)
    kb.v(lambda e: e.memset(Hrow, 0.0), [], ["Hrow"])
    mB = kb.mark()
    wv = D["w_in"].rearrange("(k p) c -> p k c", p=128)
    wg = r3(kb.bf(8 * 1024), 8, 1024)
    kb.dma(wg, wv[:, :, 1024:2048], w=["wg"], q="gpsimd")
    if wa_pref is not None:
        wa = wa_pref
    else:
        wa = r3(kb.bf(8 * 1024), 8, 1024)
        kb.dma(wa, wv[:, :, 0:1024], w=["wa"], q="gpsimd")
    hm = r3(kb.bf(8 * 512), 8, 512)
    xt = r3(kb.f32(4 * 1024), 4, 1024)
    sig = kb.f32(512)
    for tw in range(4):
        make_hmod(kb, x1src[tw * 512:(tw + 1) * 512, :], modT, 0, hm, xt, ident)
        kb.dma(HM[tw], hm.rearrange("p a b -> p (a b)"), r=["hm"], w=["HMw%d" % tw])
        for cb in range(8):
            pa = kb.ps[(cb % 4) * 2]; ka = "ps%d" % ((cb % 4) * 2)
            pg = kb.ps[(cb % 4) * 2 + 1]; kg = "ps%d" % ((cb % 4) * 2 + 1)
            for k in range(8):
                MM(kb, pa[:, 0:512], wa[:, k, cb * 128:(cb + 1) * 128], hm[:, k, :], k == 0, k == 7, ["wa", "hm"] + ([ka] if k else []), [ka])
            for k in range(8):
                MM(kb, pg[:, 0:512], wg[:, k, cb * 128:(cb + 1) * 128], hm[:, k, :], k == 0, k == 7, ["wg", "hm"] + ([kg] if k else []), [kg])
            ACT(kb, sig, pg[:, 0:512], AF.Sigmoid, [kg], ["sig"])
            if cb < 4:
                o = Hrow[:, cb, tw * 8:(tw + 1) * 8, 15:79]
                kk = "Hrow"
            else:
                o = Hcol[:, cb - 4, 16 + tw * 8:16 + (tw + 1) * 8, :]
                kk = "Hcol"
            kb.v(lambda e, o=o, pa=pa: e.tensor_tensor(out=o, in0=r3(pa[:, 0:512], 8, 64), in1=r3(sig, 8, 64), op=ALU.mult), [ka, "sig"], [kk])
    ci = CCI2.rearrange("p (s c r) -> p s c r", s=2, c=4)
    kb.dma(ci[:, 0], Hcol[:, :, 16:32, :].rearrange("p c r w -> p c (r w)"), r=["Hcol"], w=["cci2"])
    kb.dma(ci[:, 1], Hcol[:, :, 32:48, :].rearrange("p c r w -> p c (r w)"), r=["Hcol"], w=["cci2"])
    kb.P.barrier()
    kb.release(mB)
    kb.cc(lambda e: e.collective_compute("AllGather", ALU.bypass, replica_groups=[[0, 1], [2, 3], [4, 5], [6, 7]], ins=[CCI2], outs=[CCO2]),
          "cc1", [], ["cco2"])

    def halo_in():
        co = CCO2.rearrange("(r p) (s c x) -> r p s c x", p=128, s=2, c=4)
        kb.dma(Hcol[:, :, 0:16, :].rearrange("p c r w -> p c (r w)"), co[0, :, 1], r=["cco2"], w=["HcolT"])
        kb.dma(Hcol[:, :, 48:64, :].rearrange("p c r w -> p c (r w)"), co[1, :, 0], r=["cco2"], w=["HcolB"])
        top = Hcol[:, :, 0:16, :]
        bot = Hcol[:, :, 48:64, :]
        kb.v(lambda e: e.tensor_scalar(out=top, in0=top, scalar1=mh[:, 0:1], scalar2=None, op0=ALU.mult), ["HcolT", "mh"], ["HcolT"])
        kb.v(lambda e: e.tensor_scalar(out=bot, in0=bot, scalar1=mh[:, 1:2], scalar2=None, op0=ALU.mult), ["HcolB", "mh"], ["HcolB"])
    layer1_tail(kb, D, x1src, 0, OUT, HM, CV, Hrow, Hcol, ident, gate_rep, halo_in)


def layer1_tail(kb, D, xsrc, xoff, OUT, HM, CV, Hrow, Hcol, ident, gate_rep, halo_in=None):
    if True:
        mC = kb.mark()
        identb = kb.bf(128)
        kb.v(lambda e: e.tensor_copy(out=identb, in_=ident), ["ident"], ["identb"])
        dwT = r3(kb.f32(8 * 31), 8, 31)
        kb.dma(dwT.rearrange("p a b -> p (a b)"), D["dwT"], w=["dwT"])
        dwb = kb.f32(8); kb.dma(dwb, D["dwbT"], w=["dwb"])
        dgs = [r3(kb.bf(31 * 128), 31, 128) for _ in range(2)]
        cvo = [kb.f32(512) for _ in range(2)]
        ncv = 0
        for cb in range(8):
            if cb == 4 and halo_in is not None:
                halo_in()
            wk = dwT[:, cb, :].unsqueeze(2).broadcast_to([128, 31, 128])
            ib = identb.unsqueeze(1).broadcast_to([128, 31, 128])
            dg = dgs[cb % 2]
            kb.g(lambda e, wk=wk, dg=dg: e.tensor_tensor(out=dg, in0=ib, in1=wk, op=ALU.mult), ["identb", "dwT"], ["dg%d" % (cb % 2)])
            dgk = ["dg%d" % (cb % 2)]
            for t4 in range(4):
                pst = kb.ps[t4]; kps = "ps%d" % t4
                for k in range(31):
                    if cb < 4:
                        rhs = Hrow[:, cb, t4 * 8:(t4 + 1) * 8, k:k + 64]
                    else:
                        r0 = t4 * 8 + k + 1
                        rhs = Hcol[:, cb - 4, r0:r0 + 8, :]
                    MM(kb, pst[:, 0:512], dg[:, k, :], rhs, k == 0, k == 30, dgk + (["Hrow"] if cb < 4 else ["Hcol", "HcolT", "HcolB"]), [kps])
                c = cvo[ncv % 2]; kc = "cvo%d" % (ncv % 2); ncv += 1
                ACT(kb, c, pst[:, 0:512], AF.Identity, [kps, "dwb"], [kc], bias=dwb[:, cb:cb + 1])
                kb.dma(CV[cb][:, t4 * 512:(t4 + 1) * 512], c, r=[kc], w=["CV"])
        kb.P.barrier()
        kb.release(mC)
        kb.pb = 0
        wout = r3(kb.bf(8 * 1024), 8, 1024)
        kb.dma(wout, D["w_out"].rearrange("(k p) c -> p k c", p=128), w=["wout"], q="gpsimd")
        kb.v(lambda e: e.tensor_tensor(out=wout, in0=wout, in1=gate_rep.unsqueeze(1).broadcast_to([128, 8, 1024]), op=ALU.mult), ["wout"], ["wout"])
        wz = [r3(kb.bf(8 * 512), 8, 512) for _ in range(2)]
        hm = r3(kb.bf(8 * 512), 8, 512)
        act = r3(kb.bf(8 * 512), 8, 512)
        ones = kb.f32(128)
        kb.v(lambda e: e.memset(ones, 1.0), [], ["ones"])
        g_rep = kb.f32(1024); kb.dma(g_rep, D["g_rep"], w=["g_rep"])
        b_rep = kb.f32(1024); kb.dma(b_rep, D["b_rep"], w=["b_rep"])
        clng = kb.f32(8); kb.dma(clng, D["clngT"], w=["clng"])
        clnb = kb.f32(8); kb.dma(clnb, D["clnbT"], w=["clnb"])
        cvt = r3(kb.f32(8 * 512), 8, 512)
        mean = kb.f32(512); rstd = kb.f32(512)
        sqs = [kb.f32(512) for _ in range(2)]
        nrms = [kb.f32(512) for _ in range(2)]
        szs = [kb.f32(512) for _ in range(2)]
        f32b = lambda n: kb.bf(2 * n).bitcast(F32)
        rsets = [(kb.f32(1024), kb.f32(1024), kb.f32(12), kb.f32(2), kb.f32(1)),
                 (f32b(1024), f32b(1024), kb.f32(12), kb.f32(2), kb.f32(1))]
        xtiles = [kb.f32(1024), f32b(1024)]
        wzv = D["w_in"].rearrange("(k p) c -> p k c", p=128)
        for zh in range(2):
            kb.dma(wz[zh], wzv[:, :, 2048 + zh * 512:2048 + (zh + 1) * 512], w=["wz%d" % zh], q="gpsimd")
        def c2_loads(t4):
            kb.dma(hm.rearrange("p a b -> p (a b)"), HM[t4], w=["hm"])
            for cb in range(8):
                kb.dma(cvt[:, cb, :], CV[cb][:, t4 * 512:(t4 + 1) * 512], w=["cvt%d" % cb])
        p1 = kb.ps[6]; p2 = kb.ps[7]

        def c2_stats_mm(cbs):
            for cb in cbs:
                sq = sqs[cb % 2]; ksq = "sq%d" % (cb % 2)
                MM(kb, p1[:, 0:512], ones, cvt[:, cb, :], cb == 0, cb == 7, ["ones", "cvt%d" % cb], ["ps6"])
                ACT(kb, sq, cvt[:, cb, :], AF.Square, ["cvt%d" % cb], [ksq])
                MM(kb, p2[:, 0:512], ones, sq, cb == 0, cb == 7, ["ones", ksq], ["ps7"])
        c2_loads(0)
        c2_stats_mm(range(8))
        for t4 in range(4):
            sq = sqs[0]
            kb.v(lambda e: e.tensor_scalar(out=mean, in0=p1[:, 0:512], scalar1=1.0 / 1024, scalar2=None, op0=ALU.mult), ["ps6"], ["mean"])
            kb.v(lambda e, sq=sq: e.tensor_tensor(out=sq, in0=mean, in1=mean, op=ALU.mult), ["mean"], ["sq0"])
            kb.v(lambda e, sq=sq: e.scalar_tensor_tensor(out=rstd, in0=p2[:, 0:512], scalar=1.0 / 1024, in1=sq, op0=ALU.mult, op1=ALU.subtract), ["ps7", "sq0"], ["rstd"])
            kb.v(lambda e: e.tensor_scalar(out=rstd, in0=rstd, scalar1=LN_EPS, scalar2=None, op0=ALU.add), ["rstd"], ["rstd"])
            ACT(kb, rstd, rstd, AF.Sqrt, ["rstd"], ["rstd"])
            kb.v(lambda e: e.reciprocal(out=rstd, in_=rstd), ["rstd"], ["rstd"])
            for cb in range(8):
                w = wz[cb // 4]; kw = "wz%d" % (cb // 4)
                pst = kb.ps[cb % 4]; kps = "ps%d" % (cb % 4)
                nrm = nrms[cb % 2]; kn = "nrm%d" % (cb % 2)
                sz = szs[cb % 2]; kz = "sz%d" % (cb % 2)
                for k in range(8):
                    MM(kb, pst[:, 0:512], w[:, k, (cb % 4) * 128:(cb % 4 + 1) * 128], hm[:, k, :], k == 0, k == 7, [kw, "hm"], [kps])
                ACT(kb, sz, pst[:, 0:512], AF.Silu, [kps], [kz])
                kb.v(lambda e, cb=cb, nrm=nrm: e.tensor_tensor(out=nrm, in0=cvt[:, cb, :], in1=mean, op=ALU.subtract), ["cvt%d" % cb, "mean"], [kn])
                kb.v(lambda e, nrm=nrm: e.tensor_tensor(out=nrm, in0=nrm, in1=rstd, op=ALU.mult), [kn, "rstd"], [kn])
                ACT(kb, nrm, nrm, AF.Silu, [kn, "clng", "clnb"], [kn], scale=clng[:, cb:cb + 1], bias=clnb[:, cb:cb + 1])
                o = act[:, cb, :]
                kb.g(lambda e, o=o, nrm=nrm, sz=sz: e.tensor_tensor(out=o, in0=nrm, in1=sz, op=ALU.mult), [kn, kz], ["act%d" % cb])
            actk = ["act%d" % i for i in range(8)]
            if t4 < 3:
                c2_loads(t4 + 1)
            kb.dma(xtiles[0], xsrc[xoff + t4 * 512:xoff + t4 * 512 + 128, :], w=["Ox0"])
            for s in range(4):
                ssl = slice(s * 128, (s + 1) * 128)
                row0 = t4 * 512 + s * 128
                bi = s % 2
                xtile = xtiles[bi]
                if s < 3:
                    kb.dma(xtiles[1 - bi], xsrc[xoff + row0 + 128:xoff + row0 + 256, :], w=["Ox%d" % (1 - bi)])
                ys = []
                for h in range(2):
                    pi = 4 + (2 * s + h) % 2 if False else (4 + h if s % 2 == 0 else 2 + h)
                    pst = kb.ps[pi]; kps = "ps%d" % pi
                    for cb in range(8):
                        MM(kb, pst[:, 0:512], act[:, cb, ssl], wout[:, cb, h * 512:(h + 1) * 512], cb == 0, cb == 7, actk + ["wout"], [kps])
                    ys.append((pst, kps))
                if t4 < 3 and s in (0, 2):
                    c2_stats_mm(range(0, 4) if s == 0 else range(4, 8))
                resid_ln(kb, ys, xtile, gate_rep, g_rep, b_rep, OUT[row0:row0 + 128, :], rsets[bi], "O", sfx=str(bi))


def build3():
    nc = bass.Bass("TRN2", target_bir_lowering=False)
    D = {}

    def inp(name, shape):
        D[name] = nc.dram_tensor(name, shape, F32, kind="ExternalInput").ap()
    inp("xw", [4096, 1024]); inp("maskw", [128, 4096]); inp("ident", [128, 128])
    inp("cT", [128, 16]); inp("modw", [1024, 3072]); inp("modbT", [128, 24]); inp("modb_gate_rep", [128, 1024])
    inp("w_in", [1024, 3072]); inp("w_out", [1024, 1024]); inp("g_rep", [128, 1024]); inp("b_rep", [128, 1024])
    inp("dwT", [128, 8 * 31]); inp("dwbT", [128, 8]); inp("clngT", [128, 8]); inp("clnbT", [128, 8])
    OUT = nc.dram_tensor("OUT", [NTOK, 1024], F32, kind="ExternalOutput").ap()
    HM = nc.dram_tensor("HM3", [4, 128, 8 * 512], BF16, kind="Internal").ap()
    CV = nc.dram_tensor("CV3", [8, 128, NTOK], F32, kind="Internal").ap()
    with contextlib.ExitStack() as es:
        kb = KB(nc, es, F32_ARENA, BF_ARENA)
        kb.boot_clear()
        block = es.enter_context(nc.Block())
        ident = kb.f32(128)
        kb.dma(ident, D["ident"], w=["ident"])
        modT, gate_rep = adaln(kb, D)
        Hrow = r4(kb.bf(4 * 32 * 94), 4, 32, 94)
        Hcol = r4(kb.bf(4 * 64 * 64), 4, 64, 64)
        kb.v(lambda e: e.memset(Hrow, 0.0), [], ["Hrow"])
        mB = kb.mark()
        wv = D["w_in"].rearrange("(k p) c -> p k c", p=128)
        wa = r3(kb.bf(8 * 1024), 8, 1024)
        wg = r3(kb.bf(8 * 1024), 8, 1024)
        kb.dma(wa, wv[:, :, 0:1024], w=["wa"], q="gpsimd")
        kb.dma(wg, wv[:, :, 1024:2048], w=["wg"], q="gpsimd")
        hm = r3(kb.bf(8 * 512), 8, 512)
        xt = r3(kb.f32(4 * 1024), 4, 1024)
        msk = kb.f32(512)
        sig = kb.f32(512)
        for tw in range(8):
            own = 2 <= tw < 6
            make_hmod(kb, D["xw"][tw * 512:(tw + 1) * 512, :], modT, 0, hm, xt, ident)
            if own:
                kb.dma(HM[tw - 2], hm.rearrange("p a b -> p (a b)"), r=["hm"], w=["HMw%d" % tw])
            kb.dma(msk, D["maskw"][:, tw * 512:(tw + 1) * 512], w=["msk"])
            for cb in (range(8) if own else range(4, 8)):
                pa = kb.ps[(cb % 4) * 2]; ka = "ps%d" % ((cb % 4) * 2)
                pg = kb.ps[(cb % 4) * 2 + 1]; kg = "ps%d" % ((cb % 4) * 2 + 1)
                for k in range(8):
                    MM(kb, pa[:, 0:512], wa[:, k, cb * 128:(cb + 1) * 128], hm[:, k, :], k == 0, k == 7, ["wa", "hm"] + ([ka] if k else []), [ka])
                for k in range(8):
                    MM(kb, pg[:, 0:512], wg[:, k, cb * 128:(cb + 1) * 128], hm[:, k, :], k == 0, k == 7, ["wg", "hm"] + ([kg] if k else []), [kg])
                ACT(kb, sig, pg[:, 0:512], AF.Sigmoid, [kg], ["sig"])
                kb.v(lambda e, pa=pa: e.tensor_tensor(out=sig, in0=pa[:, 0:512], in1=sig, op=ALU.mult), [ka, "sig"], ["sig"])
                if cb < 4:
                    o = Hrow[:, cb, (tw - 2) * 8:(tw - 1) * 8, 15:79]
                    kb.g(lambda e, o=o: e.tensor_tensor(out=o, in0=r3(sig, 8, 64), in1=r3(msk, 8, 64), op=ALU.mult), ["sig", "msk"], ["Hrow"])
                else:
                    o = Hcol[:, cb - 4, tw * 8:(tw + 1) * 8, :]
                    kb.g(lambda e, o=o: e.tensor_tensor(out=o, in0=r3(sig, 8, 64), in1=r3(msk, 8, 64), op=ALU.mult), ["sig", "msk"], ["Hcol"])
        kb.P.barrier()
        kb.release(mB)
        layer1_tail(kb, D, D["xw"], 1024, OUT, HM, CV, Hrow, Hcol, ident, gate_rep)
        kb.P.barrier()
        print("stage3 nops", kb.P.nops, "arena", kb.pf, kb.pb)
        kb.P.emit(kb.sems, block)
    return nc


def host_consts():
    selF = np.zeros((128, 8, 240), np.float32)
    for gl in range(8):
        for h in range(16):
            selF[16 * gl + h, gl, 112 + h] = 1.0
    ident = np.eye(128, dtype=np.float32)
    si = np.arange(128) // 16
    maskF = (si[None, :] >= si[:, None]).astype(np.float32)
    maskB = (si[None, :] <= si[:, None]).astype(np.float32)
    return dict(selF=selF.reshape(128, -1), ident=ident, maskF=maskF, maskB=maskB)


def s5_host_layout(lam_re, lam_im, log_dt, b_re, b_im, c_re, c_im, d):
    o = {}
    o["lamre"] = lam_re.transpose(0, 2, 1).reshape(128, 32)
    o["lamim"] = lam_im.transpose(0, 2, 1).reshape(128, 32)
    o["logdt"] = np.repeat(log_dt[:, None, :], 64, axis=1).reshape(128, 32)
    o["bre"] = b_re.transpose(0, 2, 1, 3).reshape(128, 512)
    o["bim"] = b_im.transpose(0, 2, 1, 3).reshape(128, 512)
    o["cre"] = c_re.transpose(0, 3, 1, 2).reshape(128, 512)
    o["cim"] = c_im.transpose(0, 3, 1, 2).reshape(128, 512)
    o["dT"] = np.tile(d.reshape(32, 16).T, (8, 1))
    return {k: np.ascontiguousarray(v, dtype=np.float32) for k, v in o.items()}


def colT(v, n):
    return np.ascontiguousarray(np.asarray(v, np.float32).reshape(n, 128).T)


def rep(v):
    return np.ascontiguousarray(np.tile(np.asarray(v, np.float32)[None, :], (128, 1)))


_NC = {}


def _get(stage):
    if stage not in _NC:
        _NC[stage] = build12(stage) if stage in (1, 2, "F") else build3()
    return _NC[stage]


def kernel(x, c, ctx, c_ctx, mod_w, mod_b, norm_g, norm_b, ev_w_in, ev_w_out,
           s5_lam_re, s5_lam_im, s5_log_dt, s5_b_re, s5_b_im, s5_c_re, s5_c_im, s5_d,
           glu_w, glu_b, sgu_ln_g, sgu_ln_b, sgu_w, sgu_b,
           od_w_in, od_w_out, dw_w, dw_b, conv_ln_g, conv_ln_b):
    A = lambda a: np.asarray(a, dtype=np.float32)
    x, c, ctx, c_ctx, mod_w, mod_b = A(x), A(c), A(ctx), A(c_ctx), A(mod_w), A(mod_b)
    cores = list(range(8))
    base = dict(host_consts())
    base.update(s5_host_layout(A(s5_lam_re)[0], A(s5_lam_im)[0], A(s5_log_dt)[0], A(s5_b_re)[0], A(s5_b_im)[0],
                               A(s5_c_re)[0], A(s5_c_im)[0], A(s5_d)[0]))

    def cT_of(b):
        return np.ascontiguousarray(np.stack([c[b].reshape(8, 128).T, c_ctx.reshape(8, 128).T], axis=2).reshape(128, 16))

    def mod_of(l, sfx=""):
        return {"modw" + sfx: np.ascontiguousarray(mod_w[l]), "modbT" + sfx: colT(mod_b[l], 24),
                "modb_gate_rep" + sfx: rep(mod_b[l][2048:3072])}
    shared = dict(base)
    shared.update(mod_of(0))
    shared.update(mod_of(1, "_1"))
    shared.update(w_in=np.ascontiguousarray(A(ev_w_in)[0]), g_rep=rep(A(norm_g)[0]), b_rep=rep(A(norm_b)[0]),
                  glu_w=np.ascontiguousarray(A(glu_w)[0]), glu_bT=colT(A(glu_b)[0], 4),
                  lng_rep=rep(A(sgu_ln_g)[0]), lnb_rep=rep(A(sgu_ln_b)[0]),
                  wsT=np.ascontiguousarray(A(sgu_w)[0].transpose(2, 0, 1).reshape(128, 1024)),
                  bs_tm=np.ascontiguousarray(A(sgu_b)[0].T), w_out=np.ascontiguousarray(A(ev_w_out)[0]),
                  w_in_1=np.ascontiguousarray(A(od_w_in)[0]), w_out_1=np.ascontiguousarray(A(od_w_out)[0]),
                  g_rep_1=rep(A(norm_g)[1]), b_rep_1=rep(A(norm_b)[1]),
                  dwT_1=np.ascontiguousarray(A(dw_w)[0].T.reshape(8, 128, 31).transpose(1, 0, 2).reshape(128, 8 * 31)),
                  dwbT_1=colT(A(dw_b)[0], 8), clngT_1=colT(A(conv_ln_g)[0], 8), clnbT_1=colT(A(conv_ln_b)[0], 8))
    maps = []
    for core in cores:
        b, half = core // 2, core % 2
        m = dict(shared)
        mh = np.zeros((128, 2), np.float32)
        mh[:, 0] = float(half)
        mh[:, 1] = 1.0 - float(half)
        m.update(x=np.ascontiguousarray(x[b, half * 2048:(half + 1) * 2048]), ctx=np.ascontiguousarray(ctx[b]),
                 cT=cT_of(b), cT_1=cT_of(b), mhalf=mh)
        maps.append(m)
    res = run_bass_kernel_spmd(_get("F"), maps, core_ids=cores).results
    out = np.zeros((4, 4096, 1024), np.float32)
    for core in cores:
        b, half = core // 2, core % 2
        out[b, half * 2048:(half + 1) * 2048] = res[core]["OUT"]
    return out
```

```python
import math
import contextlib
import numpy as np
import concourse.bass as bass
import concourse.mybir as mybir
from concourse.bass_utils import run_bass_kernel_spmd

F32 = mybir.dt.float32
BF16 = mybir.dt.bfloat16
ALU = mybir.AluOpType
AF = mybir.ActivationFunctionType

NDMASEM = 12
W = 32
T = 8
G = 32
J_OWN = 256
J_CTX = 32


class Prog:
    ENGS = ["sync", "scalar", "vector", "gpsimd", "tensor"]

    def __init__(self, nc, dma_queues=("sync", "gpsimd")):
        self.nc = nc
        self.ops = {e: [] for e in self.ENGS}
        self.cnt = {}
        self.last_w = {}
        self.readers = {}
        self.seen = {e: {} for e in self.ENGS}
        self.dma_i = {q: 0 for q in dma_queues}
        self.semnames = list(self.ENGS)
        for q in dma_queues:
            for i in range(NDMASEM):
                self.semnames.append(f"{q}_d{i}")
        for s in self.semnames:
            self.cnt[s] = 0
        self.nops = 0

    def op(self, eng, fn, reads=(), writes=(), dma=False):
        reads = [reads] if isinstance(reads, str) else list(reads)
        writes = [writes] if isinstance(writes, str) else list(writes)
        deps = {}

        def add(sv):
            if sv is not None and sv[1] > deps.get(sv[0], 0):
                deps[sv[0]] = sv[1]

        for k in reads:
            add(self.last_w.get(k))
        for k in writes:
            add(self.last_w.get(k))
            for r in self.readers.get(k, ()):
                add(r)
        if dma:
            i = self.dma_i[eng]
            self.dma_i[eng] += 1
            sem = f"{eng}_d{i % NDMASEM}"
            inc = 16
            if self.cnt[sem] > 0:
                deps[sem] = max(deps.get(sem, 0), self.cnt[sem])
        else:
            sem = eng
            inc = 1
        waits = []
        for s, v in deps.items():
            if s == "tensor" and eng == "tensor":
                continue
            if v > self.seen[eng].get(s, 0):
                waits.append((s, v))
                self.seen[eng][s] = v
        self.cnt[sem] += inc
        val = self.cnt[sem]
        self.ops[eng].append((waits, fn, sem, inc))
        for k in writes:
            self.last_w[k] = (sem, val)
            self.readers[k] = []
        for k in reads:
            self.readers.setdefault(k, []).append((sem, val))
        self.nops += 1

    def cc(self, fn, sem, reads, writes):
        deps = {}
        for kx in reads:
            sv = self.last_w.get(kx)
            if sv is not None and sv[1] > deps.get(sv[0], 0):
                deps[sv[0]] = sv[1]
        waits = []
        for s_, v in deps.items():
            if v > self.seen["gpsimd"].get(s_, 0):
                waits.append((s_, v))
                self.seen["gpsimd"][s_] = v
        self.ops["gpsimd"].append((waits, fn, sem, 1))
        self.cnt[sem] = 1
        for kx in writes:
            self.last_w[kx] = (sem, 1)
            self.readers[kx] = []
        self.nops += 1

    barrier_skip = ()

    def barrier(self):
        skip = set(self.barrier_skip)
        for eng in self.ENGS:
            waits = [(s, v) for s, v in self.cnt.items() if s not in skip and v > self.seen[eng].get(s, 0)]
            for s, v in waits:
                self.seen[eng][s] = v
            if waits:
                self.ops[eng].append((waits, None, None, 0))
        self.last_w = {k_: sv for k_, sv in self.last_w.items() if sv[0] in skip}
        self.readers = {}

    def emit(self, sems, block):
        prog = self

        def replay(engname, eng):
            for waits, fn, sem, inc in prog.ops[engname]:
                for s, v in waits:
                    eng.wait_ge(sems[s], v)
                if fn is not None:
                    fn(eng).then_inc(sems[sem], inc)

        @block.sync
        def _(e):
            replay("sync", e)

        @block.scalar
        def _(e):
            replay("scalar", e)

        @block.vector
        def _(e):
            replay("vector", e)

        @block.gpsimd
        def _(e):
            replay("gpsimd", e)

        @block.tensor
        def _(e):
            replay("tensor", e)


class KB:
    def __init__(self, nc, es, f32_elems, bf_elems):
        self.nc = nc
        self.es = es
        self.P = Prog(nc)
        self.sems = {s: es.enter_context(nc.semaphore(s)) for s in self.P.semnames}
        self.af = es.enter_context(nc.sbuf_tensor("arena_f", [128, f32_elems], F32))
        self.ab = es.enter_context(nc.sbuf_tensor("arena_b", [128, bf_elems], BF16))
        self.pf = 0
        self.pb = 0
        self.capf = f32_elems
        self.ptop = f32_elems
        self.capb = bf_elems
        self.ps = [es.enter_context(nc.psum_tensor(f"ps{i}", [128, 512], F32)) for i in range(8)]
        self.uid = 0

    def boot_clear(self):
        for s in self.sems.values():
            self.nc.gpsimd.sem_clear(s)
        self.nc.all_engine_barrier()

    def f32t(self, n):
        self.ptop -= n
        assert self.ptop >= self.pf, ("f32 arena overflow (top)", self.ptop, self.pf)
        return self.af[:, self.ptop:self.ptop + n]

    def mark(self):
        return (self.pf, self.pb)

    def release(self, m):
        self.pf, self.pb = m

    def f32(self, n):
        a = self.af[:, self.pf:self.pf + n]
        self.pf += n
        assert self.pf <= self.ptop, ("f32 arena overflow", self.pf, self.ptop)
        return a

    def bf(self, n):
        n = (n + 1) // 2 * 2
        a = self.ab[:, self.pb:self.pb + n]
        self.pb += n
        assert self.pb <= self.capb, ("bf16 arena overflow", self.pb)
        return a

    def key(self, pfx="k"):
        self.uid += 1
        return f"{pfx}{self.uid}"

    def v(self, fn, r=(), w=()):
        self.P.op("vector", fn, r, w)

    def g(self, fn, r=(), w=()):
        self.P.op("gpsimd", fn, r, w)

    def a(self, fn, r=(), w=()):
        self.P.op("scalar", fn, r, w)

    def t(self, fn, r=(), w=()):
        self.P.op("tensor", fn, r, w)

    def cc(self, fn, sem, r, w):
        self.P.cc(fn, sem, r, w)

    def dma(self, out, in_, r=(), w=(), q="sync"):
        self.P.op(q, lambda e: e.dma_start(out=out, in_=in_), r, w, dma=True)

    def tt(self, eng, out, a, b, op, r, w):
        self.P.op(eng, lambda e: e.tensor_tensor(out=out, in0=a, in1=b, op=op), r, w)

    def cmul(self, ore, oim, are, aim, bre, bim, t1, t2, r, w, kt, engs=("vector", "gpsimd")):
        e0, e1 = engs
        k = [kt + "_t%d" % i for i in range(4)]
        self.tt(e0, t1[0], are, bre, ALU.mult, r, [k[0]])
        self.tt(e0, t2[0], aim, bim, ALU.mult, r, [k[1]])
        self.tt(e0, ore, t1[0], t2[0], ALU.subtract, [k[0], k[1]], w)
        self.tt(e1, t1[1], are, bim, ALU.mult, r, [k[2]])
        self.tt(e1, t2[1], aim, bre, ALU.mult, r, [k[3]])
        self.tt(e1, oim, t1[1], t2[1], ALU.add, [k[2], k[3]], w)


def r3(ap, a, b):
    return ap.rearrange("p (a b) -> p a b", a=a, b=b)


def r4(ap, a, b, c):
    return ap.rearrange("p (a b c) -> p a b c", a=a, b=b, c=c)


def bc_last(ap2, n):
    return ap2.unsqueeze(2).broadcast_to([ap2.shape[0], ap2.shape[1], n])


def s5_pre_init(kb, din, need_out=True):
    P = kb.P
    out = {}
    B8 = kb.bf(G * 2 * 128)
    out["B8"] = r4(B8, G, 2, 128)
    if need_out:
        C8b = kb.bf(2 * G * 128)
        out["C8"] = r4(C8b, 2, G, 128)
        Toep = kb.bf(G * 128)
        out["Toep"] = r3(Toep, G, 128)
    tabs = {}
    for nm in ["Tpre", "Tpim", "Ture", "Tuim"]:
        tabs[nm] = r3(kb.f32(G * W), G, W)
    out.update(tabs)
    rmask = kb.f32(W)
    out["rmask"] = rmask
    ptop0 = kb.ptop
    sm = {}
    for nm in ["lamre", "lamim", "logdt", "dT"]:
        sm[nm] = kb.f32t(G)
        kb.dma(sm[nm], din[nm], w=[nm])
    big = {}
    for nm in ["bre", "bim", "cre", "cim"]:
        big[nm] = r3(kb.f32t(G * 16), G, 16)
        kb.dma(big[nm], din[nm].rearrange("p (g h) -> p g h", g=G), w=[nm])
    ident = kb.f32t(128)
    kb.dma(ident, din["ident"], w=["ident"])
    maskF = kb.f32t(128)
    maskB = kb.f32t(128)
    kb.dma(maskF, din["maskF"], w=["maskF"])
    kb.dma(maskB, din["maskB"], w=["maskB"])
    S = lambda: kb.f32t(G)
    dt, are, th, mag, c, s, cc, ss, cs = [S() for _ in range(9)]
    kb.a(lambda e: e.activation(out=dt, in_=sm["logdt"], func=AF.Exp), ["logdt"], ["dt"])
    kb.tt("vector", are, sm["lamre"], dt, ALU.mult, ["lamre", "dt"], ["are"])
    kb.tt("vector", th, sm["lamim"], dt, ALU.mult, ["lamim", "dt"], ["th"])
    kb.a(lambda e: e.activation(out=mag, in_=are, func=AF.Exp), ["are"], ["mag"])
    NSQ = 6
    halfpi = kb.f32t(1)
    kb.v(lambda e: e.memset(halfpi, math.pi / 2), [], ["halfpi"])
    kb.a(lambda e: e.activation(out=s, in_=th, func=AF.Sin, scale=1.0 / 2 ** NSQ), ["th"], ["s"])
    kb.a(lambda e: e.activation(out=c, in_=th, func=AF.Sin, scale=1.0 / 2 ** NSQ, bias=halfpi), ["th", "halfpi"], ["c"])
    for _ in range(NSQ):
        kb.tt("vector", cc, c, c, ALU.mult, ["c"], ["cc"])
        kb.tt("vector", ss, s, s, ALU.mult, ["s"], ["ss"])
        kb.tt("vector", cs, c, s, ALU.mult, ["c", "s"], ["cs"])
        kb.tt("vector", c, cc, ss, ALU.subtract, ["cc", "ss"], ["c"])
        kb.v(lambda e: e.tensor_scalar(out=s, in0=cs, scalar1=2.0, scalar2=None, op0=ALU.mult), ["cs"], ["s"])
    pwre = r3(kb.f32t(9 * G), 9, G)
    pwim = r3(kb.f32t(9 * G), 9, G)
    kb.v(lambda e: e.memset(pwre[:, 0, :], 1.0), [], ["pw0"])
    kb.v(lambda e: e.memset(pwim[:, 0, :], 0.0), [], ["pw0"])
    kb.tt("vector", pwre[:, 1, :], mag, c, ALU.mult, ["mag", "c"], ["pw1"])
    kb.tt("vector", pwim[:, 1, :], mag, s, ALU.mult, ["mag", "s"], ["pw1"])
    t1 = [S(), S()]
    t2 = [S(), S()]
    for k in range(2, 9):
        kb.cmul(pwre[:, k, :], pwim[:, k, :], pwre[:, k - 1, :], pwim[:, k - 1, :], pwre[:, 1, :], pwim[:, 1, :],
                t1, t2, ["pw%d" % (k - 1), "pw1"], ["pw%d" % k], "pwt")
    nre, den, rden, qre, qim, u1, u2 = [S() for _ in range(7)]
    kb.v(lambda e: e.tensor_scalar(out=nre, in0=pwre[:, 1, :], scalar1=-1.0, scalar2=None, op0=ALU.add), ["pw1"], ["nre"])
    nim = pwim[:, 1, :]
    kb.tt("vector", u1, sm["lamre"], sm["lamre"], ALU.mult, ["lamre"], ["u1"])
    kb.tt("vector", u2, sm["lamim"], sm["lamim"], ALU.mult, ["lamim"], ["u2"])
    kb.tt("vector", den, u1, u2, ALU.add, ["u1", "u2"], ["den"])
    kb.v(lambda e: e.reciprocal(out=rden, in_=den), ["den"], ["rden"])
    kb.tt("vector", u1, nre, sm["lamre"], ALU.mult, ["nre", "lamre"], ["u1"])
    kb.tt("vector", u2, nim, sm["lamim"], ALU.mult, ["pw1", "lamim"], ["u2"])
    kb.tt("vector", qre, u1, u2, ALU.add, ["u1", "u2"], ["qre0"])
    kb.tt("vector", qre, qre, rden, ALU.mult, ["qre0", "rden"], ["qre"])
    kb.tt("vector", u1, nim, sm["lamre"], ALU.mult, ["pw1", "lamre"], ["u1"])
    kb.tt("vector", u2, nre, sm["lamim"], ALU.mult, ["nre", "lamim"], ["u2"])
    kb.tt("vector", qim, u1, u2, ALU.subtract, ["u1", "u2"], ["qim0"])
    kb.tt("vector", qim, qim, rden, ALU.mult, ["qim0", "rden"], ["qim"])
    L = lambda: r3(kb.f32t(G * 16), G, 16)
    bbre, bbim = L(), L()
    mL = kb.ptop
    lt1, lt2 = [L(), L()], [L(), L()]
    kb.cmul(bbre, bbim, bc_last(qre, 16), bc_last(qim, 16), big["bre"], big["bim"], lt1, lt2,
            ["qre", "qim", "bre", "bim"], ["bbar"], "bbt")
    kb.P.barrier()
    kb.ptop = mL
    l8re, l8im = pwre[:, 8, :], pwim[:, 8, :]
    i8re, i8im = S(), S()
    kb.tt("vector", u1, l8re, l8re, ALU.mult, ["pw8"], ["u1"])
    kb.tt("vector", u2, l8im, l8im, ALU.mult, ["pw8"], ["u2"])
    kb.tt("vector", den, u1, u2, ALU.add, ["u1", "u2"], ["den"])
    kb.v(lambda e: e.reciprocal(out=rden, in_=den), ["den"], ["rden"])
    kb.tt("vector", i8re, l8re, rden, ALU.mult, ["pw8", "rden"], ["i8"])
    kb.tt("vector", u1, l8im, rden, ALU.mult, ["pw8", "rden"], ["u1"])
    kb.v(lambda e: e.tensor_scalar(out=i8im, in0=u1, scalar1=-1.0, scalar2=None, op0=ALU.mult), ["u1"], ["i8"])
    mW = kb.ptop
    for (tre, tim, bre_, bim_, nm, dep) in [(tabs["Ture"], tabs["Tuim"], l8re, l8im, "Tu", "pw8"),
                                           (tabs["Tpre"], tabs["Tpim"], i8re, i8im, "Tp", "i8")]:
        kb.v(lambda e, tre=tre, bre_=bre_: e.tensor_copy(out=tre[:, :, 0], in_=bre_), [dep], [nm])
        kb.v(lambda e, tim=tim, bim_=bim_: e.tensor_copy(out=tim[:, :, 0], in_=bim_), [dep], [nm])
        n = 1
        if nm == "Tu":
            wt1 = [r3(kb.f32t(G * W // 2), G, W // 2) for _ in range(2)]
            wt2 = [r3(kb.f32t(G * W // 2), G, W // 2) for _ in range(2)]
        else:
            kb.P.barrier()
        while n < W:
            kb.cmul(tre[:, :, n:2 * n], tim[:, :, n:2 * n], tre[:, :, 0:n], tim[:, :, 0:n],
                    bc_last(tre[:, :, n - 1], n), bc_last(tim[:, :, n - 1], n),
                    [x[:, :, 0:n] for x in wt1], [x[:, :, 0:n] for x in wt2], [nm], [nm], nm + "t")
            n *= 2
    kb.P.barrier()
    kb.ptop = mW
    kb.v(lambda e: e.memset(rmask, 1.0), [], ["rmask"])
    kb.v(lambda e: e.memset(rmask[:, 0:1], 0.0), ["rmask"], ["rmask"])
    PWbre, PWbim, PWcre, PWcim = [r3(kb.f32t(G * 8), G, 8) for _ in range(4)]
    ci = 0
    for sidx in range(8):
        for (ps_, eb, ec) in [(slice(0, 64), 7 - sidx, sidx + 1), (slice(64, 128), sidx, 8 - sidx)]:
            for (dst, srcp, ex) in [(PWbre, pwre, eb), (PWbim, pwim, eb), (PWcre, pwre, ec), (PWcim, pwim, ec)]:
                o = dst[ps_, :, sidx]
                i_ = srcp[ps_, ex, :]
                eng = kb.v if ci % 2 == 0 else kb.g
                ci += 1
                eng(lambda e, o=o, i_=i_: e.tensor_copy(out=o, in_=i_), ["pw%d" % ex], ["PW"])
    kb.P.barrier()
    out["lamWre"] = tabs["Ture"][:, :, W - 1]
    out["lamWim"] = tabs["Tuim"][:, :, W - 1]
    ctx = dict(out=out, need_out=need_out, ptop0=ptop0, PW=(PWbre, PWbim, PWcre, PWcim), bb=(bbre, bbim), big=big, sm=sm,
               i8=(i8re, i8im), ident=ident, maskF=maskF, maskB=maskB)
    return out, ctx


def s5_pre_batches(kb, ctx):
    out = ctx["out"]; need_out = ctx["need_out"]
    PWbre, PWbim, PWcre, PWcim = ctx["PW"]
    bbre, bbim = ctx["bb"]; big = ctx["big"]; sm = ctx["sm"]
    i8re, i8im = ctx["i8"]; ident = ctx["ident"]; maskF = ctx["maskF"]; maskB = ctx["maskB"]
    GH = 2
    A4 = lambda: r4(kb.f32t(GH * 128), GH, 8, 16)
    BLre, BLim = A4(), A4()
    if need_out:
        BLsre = r4(kb.bf(GH * 128), GH, 8, 16)
        BLsim = r4(kb.bf(GH * 128), GH, 8, 16)
    T = [A4() for _ in range(4)]
    f3 = lambda x: x.rearrange("p g s h -> p g (s h)")
    for gh in range(0, G, GH):
        gs = slice(gh, gh + GH)
        bs = lambda x: x[:, gs, :].unsqueeze(3).broadcast_to([128, GH, 8, 16])
        bh = lambda x: x[:, gs, :].unsqueeze(2).broadcast_to([128, GH, 8, 16])
        kb.cmul(BLre, BLim, bs(PWbre), bs(PWbim), bh(bbre), bh(bbim), [T[0], T[1]], [T[2], T[3]], ["PW", "bbar"], ["BL"], "bT")
        for gi in range(GH):
            g = gh + gi
            for ri, src in enumerate([BLre, BLim]):
                pi = 4 + (gi * 2 + ri) % 2
                pst = kb.ps[pi]
                kps = "ps%d" % pi
                srcg = src[:, gi, :, :].rearrange("p s h -> p (s h)")
                kb.t(lambda e, pst=pst, srcg=srcg: e.transpose(pst[:, 0:128], srcg, ident), ["BL", "ident"], [kps])
                dst = out["B8"][:, g, ri, :]
                kb.a(lambda e, pst=pst, dst=dst: e.activation(out=dst, in_=pst[:, 0:128], func=AF.Copy), [kps], ["B8"])
        if need_out:
            kb.cmul(f3(BLsre), f3(BLsim), f3(BLre), f3(BLim), bc_last(i8re[:, gs], 128), bc_last(i8im[:, gs], 128),
                    [f3(T[0]), f3(T[1])], [f3(T[2]), f3(T[3])], ["BL", "i8"], ["BLs"], "bT")
            yield
            C8re, C8im = BLre, BLim
            kb.cmul(C8re, C8im, bs(PWcre), bs(PWcim), bh(big["cre"]), bh(big["cim"]), [T[0], T[1]], [T[2], T[3]],
                    ["PW", "cre", "cim"], ["BL"], "bT")
            kb.v(lambda e: e.tensor_scalar(out=C8im, in0=C8im, scalar1=-1.0, scalar2=None, op0=ALU.mult), ["BL"], ["BL"])
            d0 = out["C8"][:, 0, gs, :]
            d1 = out["C8"][:, 1, gs, :]
            kb.v(lambda e, d0=d0: e.tensor_copy(out=d0, in_=f3(C8re)), ["BL"], ["C8b"])
            kb.g(lambda e, d1=d1: e.tensor_copy(out=d1, in_=f3(C8im)), ["BL"], ["C8b"])
            yield
            for gi in range(GH):
                g = gh + gi
                pss = []
                for d in range(2):
                    pi = 6 + d
                    pst = kb.ps[pi]
                    kps = "ps%d" % pi
                    rows = slice(64 * d, 64 * d + 64)
                    lre = BLsre[rows, gi, :, :].rearrange("p s h -> p (s h)")
                    lim = BLsim[rows, gi, :, :].rearrange("p s h -> p (s h)")
                    rre = out["C8"][rows, 0, g, :]
                    rim = out["C8"][rows, 1, g, :]
                    kb.t(lambda e, pst=pst, lre=lre, rre=rre: e.matmul(pst[:, 0:128], lhsT=lre, rhs=rre, start=True, stop=False), ["BLs", "C8b"], [kps])
                    kb.t(lambda e, pst=pst, lim=lim, rim=rim: e.matmul(pst[:, 0:128], lhsT=lim, rhs=rim, start=False, stop=True), ["BLs", "C8b"], [kps])
                    pss.append((pst, kps))
                tA = T[0][:, gi, :, :].rearrange("p s h -> p (s h)")
                tB = T[1][:, gi, :, :].rearrange("p s h -> p (s h)")
                kA = "bT_t0"
                kBt = "bT_t2"
                kb.v(lambda e, tA=tA, p0=pss[0][0]: e.tensor_tensor(out=tA, in0=p0[:, 0:128], in1=maskF, op=ALU.mult), [pss[0][1], "maskF"], [kA])
                kb.v(lambda e, tB=tB, p1=pss[1][0]: e.tensor_tensor(out=tB, in0=p1[:, 0:128], in1=maskB, op=ALU.mult), [pss[1][1], "maskB"], [kBt])
                kb.g(lambda e, tA=tA, tB=tB: e.tensor_tensor(out=tA, in0=tA, in1=tB, op=ALU.add), [kA, kBt], [kA])
                dcol = sm["dT"][:, g:g + 1]
                dst = out["Toep"][:, g, :]
                kb.v(lambda e, tA=tA, dcol=dcol, dst=dst: e.scalar_tensor_tensor(out=dst, in0=ident, scalar=dcol, in1=tA, op0=ALU.mult, op1=ALU.add),
                     [kA, "dT", "ident"], ["Toep"])
        yield
    return


def s5_precompute(kb, din, need_out=True):
    out, ctx = s5_pre_init(kb, din, need_out)
    for _ in s5_pre_batches(kb, ctx):
        pass
    kb.P.barrier()
    kb.ptop = ctx["ptop0"]
    return out


def interleave(gens):
    gens = list(gens)
    while gens:
        for g_ in list(gens):
            try:
                next(g_)
            except StopIteration:
                gens.remove(g_)


def _s5_gq(kb, cst, X, J, gb, GB, bufs, tag, sfx, psb):
    NW = J // W
    m1a, m2a, m3a, m4a, qre, qim = bufs
    v4 = lambda x: r4(x[:, 0:GB * J], GB, NW, W)
    v3 = lambda x: r3(x[:, 0:GB * J], GB, J)
    gs = slice(gb, gb + GB)
    tb = lambda nm: cst[nm][:, gs, :].unsqueeze(2).broadcast_to([128, GB, NW, W])
    K = lambda n: n + sfx
    assert GB * J <= 512
    for ri in range(2):
        pi = psb + ri
        pst = kb.ps[pi]
        kps = "ps%d" % pi
        for gi in range(GB):
            g = gb + gi
            lhs = cst["B8"][:, g, ri, :]
            rhs = X[:, g, :]
            kb.t(lambda e, pst=pst, lhs=lhs, rhs=rhs, gi=gi: e.matmul(pst[:, gi * J:(gi + 1) * J], lhsT=lhs, rhs=rhs, start=True, stop=True),
                 ["B8", tag + "X"], [kps])
        dst = v3(m1a if ri == 0 else m2a)
        kd = K("m1") if ri == 0 else K("m2")
        src = r3(pst[:, 0:GB * J], GB, J)
        kb.a(lambda e, src=src, dst=dst: e.activation(out=dst[0:64, :, :], in_=src[0:64, :, :], func=AF.Copy),
             [kps], [kd + "_f%d" % i for i in range(GB)])
        kb.a(lambda e, src=src, dst=dst: e.activation(out=dst[64:128, :, ::-1], in_=src[64:128, :, :], func=AF.Copy),
             [kps], [kd + "_b%d" % i for i in range(GB)])
        yield
    K1 = [K("m1") + "_f%d" % i for i in range(GB)] + [K("m1") + "_b%d" % i for i in range(GB)]
    K2 = [K("m2") + "_f%d" % i for i in range(GB)] + [K("m2") + "_b%d" % i for i in range(GB)]
    kb.tt("vector", v4(m3a), v4(m1a), tb("Tpre"), ALU.mult, K1 + ["Tp"], [K("m3")])
    kb.tt("gpsimd", v4(m4a), v4(m2a), tb("Tpim"), ALU.mult, K2 + ["Tp"], [K("m4")])
    yield
    kb.tt("vector", v4(qre), v4(m3a), v4(m4a), ALU.subtract, [K("m3"), K("m4")], [K("qre")])
    yield
    kb.tt("vector", v4(m3a), v4(m1a), tb("Tpim"), ALU.mult, K1 + ["Tp"], [K("m3")])
    kb.tt("gpsimd", v4(m4a), v4(m2a), tb("Tpre"), ALU.mult, K2 + ["Tp"], [K("m4")])
    yield
    kb.tt("vector", v4(qim), v4(m3a), v4(m4a), ALU.add, [K("m3"), K("m4")], [K("qim")])
    yield
    return


def s5_pass1(kb, cst, X, J, Qend_re, Qend_im, tag, QD=None, GB=2):
    NW = J // W
    assert G % (2 * GB) == 0 and GB * J <= 512
    m0 = kb.mark()
    sets = [[kb.f32(GB * J) for _ in range(6)] for _ in range(2)]
    v4 = lambda x: r4(x, GB, NW, W)

    def stream(si):
        bufs = sets[si]
        sfx = "_s%d" % si
        for gb in range(si * GB, G, 2 * GB):
            gs = slice(gb, gb + GB)
            yield from _s5_gq(kb, cst, X, J, gb, GB, bufs, tag, sfx, 2 * si)
            if QD is not None:
                kb.dma(QD[gb // GB, 0], bufs[4], r=["qre" + sfx], w=["QD%d" % (gb // GB)])
                kb.dma(QD[gb // GB, 1], bufs[5], r=["qim" + sfx], w=["QD%d" % (gb // GB)])
            o1 = Qend_re[:, gs, :]
            o2 = Qend_im[:, gs, :]
            kb.v(lambda e, o1=o1: e.tensor_reduce(out=o1, in_=v4(bufs[4]), axis=mybir.AxisListType.X, op=ALU.add), ["qre" + sfx], [tag + "Qend"])
            yield
            kb.v(lambda e, o2=o2: e.tensor_reduce(out=o2, in_=v4(bufs[5]), axis=mybir.AxisListType.X, op=ALU.add), ["qim" + sfx], [tag + "Qend"])
            yield
    interleave([stream(0), stream(1)])
    kb.P.barrier()
    kb.release(m0)


def s5_carries(kb, cst, Qend_re, Qend_im, init_re, init_im, car_re, car_im, NW, tag):
    m0 = kb.mark()
    ct = [r3(kb.f32(G), G, 1) for _ in range(6)]
    kb.v(lambda e: e.tensor_copy(out=car_re[:, :, 0], in_=init_re), [tag + "init"], [tag + "car"])
    kb.v(lambda e: e.tensor_copy(out=car_im[:, :, 0], in_=init_im), [tag + "init"], [tag + "car"])
    lw_re = cst["lamWre"].unsqueeze(2)
    lw_im = cst["lamWim"].unsqueeze(2)
    for w in range(NW):
        cr, ci = car_re[:, :, w:w + 1], car_im[:, :, w:w + 1]
        nr, ni = car_re[:, :, w + 1:w + 2], car_im[:, :, w + 1:w + 2]
        kb.tt("vector", ct[0], Qend_re[:, :, w:w + 1], cr, ALU.add, [tag + "Qend", tag + "car"], ["ct0"])
        kb.tt("gpsimd", ct[1], Qend_im[:, :, w:w + 1], ci, ALU.add, [tag + "Qend", tag + "car"], ["ct1"])
        kb.cmul(nr, ni, ct[0], ct[1], lw_re, lw_im, [ct[2], ct[3]], [ct[4], ct[5]], ["ct0", "ct1", "Tu"], [tag + "car"], "ctt")
    kb.P.barrier()
    kb.release(m0)


def s5_pass2(kb, cst, X, J, car_re, car_im, SPre, SPim, tag, g0=0, g1=G, QD=None, car_tag=None, pre_barrier=True):
    NW = J // W
    GB = 2
    ctag = tag if car_tag is None else car_tag
    m0 = kb.mark()
    sets = [[kb.f32(GB * J) for _ in range(6)] for _ in range(2)]
    mska = kb.f32(GB * J)
    v4 = lambda x: r4(x, GB, NW, W)
    v3 = lambda x: r3(x, GB, J)
    mk = cst["rmask"].unsqueeze(1).unsqueeze(1).broadcast_to([128, GB, NW, W])
    kb.v(lambda e: e.tensor_copy(out=v4(mska), in_=mk), ["rmask"], ["mska"])

    def stream(si):
        bufs = sets[si]
        m1a, m2a, m3a, m4a, qre, qim = bufs
        sfx = "_s%d" % si
        K = lambda n: n + sfx

        def load(gb):
            kb.dma(qre, QD[gb // GB, 0], w=[K("qre")])
            kb.dma(qim, QD[gb // GB, 1], w=[K("qim")])
        first = g0 + si * GB
        load(first)
        for gb in range(first, g1, 2 * GB):
            gs = slice(gb, gb + GB)
            gsl = slice(gb - g0, gb - g0 + GB)
            tb = lambda nm: cst[nm][:, gs, :].unsqueeze(2).broadcast_to([128, GB, NW, W])
            q4r, q4i = v4(qre), v4(qim)
            cr = car_re[:, gs, 0:NW].unsqueeze(3)
            ci = car_im[:, gs, 0:NW].unsqueeze(3)
            kb.tt("vector", q4r[:, :, :, 0:1], q4r[:, :, :, 0:1], cr, ALU.add, [K("qre"), ctag + "car"], [K("qre")])
            kb.tt("vector", q4i[:, :, :, 0:1], q4i[:, :, :, 0:1], ci, ALU.add, [K("qim"), ctag + "car"], [K("qim")])
            yield
            kb.v(lambda e: e.tensor_tensor_scan(out=m1a, data0=mska, data1=qre, initial=0.0, op0=ALU.mult, op1=ALU.add), ["mska", K("qre")], [K("m1")])
            yield
            kb.v(lambda e: e.tensor_tensor_scan(out=m2a, data0=mska, data1=qim, initial=0.0, op0=ALU.mult, op1=ALU.add), ["mska", K("qim")], [K("m2")])
            yield
            if gb + 2 * GB < g1:
                load(gb + 2 * GB)
            cre4, cim4 = v4(m1a), v4(m2a)
            kb.tt("vector", v4(m3a), cre4, tb("Ture"), ALU.mult, [K("m1"), "Tu"], [K("m3")])
            kb.tt("gpsimd", v4(m4a), cim4, tb("Tuim"), ALU.mult, [K("m2"), "Tu"], [K("m4")])
            yield
            x3, y3 = v3(m3a), v3(m4a)
            kb.tt("vector", SPre[0:64, gsl, 1:J], x3[0:64, :, 0:J - 1], y3[0:64, :, 0:J - 1], ALU.subtract, [K("m3"), K("m4")], [tag + "SP"])
            kb.tt("gpsimd", SPre[64:128, gsl, 0:J - 1], x3[64:128, :, J - 2::-1], y3[64:128, :, J - 2::-1], ALU.subtract, [K("m3"), K("m4")], [tag + "SP"])
            yield
            kb.tt("vector", v4(m3a), cre4, tb("Tuim"), ALU.mult, [K("m1"), "Tu"], [K("m3")])
            kb.tt("gpsimd", v4(m4a), cim4, tb("Ture"), ALU.mult, [K("m2"), "Tu"], [K("m4")])
            yield
            kb.tt("vector", SPim[0:64, gsl, 1:J], x3[0:64, :, 0:J - 1], y3[0:64, :, 0:J - 1], ALU.add, [K("m3"), K("m4")], [tag + "SP"])
            kb.tt("vector", SPim[64:128, gsl, 0:J - 1], x3[64:128, :, J - 2::-1], y3[64:128, :, J - 2::-1], ALU.add, [K("m3"), K("m4")], [tag + "SP"])
            yield
    interleave([stream(0), stream(1)])
    i_re, i_im = car_re[:, g0:g1, 0], car_im[:, g0:g1, 0]
    kb.v(lambda e: e.tensor_copy(out=SPre[0:64, :, 0], in_=i_re[0:64, :]), [ctag + "car"], [tag + "SP"])
    kb.v(lambda e: e.tensor_copy(out=SPim[0:64, :, 0], in_=i_im[0:64, :]), [ctag + "car"], [tag + "SP"])
    kb.v(lambda e: e.tensor_copy(out=SPre[64:128, :, J - 1], in_=i_re[64:128, :]), [ctag + "car"], [tag + "SP"])
    kb.v(lambda e: e.tensor_copy(out=SPim[64:128, :, J - 1], in_=i_im[64:128, :]), [ctag + "car"], [tag + "SP"])
    kb.P.barrier()
    kb.release(m0)


def s5_relayout(kb, selF, uaT, X, J, tag):
    for g in range(G):
        pt, gl = g // 8, g % 8
        pst = kb.ps[g % 8]
        kps = "ps%d" % (g % 8)
        for s in range(8):
            rhs = uaT[:, pt, s::8]
            lhs = selF[:, gl, 16 * (7 - s):16 * (7 - s) + 128]
            kb.t(lambda e, pst=pst, lhs=lhs, rhs=rhs, s=s: e.matmul(pst[:, 0:J], lhsT=lhs, rhs=rhs, start=(s == 0), stop=(s == 7)),
                 ["selF", tag + "uaT"] + ([kps] if s else []), [kps])
        dst = X[:, g, :]
        if g % 2 == 0:
            kb.a(lambda e, pst=pst, dst=dst: e.activation(out=dst, in_=pst[:, 0:J], func=AF.Copy), [kps], [tag + "X"])
        else:
            kb.v(lambda e, pst=pst, dst=dst: e.tensor_copy(out=dst, in_=pst[:, 0:J]), [kps], [tag + "X"])


def s5_output(kb, cst, X, SPre, SPim, J, consume, tag, g0=0, g1=G):
    for g in range(g0, g1):
        pst = kb.ps[g % 8]
        kps = "ps%d" % (g % 8)
        ops = [(cst["Toep"][:, g, :], X[:, g, :]), (cst["C8"][:, 0, g, :], SPre[:, g - g0, :]), (cst["C8"][:, 1, g, :], SPim[:, g - g0, :])]
        for i, (lhs, rhs) in enumerate(ops):
            kb.t(lambda e, pst=pst, lhs=lhs, rhs=rhs, i=i: e.matmul(pst[:, 0:J], lhsT=lhs, rhs=rhs, start=(i == 0), stop=(i == 2)),
                 ["Toep", "C8b", tag + "X", tag + "SP"] + ([kps] if i else []), [kps])
        consume(g, pst, kps)


DN_ALPHA = 4 ** 0.25
DEBUG_F = False
LN_EPS = 1e-5
NTOK = 2048
F32_ARENA = 17200
BF_ARENA = 52000


def MM(kb, pst_ap, lhs, rhs, start, stop, r, w):
    kb.t(lambda e: e.matmul(pst_ap, lhsT=lhs, rhs=rhs, start=start, stop=stop), r, w)


def ACT(kb, out, in_, func, r, w, **kw):
    kb.a(lambda e: e.activation(out=out, in_=in_, func=func, **kw), r, w)


def adaln_loads(kb, D, sfx, wb, csb, csrep, gate_rep):
    cs = r3(kb.f32t(16), 8, 2)
    kb.dma(cs.rearrange("p a b -> p (a b)"), D["cT"], w=["cs" + sfx])
    modbT = kb.f32t(24)
    kb.dma(modbT, D["modbT"], w=["modbT" + sfx])
    gb = gate_rep
    kb.dma(gb, D["modb_gate_rep"], w=["gate_rep" + sfx])
    wv = D["modw"].rearrange("(k p) c -> p k c", p=128)
    for blk in range(6):
        kb.dma(wb[blk], wv[:, :, blk * 512:(blk + 1) * 512], w=["adw%d" % blk + sfx], q="gpsimd")
    return dict(cs=cs, modbT=modbT, gb=gb, wb=wb, csb=csb, csrep=csrep, sfx=sfx)


def adaln_compute(kb, A, modT, gate_rep):
    sfx = A["sfx"]
    cs, modbT, gb, wb, csb, csrep = A["cs"], A["modbT"], A["gb"], A["wb"], A["csb"], A["csrep"]
    ACT(kb, csb, cs, AF.Silu, ["cs" + sfx], ["csb" + sfx])
    kb.v(lambda e: e.tensor_copy(out=csrep, in_=csb[:, :, 0:1].broadcast_to([128, 8, 128])), ["csb" + sfx], ["csrep" + sfx])
    for blk in range(6):
        w = wb[blk]
        kw = "adw%d" % blk + sfx
        for cb in range(4):
            pst = kb.ps[cb]
            kps = "ps%d" % cb
            for k in range(8):
                MM(kb, pst[:, 0:2], w[:, k, cb * 128:(cb + 1) * 128], csb[:, k, :], k == 0, k == 7, [kw, "csb" + sfx], [kps])
            col = blk * 4 + cb
            o = modT[:, col, :]
            b_ = modbT[:, col:col + 1].broadcast_to([128, 2])
            kb.v(lambda e, o=o, pst=pst, b_=b_: e.tensor_tensor(out=o, in0=pst[:, 0:2], in1=b_, op=ALU.add), [kps, "modbT" + sfx], ["modT" + sfx])
        if blk >= 4:
            half = blk - 4
            pst = kb.ps[4 + half]
            kps = "ps%d" % (4 + half)
            for k in range(8):
                MM(kb, pst[:, 0:512], csrep[:, k, :], w[:, k, :], k == 0, k == 7, [kw, "csrep" + sfx], [kps])
            o = gate_rep[:, half * 512:(half + 1) * 512]
            b_ = gb[:, half * 512:(half + 1) * 512]
            kb.v(lambda e, o=o, pst=pst, b_=b_: e.tensor_tensor(out=o, in0=pst[:, 0:512], in1=b_, op=ALU.add), [kps, "gate_rep" + sfx], ["gate_rep" + sfx])
    sc = modT[:, 8:16, :]
    kb.v(lambda e: e.tensor_scalar(out=sc, in0=sc, scalar1=1.0, scalar2=None, op0=ALU.add), ["modT" + sfx], ["modT" + sfx])


def adaln(kb, D):
    modT = r3(kb.f32(48), 24, 2)
    gate_rep = kb.f32(1024)
    m0 = kb.mark()
    pt0 = kb.ptop
    wb = [r3(kb.bf(8 * 512), 8, 512) for _ in range(6)]
    A = adaln_loads(kb, D, "", wb, r3(kb.bf(16), 8, 2), r3(kb.bf(8 * 128), 8, 128), gate_rep)
    adaln_compute(kb, A, modT, gate_rep)
    kb.P.barrier()
    kb.release(m0)
    kb.ptop = pt0
    return modT, gate_rep


def make_hmod(kb, xsrc, modT, ci, hm, xt, ident):
    kb.dma(xt, xsrc.rearrange("(s p) d -> p s d", p=128), w=["xt"])
    for s in range(4):
        for k in range(8):
            pi = (s * 8 + k) % 8
            pst = kb.ps[pi]
            kps = "ps%d" % pi
            src = xt[:, s, k * 128:(k + 1) * 128]
            kb.t(lambda e, pst=pst, src=src: e.transpose(pst[:, 0:128], src, ident), ["xt", "ident"], [kps])
            o = hm[:, k, s * 128:(s + 1) * 128]
            ACT(kb, o, pst[:, 0:128], AF.Identity, [kps, "modT"], ["hm"], scale=modT[:, 8 + k, ci:ci + 1], bias=modT[:, k, ci:ci + 1])


def resid_ln(kb, y_ps_list, xtile, gate_rep, g_rep, b_rep, dst_dram, tmp, tag, sfx=""):
    t1, t2, st, mv, rs = tmp
    K = lambda n: "rl%s_%s" % (sfx, n)
    for h in range(2):
        pst, kps = y_ps_list[h]
        sl = slice(h * 512, (h + 1) * 512)
        kb.v(lambda e, pst=pst, sl=sl: e.scalar_tensor_tensor(out=t2[:, sl], in0=xtile[:, sl], scalar=DN_ALPHA, in1=pst[:, 0:512], op0=ALU.mult, op1=ALU.add),
             [kps, tag + "x" + sfx, K("t10"), K("t11")], [K("t2%d" % h)])
        kb.v(lambda e, sl=sl, h=h: e.bn_stats(out=st[:, h * 6:(h + 1) * 6], in_=t2[:, sl]), [K("t2%d" % h)], [K("st%d" % h)])
    kb.v(lambda e: e.bn_aggr(out=mv, in_=st), [K("st0"), K("st1")], [K("mv")])
    kb.v(lambda e: e.tensor_scalar(out=rs, in0=mv[:, 1:2], scalar1=LN_EPS, scalar2=None, op0=ALU.add), [K("mv")], [K("rs")])
    ACT(kb, rs, rs, AF.Sqrt, [K("rs")], [K("rs")])
    kb.v(lambda e: e.reciprocal(out=rs, in_=rs), [K("rs")], [K("rs")])
    nb = st[:, 0:1]
    kb.v(lambda e: e.scalar_tensor_tensor(out=nb, in0=mv[:, 0:1], scalar=-1.0, in1=rs, op0=ALU.mult, op1=ALU.mult), [K("mv"), K("rs"), K("st0"), K("st1")], [K("nb")])
    ACT(kb, t2, t2, AF.Identity, [K("t20"), K("t21"), K("nb"), K("rs")], [K("t20"), K("t21")], scale=rs, bias=nb)
    kb.v(lambda e: e.tensor_tensor(out=t2, in0=t2, in1=g_rep, op=ALU.mult), [K("t20"), K("t21"), "g_rep"], [K("t20"), K("t21")])
    kb.g(lambda e: e.tensor_tensor(out=t1, in0=t2, in1=b_rep, op=ALU.add), [K("t20"), K("t21"), "b_rep"], [K("t10"), K("t11")])
    kb.dma(dst_dram, t1, r=[K("t10"), K("t11")])


def build12(stage):
    assert stage == "F"
    S1 = stage in (1, "F")
    S2 = stage in (2, "F")
    nc = bass.Bass("TRN2", target_bir_lowering=False)
    D = {}

    def inp(name, shape):
        D[name] = nc.dram_tensor(name, shape, F32, kind="ExternalInput").ap()
    for nm in ["lamre", "lamim", "logdt", "dT"]:
        inp(nm, [128, 32])
    for nm in ["bre", "bim", "cre", "cim"]:
        inp(nm, [128, 512])
    inp("selF", [128, 8 * 240]); inp("ident", [128, 128]); inp("maskF", [128, 128]); inp("maskB", [128, 128])
    inp("x", [NTOK, 1024]); inp("cT", [128, 16]); inp("modw", [1024, 3072]); inp("modbT", [128, 24]); inp("modb_gate_rep", [128, 1024])
    inp("w_in", [1024, 2560])
    if S1:
        inp("ctx", [256, 1024])
    if stage == 1:
        FIN = nc.dram_tensor("FIN", [128, 64], F32, kind="ExternalOutput").ap()
        CFIN = nc.dram_tensor("CFIN", [128, 64], F32, kind="ExternalOutput").ap()
    if stage == 2:
        inp("init", [128, 64])
    if S2:
        inp("g_rep", [128, 1024]); inp("b_rep", [128, 1024])
        inp("glu_w", [512, 512]); inp("glu_bT", [128, 4]); inp("lng_rep", [128, 512]); inp("lnb_rep", [128, 512])
        inp("wsT", [128, 1024]); inp("bs_tm", [128, 8]); inp("w_out", [1024, 1024])
        X1 = nc.dram_tensor("X1", [NTOK, 1024], F32, kind=("ExternalOutput" if (stage == 2 or DEBUG_F) else "Internal")).ap()
        HM = nc.dram_tensor("HM", [4, 128, 8 * 512], BF16, kind="Internal").ap()
    if stage == "F":
        inp("mhalf", [128, 2])
        D1 = {}
        for nm, shp in [("cT", [128, 16]), ("modw", [1024, 3072]), ("modbT", [128, 24]), ("modb_gate_rep", [128, 1024]),
                        ("w_in", [1024, 3072]), ("w_out", [1024, 1024]), ("g_rep", [128, 1024]), ("b_rep", [128, 1024]),
                        ("dwT", [128, 8 * 31]), ("dwbT", [128, 8]), ("clngT", [128, 8]), ("clnbT", [128, 8])]:
            D1[nm] = nc.dram_tensor(nm + "_1", shp, F32, kind="ExternalInput").ap()
        D1["ident"] = D["ident"]
        D1["mhalf"] = D["mhalf"]
        OUT = nc.dram_tensor("OUT", [NTOK, 1024], F32, kind="ExternalOutput").ap()
        HM3 = nc.dram_tensor("HM3", [4, 128, 8 * 512], BF16, kind="Internal").ap()
        CV3 = nc.dram_tensor("CV3", [8, 128, NTOK], F32, kind="Internal").ap()
        DBGI = nc.dram_tensor("DBGI", [128, 64], F32, kind="ExternalOutput").ap() if DEBUG_F else None
        DBG2 = nc.dram_tensor("DBG2", [128, 64 * 3 + 1024 + 1024], F32, kind="ExternalOutput").ap() if DEBUG_F else None
        QD = nc.dram_tensor("QD", [16, 2, 128, 2 * J_OWN], F32, kind="Internal").ap()
        CCI1 = nc.dram_tensor("cci1", [128, 64], F32, kind="Internal").ap()
        CCO1 = nc.dram_tensor("cco1", [256, 64], F32, kind="Internal").ap()
        CCI2 = nc.dram_tensor("cci2", [128, 8192], BF16, kind="Internal").ap()
        CCO2 = nc.dram_tensor("cco2", [256, 8192], BF16, kind="Internal").ap()
    with contextlib.ExitStack() as es:
        kb = KB(nc, es, F32_ARENA, BF_ARENA)
        for nm in ["cc0", "cc1"]:
            kb.sems[nm] = es.enter_context(nc.semaphore(nm))
            kb.P.cnt[nm] = 0
        kb.boot_clear()
        block = es.enter_context(nc.Block())
        ygT = r3(kb.ab[:, BF_ARENA - 4 * NTOK:BF_ARENA], 4, NTOK) if S2 else None
        ident = kb.f32(128)
        kb.dma(ident, D["ident"], w=["ident"])
        modT = r3(kb.f32(48), 24, 2); gate_rep = kb.f32(1024)
        modT1 = r3(kb.f32(48), 24, 2); gate_rep1 = kb.f32(1024)
        mPers = kb.mark()
        ad = []
        for li, DD in enumerate([D, D1]):
            wb = [r3(kb.ab[:, (6 * li + i) * 4096:(6 * li + i + 1) * 4096], 8, 512) for i in range(6)]
            base = 49152 + li * 1040
            ad.append(adaln_loads(kb, DD, "_a%d" % li, wb, r3(kb.ab[:, base:base + 16], 8, 2), r3(kb.ab[:, base + 16:base + 1040], 8, 128), [gate_rep, gate_rep1][li]))
        kb.P.barrier_skip = tuple("gpsimd_d%d" % i for i in range(NDMASEM))
        cst, pctx = s5_pre_init(kb, D, need_out=S2)
        kb.P.barrier_skip = ()
        adaln_compute(kb, ad[0], modT, gate_rep)
        adaln_compute(kb, ad[1], modT1, gate_rep1)
        kb.P.barrier()
        selF = r3(kb.bf(8 * 240), 8, 240)
        kb.dma(selF.rearrange("p a b -> p (a b)"), D["selF"], w=["selF"], q="gpsimd")
        X = r3(kb.bf(G * J_OWN), G, J_OWN)
        init = kb.f32(64)
        fin = kb.f32(64)
        NWO = J_OWN // W
        Qre = r3(kb.f32(G * NWO), G, NWO); Qim = r3(kb.f32(G * NWO), G, NWO)
        car_re = r3(kb.f32(G * (NWO + 1)), G, NWO + 1); car_im = r3(kb.f32(G * (NWO + 1)), G, NWO + 1)
        QCre = r3(kb.f32(G), G, 1); QCim = r3(kb.f32(G), G, 1)
        cC_re = r3(kb.f32(G * 2), G, 2); cC_im = r3(kb.f32(G * 2), G, 2)
        if S1:
            Xc = r3(kb.ab[:, BF_ARENA - G * 64:BF_ARENA], G, 64)
            cfin = kb.f32(64)
        mP2 = kb.mark()
        hm = r3(kb.bf(8 * 512), 8, 512)
        wua = r3(kb.bf(8 * 512), 8, 512)
        kb.dma(wua, D["w_in"].rearrange("(k p) c -> p k c", p=128)[:, :, 0:512], w=["wua"], q="gpsimd")
        NTC = NTOK + 256
        uaT = r3(kb.bf(4 * NTC), 4, NTC)
        xt = r3(kb.f32(1 * 1024), 1, 1024)

        def ua_proj(dst, ncol0, n=512):
            for cb in range(4):
                pst = kb.ps[cb]
                kps = "ps%d" % cb
                for k in range(8):
                    MM(kb, pst[:, 0:n], wua[:, k, cb * 128:(cb + 1) * 128], hm[:, k, 0:n], k == 0, k == 7, ["wua", "hm"], [kps])
                o = dst[:, cb, ncol0:ncol0 + n]
                ACT(kb, o, pst[:, 0:n], AF.Copy, [kps], ["LuaT"])

        def hmod_sub(src_rows, s, ci, buf):
            kx = "xt%d" % buf
            kb.dma(xt[:, buf, :], src_rows, w=[kx])
            for k in range(8):
                pst = kb.ps[k % 4]
                kps = "ps%d" % (k % 4)
                src = xt[:, buf, k * 128:(k + 1) * 128]
                kb.t(lambda e, pst=pst, src=src: e.transpose(pst[:, 0:128], src, ident), [kx, "ident"], [kps])
                o = hm[:, k, s * 128:(s + 1) * 128]
                ACT(kb, o, pst[:, 0:128], AF.Identity, [kps, "modT"], ["hm"], scale=modT[:, 8 + k, ci:ci + 1], bias=modT[:, k, ci:ci + 1])

        def p2_stream():
            for tt_ in range(4):
                for s in range(4):
                    r0 = tt_ * 512 + s * 128
                    hmod_sub(D["x"][r0:r0 + 128, :], s, 0, 0)
                    yield
                if S2:
                    kb.dma(HM[tt_], hm.rearrange("p a b -> p (a b)"), r=["hm"], w=["HM%d" % tt_])
                ua_proj(uaT, tt_ * 512)
                yield
            for s in range(2):
                hmod_sub(D["ctx"][s * 128:(s + 1) * 128, :], s, 1, 0)
            yield
            ua_proj(uaT, NTOK, 256)
            yield
            JT = NTC // 8
            for g in range(G):
                pt, gl = g // 8, g % 8
                pst = kb.ps[g % 4]
                kps = "ps%d" % (g % 4)
                for s in range(8):
                    MM(kb, pst[:, 0:JT], selF[:, gl, 16 * (7 - s):16 * (7 - s) + 128], uaT[:, pt, s::8], s == 0, s == 7, ["selF", "LuaT"], [kps])
                ACT(kb, X[:, g, :], pst[:, 0:J_OWN], AF.Copy, [kps], ["LX"])
                ACT(kb, Xc[:, g, 0:32], pst[:, J_OWN:JT], AF.Copy, [kps], ["CX"])
                yield
        interleave([s5_pre_batches(kb, pctx), p2_stream()])
        kb.P.barrier()
        kb.ptop = pctx["ptop0"]
        if S1:
            kb.P.barrier()
            kb.release(mP2)
            kb.v(lambda e: e.memset(init, 0.0), [], ["Cinit"])
            s5_pass1(kb, cst, Xc[:, :, 0:32], 32, QCre, QCim, "C", GB=8)
            s5_carries(kb, cst, QCre, QCim, init[:, 0:32], init[:, 32:64], cC_re, cC_im, 1, "C")
            kb.v(lambda e: e.tensor_copy(out=cfin[:, 0:32], in_=cC_re[:, :, 1]), [], ["Cfin"])
            kb.v(lambda e: e.tensor_copy(out=cfin[:, 32:64], in_=cC_im[:, :, 1]), [], ["Cfin"])
            kb.v(lambda e: e.tensor_copy(out=init, in_=cfin), ["Cfin"], ["Linit"])
            s5_pass1(kb, cst, X, J_OWN, Qre, Qim, "L", QD=QD)
            s5_carries(kb, cst, Qre, Qim, init[:, 0:32], init[:, 32:64], car_re, car_im, NWO, "L")
            kb.v(lambda e: e.tensor_copy(out=fin[:, 0:32], in_=car_re[:, :, NWO]), [], ["Lfin"])
            kb.v(lambda e: e.tensor_copy(out=fin[:, 32:64], in_=car_im[:, :, NWO]), [], ["Lfin"])
            if stage == 1:
                kb.dma(FIN, fin, r=["Lfin"])
                kb.P.barrier()
                print("stage1 nops", kb.P.nops, "arena", kb.pf, kb.pb)
                kb.P.emit(kb.sems, block)
                return nc
            if DEBUG_F:
                mdd = kb.mark()
                dd = kb.f32(64 * 3 + 2048)
                kb.v(lambda e: e.tensor_copy(out=dd[:, 0:64], in_=cfin), [], ["dd"])
                kb.v(lambda e: e.tensor_copy(out=dd[:, 64:128], in_=fin), [], ["dd"])
                kb.v(lambda e: e.tensor_copy(out=dd[:, 128:160], in_=cst["lamWre"]), [], ["dd"])
                kb.v(lambda e: e.tensor_copy(out=dd[:, 160:192], in_=cst["lamWim"]), [], ["dd"])
                kb.v(lambda e: e.tensor_copy(out=r3(dd[:, 192:192 + 1024], 32, 32), in_=Xc[:, :, 0:32]), [], ["dd"])
                kb.v(lambda e: e.tensor_copy(out=r3(dd[:, 1216:1216 + 1024], 32, 32), in_=X[:, :, 0:32]), [], ["dd"])
                kb.dma(DBG2, dd, r=["dd"])
                kb.P.barrier()
                kb.release(mdd)
            mh = kb.f32(2)
            kb.dma(mh, D["mhalf"], w=["mh"])
            kb.dma(CCI1, fin, r=["Lfin"], w=["cci1"])
            kb.P.barrier()
            kb.cc(lambda e: e.collective_compute("AllGather", ALU.bypass, replica_groups=[[0, 1], [2, 3], [4, 5], [6, 7]], ins=[CCI1], outs=[CCO1]),
                  "cc0", [], ["cco1"])
            kb.P.barrier()
            g01 = r3(kb.f32(128), 2, 64)
            kb.dma(g01, CCO1.rearrange("(r p) c -> p r c", p=128), r=["cco1"], w=["g01"])
            dtmp = kb.f32(64)
            kb.v(lambda e: e.tensor_tensor(out=dtmp[0:64, :], in0=g01[0:64, 0, :], in1=cfin[0:64, :], op=ALU.subtract), ["g01", "Cfin"], ["dtmp"])
            kb.v(lambda e: e.scalar_tensor_tensor(out=init[0:64, :], in0=dtmp[0:64, :], scalar=mh[0:64, 0:1], in1=cfin[0:64, :], op0=ALU.mult, op1=ALU.add),
                 ["dtmp", "mh", "Cfin"], ["Linit"])
            kb.v(lambda e: e.tensor_tensor(out=dtmp[64:128, :], in0=cfin[64:128, :], in1=g01[64:128, 1, :], op=ALU.subtract), ["g01", "Cfin"], ["dtmp"])
            kb.v(lambda e: e.scalar_tensor_tensor(out=init[64:128, :], in0=dtmp[64:128, :], scalar=mh[64:128, 0:1], in1=g01[64:128, 1, :], op0=ALU.mult, op1=ALU.add),
                 ["dtmp", "mh", "g01"], ["Linit"])
            if DEBUG_F:
                kb.dma(DBGI, init, r=["Linit"])
            kb.P.barrier()
            s5_carries(kb, cst, Qre, Qim, init[:, 0:32], init[:, 32:64], car_re, car_im, NWO, "L")
        kb.P.barrier()
        if stage == 2:
            kb.release(mP2)
            kb.dma(init, D["init"], w=["Linit"])
        b8flat = cst["B8"].rearrange("p g r m -> p (g r m)")
        SPsets = [(r3(kb.bf(16 * J_OWN), 16, J_OWN), r3(kb.bf(16 * J_OWN), 16, J_OWN)),
                  (r3(b8flat[:, 0:16 * J_OWN], 16, J_OWN), r3(b8flat[:, 16 * J_OWN:32 * J_OWN], 16, J_OWN))]
        Yg = r3(kb.bf(16 * J_OWN), 16, J_OWN)

        def s5_half_out(half):
            g0, g1 = 16 * half, 16 * half + 16
            SPre, SPim = SPsets[half]
            tg = "L%d" % half

            def consume(g, pst, kps):
                o = Yg[:, g - g0, :]
                ACT(kb, o, pst[:, 0:J_OWN], AF.Gelu, [kps], ["Yg"])
            s5_output(kb, cst, X, SPre, SPim, J_OWN, consume, tg, g0, g1)
            for ptl in range(2):
                pt = 2 * half + ptl
                for t in range(8):
                    pi = t % 8
                    pst = kb.ps[pi]
                    kps = "ps%d" % pi
                    for gl in range(8):
                        lhs = selF[:, t, 16 * (7 - gl):16 * (7 - gl) + 128]
                        rhs = Yg[:, ptl * 8 + gl, :]
                        MM(kb, pst[:, 0:J_OWN], lhs, rhs, gl == 0, gl == 7, ["selF", "Yg"], [kps])
                    o = ygT[:, pt, t::8]
                    ACT(kb, o, pst[:, 0:J_OWN], AF.Copy, [kps], ["ygT"])
        s5_pass2(kb, cst, X, J_OWN, car_re, car_im, SPsets[0][0], SPsets[0][1], "L0", 0, 16, QD=QD, car_tag="L")
        s5_half_out(0)
        s5_pass2(kb, cst, X, J_OWN, car_re, car_im, SPsets[1][0], SPsets[1][1], "L1", 16, 32, QD=QD, car_tag="L", pre_barrier=False)
        s5_half_out(1)
        kb.P.barrier()
        kb.pb = 0
        kb.pf = mPers[0]
        kb.ptop = kb.capf
        wv = D["w_in"].rearrange("(k p) c -> p k c", p=128)
        wout = r3(kb.bf(8 * 1024), 8, 1024)
        gluw = r3(kb.bf(4 * 512), 4, 512)
        wsT = r3(kb.bf(8 * 128), 8, 128)
        wsec = [r3(kb.bf(8 * 512), 8, 512) for _ in range(4)]
        hms = [r3(kb.bf(8 * 512), 8, 512)]
        act = r3(kb.bf(8 * 512), 8, 512)
        vtm4 = r3(kb.bf(4 * 512), 4, 512)
        utm4 = r3(kb.bf(4 * 512), 4, 512)
        ztm4 = r3(kb.bf(4 * 512), 4, 512)
        g_rep = kb.f32(1024); kb.dma(g_rep, D["g_rep"], w=["g_rep"])
        b_rep = kb.f32(1024); kb.dma(b_rep, D["b_rep"], w=["b_rep"])
        lng = kb.f32(512); kb.dma(lng, D["lng_rep"], w=["lng"])
        lnb = kb.f32(512); kb.dma(lnb, D["lnb_rep"], w=["lnb"])
        glub = kb.f32(4); kb.dma(glub, D["glu_bT"], w=["glub"])
        bstm = kb.f32(8); kb.dma(bstm, D["bs_tm"], w=["bstm"])
        sza = r3(kb.f32(4 * 512), 4, 512)
        sigs = [kb.f32(512) for _ in range(2)]
        gv4 = r3(kb.f32(4 * 512), 4, 512)
        rsets = [(kb.f32(1024), kb.f32(1024), kb.f32(12), kb.f32(2), kb.f32(1)) for _ in range(2)]
        xtiles = [kb.f32(1024) for _ in range(2)]
        sm2 = [(kb.f32(6), kb.f32(2), kb.f32(1)) for _ in range(4)]

        def load_w(i, c0):
            kb.dma(wsec[i], wv[:, :, c0:c0 + 512], w=["wsec%d" % i], q="gpsimd")
            return wsec[i], "wsec%d" % i
        wvv, kwv = load_w(2, 1536)
        wza, kza = load_w(0, 512)
        kb.dma(gluw, D["glu_w"].rearrange("(k p) c -> p k c", p=128), w=["gluw"], q="gpsimd")
        wu, kwu = load_w(1, 1024)
        kb.dma(wsT.rearrange("p a b -> p (a b)"), D["wsT"], w=["wsT"], q="gpsimd")
        wz, kwz = load_w(3, 2048)
        kb.dma(wout, D["w_out"].rearrange("(k p) c -> p k c", p=128), w=["wout"], q="gpsimd")
        kb.v(lambda e: e.tensor_tensor(out=wout, in0=wout, in1=gate_rep.unsqueeze(1).broadcast_to([128, 8, 1024]), op=ALU.mult), ["wout"], ["wout"])
        for tt_ in range(4):
            tsl = slice(tt_ * 512, (tt_ + 1) * 512)
            hm = hms[0]
            khm = "hm0"
            assert kb.pb <= BF_ARENA - 4 * NTOK, kb.pb
            if tt_ == 0:
                kb.dma(hm.rearrange("p a b -> p (a b)"), HM[0], w=[khm])
            EV = [0, 2, 4, 6]
            for s in range(4):
                ssl = slice(s * 128, (s + 1) * 128)
                pst = kb.ps[EV[s]]; kps = "ps%d" % EV[s]
                for k in range(8):
                    MM(kb, pst[:, 0:512], hm[:, k, ssl], wvv[:, k, :], k == 0, k == 7, [kwv, khm], [kps])
                ACT(kb, gv4[:, s, :], pst[:, 0:512], AF.Gelu, [kps], ["gv%d" % s])
            for s in range(4):
                st2, mv2, rs2 = sm2[s]
                gv = gv4[:, s, :]
                kg = "gv%d" % s
                ks = "sm%d" % s
                kb.v(lambda e, st2=st2, gv=gv: e.bn_stats(out=st2, in_=gv), [kg], [ks])
                kb.v(lambda e, st2=st2, mv2=mv2: e.bn_aggr(out=mv2, in_=st2), [ks], [ks])
                kb.v(lambda e, mv2=mv2, rs2=rs2: e.tensor_scalar(out=rs2, in0=mv2[:, 1:2], scalar1=LN_EPS, scalar2=None, op0=ALU.add), [ks], [ks])
                ACT(kb, rs2, rs2, AF.Sqrt, [ks], [ks])
                kb.v(lambda e, rs2=rs2: e.reciprocal(out=rs2, in_=rs2), [ks], [ks])
                kb.v(lambda e, gv=gv, mv2=mv2, rs2=rs2: e.tensor_scalar(out=gv, in0=gv, scalar1=mv2[:, 0:1], scalar2=rs2, op0=ALU.subtract, op1=ALU.mult), [kg, ks], [kg])
                kb.g(lambda e, gv=gv: e.tensor_tensor(out=gv, in0=gv, in1=lng, op=ALU.mult), [kg, "lng"], [kg])
                o = vtm4[:, s, :]
                kb.g(lambda e, gv=gv, o=o: e.tensor_tensor(out=o, in0=gv, in1=lnb, op=ALU.add), [kg, "lnb"], ["vtm%d" % s])
            for cb in range(4):
                pst = kb.ps[EV[cb]]; kps = "ps%d" % EV[cb]
                for k in range(8):
                    MM(kb, pst[:, 0:512], wza[:, k, cb * 128:(cb + 1) * 128], hm[:, k, :], k == 0, k == 7, [kza, khm], [kps])
                ACT(kb, sza[:, cb, :], pst[:, 0:512], AF.Silu, [kps], ["sza%d" % cb])
            for cb in range(4):
                pst = kb.ps[EV[cb]]; kps = "ps%d" % EV[cb]
                sig = sigs[cb % 2]; ksig = "sig%d" % (cb % 2)
                for k in range(4):
                    MM(kb, pst[:, 0:512], gluw[:, k, cb * 128:(cb + 1) * 128], ygT[:, k, tsl], k == 0, k == 3, ["gluw", "ygT"], [kps])
                ACT(kb, sig, pst[:, 0:512], AF.Sigmoid, [kps, "glub"], [ksig], bias=glub[:, cb:cb + 1])
                yv = ygT[:, cb, tsl]
                kb.v(lambda e, yv=yv, sig=sig: e.tensor_tensor(out=sig, in0=sig, in1=yv, op=ALU.mult), [ksig, "ygT"], [ksig])
                o = act[:, cb, :]
                zz = sza[:, cb, :]
                kb.g(lambda e, o=o, zz=zz, sig=sig: e.tensor_tensor(out=o, in0=sig, in1=zz, op=ALU.mult), [ksig, "sza%d" % cb], ["actA%d" % cb])
            if tt_ == 3 and stage == "F":
                wa_pref = r3(kb.ab[:, BF_ARENA - 8 * 1024:BF_ARENA], 8, 1024)
                kb.dma(wa_pref, D1["w_in"].rearrange("(k p) c -> p k c", p=128)[:, :, 0:1024], w=["ygT", "wa"], q="gpsimd")
            for s in range(4):
                ssl = slice(s * 128, (s + 1) * 128)
                pst = kb.ps[2 * s + 1]; kps = "ps%d" % (2 * s + 1)
                for k in range(8):
                    MM(kb, pst[:, 0:512], hm[:, k, ssl], wu[:, k, :], k == 0, k == 7, [kwu, khm], [kps])
                ACT(kb, utm4[:, s, :], pst[:, 0:512], AF.Gelu, [kps], ["utm%d" % s])
            for s in range(4):
                ssl = slice(s * 128, (s + 1) * 128)
                pst = kb.ps[2 * s]; kps = "ps%d" % (2 * s)
                for h in range(8):
                    MM(kb, pst[:, 64 * h:64 * h + 64], wsT[:, h, :], vtm4[:, s, 64 * h:64 * h + 64], True, True, ["wsT", "vtm%d" % s], [kps])
                sgo = gv4[:, s, :]
                kg = "gv%d" % s
                kb.v(lambda e, pst=pst, sgo=sgo: e.tensor_tensor(out=r3(sgo, 8, 64), in0=r3(pst[:, 0:512], 8, 64), in1=bc_last(bstm, 64), op=ALU.add),
                     [kps, "bstm", "vtm%d" % s], [kg])
                us = utm4[:, s, :]
                kb.v(lambda e, sgo=sgo, us=us: e.tensor_tensor(out=sgo, in0=sgo, in1=us, op=ALU.mult), [kg, "utm%d" % s], [kg])
                pz = kb.ps[2 * s + 1]; kpz = "ps%d" % (2 * s + 1)
                for k in range(8):
                    MM(kb, pz[:, 0:512], hm[:, k, ssl], wz[:, k, :], k == 0, k == 7, [kwz, khm], [kpz])
                zt = ztm4[:, s, :]
                ACT(kb, zt, pz[:, 0:512], AF.Silu, [kpz], ["ztm%d" % s])
                kb.g(lambda e, sgo=sgo, zt=zt: e.tensor_tensor(out=sgo, in0=sgo, in1=zt, op=ALU.mult), [kg, "ztm%d" % s], [kg])
            if tt_ < 3:
                kb.dma(hm.rearrange("p a b -> p (a b)"), HM[tt_ + 1], w=[khm])
            for s in range(4):
                ssl = slice(s * 128, (s + 1) * 128)
                pst = kb.ps[2 * s]; kps = "ps%d" % (2 * s)
                for cb in range(4):
                    src = gv4[:, s, cb * 128:(cb + 1) * 128]
                    kb.t(lambda e, pst=pst, src=src, cb=cb: e.transpose(pst[:, cb * 128:(cb + 1) * 128], src, ident), ["gv%d" % s, "ident"], [kps])
                o = act[:, 4:8, ssl]
                ACT(kb, o, r3(pst[:, 0:512], 4, 128), AF.Copy, [kps], ["actB%d" % s])
            actk = ["actA%d" % i for i in range(4)] + ["actB%d" % i for i in range(4)]
            kb.dma(xtiles[0], D["x"][tt_ * 512:tt_ * 512 + 128, :], w=["Ox0"])
            for s in range(4):
                ssl = slice(s * 128, (s + 1) * 128)
                row0 = tt_ * 512 + s * 128
                bi = s % 2
                xtile = xtiles[bi]
                if s < 3:
                    kb.dma(xtiles[1 - bi], D["x"][row0 + 128:row0 + 256, :], w=["Ox%d" % (1 - bi)])
                ys = []
                for h in range(2):
                    pst = kb.ps[(2 * s + 1 + 2 * h) % 8]; kps = "ps%d" % ((2 * s + 1 + 2 * h) % 8)
                    for cb in range(8):
                        MM(kb, pst[:, 0:512], act[:, cb, ssl], wout[:, cb, h * 512:(h + 1) * 512], cb == 0, cb == 7, actk + ["wout"], [kps])
                    ys.append((pst, kps))
                resid_ln(kb, ys, xtile, gate_rep, g_rep, b_rep, X1[row0:row0 + 128, :], rsets[bi], "O", sfx=str(bi))
        kb.P.barrier()
        if stage == "F":
            kb.pf = mPers[0]
            kb.pb = 0
            kb.ptop = kb.capf
            layer1(kb, D1, X1, OUT, HM3, CV3, (CCI2, CCO2), ident, modT1, gate_rep1, wa_pref)
        print("stage", stage, "nops", kb.P.nops, "arena", kb.pf, kb.pb)
        kb.P.emit(kb.sems, block)
    return nc


def layer1(kb, D, x1src, OUT, HM, CV, ccb, ident, modT, gate_rep, wa_pref=None):
    CCI2, CCO2 = ccb
    mh = kb.f32(2)
    kb.dma(mh, D["mhalf"], w=["mh"])
    Hrow = r4(kb.bf(4 * 32 * 94), 4, 32, 94)
    Hcol = r4(kb.bf(4 * 64 * 64), 4, 64, 64)
    kb.v(lambda e: e.memset(Hrow, 0.0), [], ["Hrow"])
    mB = kb.mark()
    wv = D["w_in"].rearrange("(k p) c -> p k c", p=128)
    wg = r3(kb.bf(8 * 1024), 8, 1024)
    kb.dma(wg, wv[:, :, 1024:2048], w=["wg"], q="gpsimd")
    if wa_pref is not None:
        wa = wa_pref
    else:
        wa = r3(kb.bf(8 * 1024), 8, 1024)
        kb.dma(wa, wv[:, :, 0:1024], w=["wa"], q="gpsimd")
    hm = r3(kb.bf(8 * 512), 8, 512)
    xt = r3(kb.f32(4 * 1024), 4, 1024)
    sig = kb.f32(512)
    for tw in range(4):
        make_hmod(kb, x1src[tw * 512:(tw + 1) * 512, :], modT, 0, hm, xt, ident)
        kb.dma(HM[tw], hm.rearrange("p a b -> p (a b)"), r=["hm"], w=["HMw%d" % tw])
        for cb in range(8):
            pa = kb.ps[(cb % 4) * 2]; ka = "ps%d" % ((cb % 4) * 2)
            pg = kb.ps[(cb % 4) * 2 + 1]; kg = "ps%d" % ((cb % 4) * 2 + 1)
            for k in range(8):
                MM(kb, pa[:, 0:512], wa[:, k, cb * 128:(cb + 1) * 128], hm[:, k, :], k == 0, k == 7, ["wa", "hm"] + ([ka] if k else []), [ka])
            for k in range(8):
                MM(kb, pg[:, 0:512], wg[:, k, cb * 128:(cb + 1) * 128], hm[:, k, :], k == 0, k == 7, ["wg", "hm"] + ([kg] if k else []), [kg])
            ACT(kb, sig, pg[:, 0:512], AF.Sigmoid, [kg], ["sig"])
            if cb < 4:
                o = Hrow[:, cb, tw * 8:(tw + 1) * 8, 15:79]
                kk = "Hrow"
            else:
                o = Hcol[:, cb - 4, 16 + tw * 8:16 + (tw + 1) * 8, :]
                kk = "Hcol"
            kb.v(lambda e, o=o, pa=pa: e.tensor_tensor(out=o, in0=r3(pa[:, 0:512], 8, 64), in1=r3(sig, 8, 64), op=ALU.mult), [ka, "sig"], [kk])
    ci = CCI2.rearrange("p (s c r) -> p s c r", s=2, c=4)
    kb.dma(ci[:, 0], Hcol[:, :, 16:32, :].rearrange("p c r w -> p c (r w)"), r=["Hcol"], w=["cci2"])
    kb.dma(ci[:, 1], Hcol[:, :, 32:48, :].rearrange("p c r w -> p c (r w)"), r=["Hcol"], w=["cci2"])
    kb.P.barrier()
    kb.release(mB)
    kb.cc(lambda e: e.collective_compute("AllGather", ALU.bypass, replica_groups=[[0, 1], [2, 3], [4, 5], [6, 7]], ins=[CCI2], outs=[CCO2]),
          "cc1", [], ["cco2"])

    def halo_in():
        co = CCO2.rearrange("(r p) (s c x) -> r p s c x", p=128, s=2, c=4)
        kb.dma(Hcol[:, :, 0:16, :].rearrange("p c r w -> p c (r w)"), co[0, :, 1], r=["cco2"], w=["HcolT"])
        kb.dma(Hcol[:, :, 48:64, :].rearrange("p c r w -> p c (r w)"), co[1, :, 0], r=["cco2"], w=["HcolB"])
        top = Hcol[:, :, 0:16, :]
        bot = Hcol[:, :, 48:64, :]
        kb.v(lambda e: e.tensor_scalar(out=top, in0=top, scalar1=mh[:, 0:1], scalar2=None, op0=ALU.mult), ["HcolT", "mh"], ["HcolT"])
        kb.v(lambda e: e.tensor_scalar(out=bot, in0=bot, scalar1=mh[:, 1:2], scalar2=None, op0=ALU.mult), ["HcolB", "mh"], ["HcolB"])
    layer1_tail(kb, D, x1src, 0, OUT, HM, CV, Hrow, Hcol, ident, gate_rep, halo_in)


def layer1_tail(kb, D, xsrc, xoff, OUT, HM, CV, Hrow, Hcol, ident, gate_rep, halo_in=None):
    if True:
        mC = kb.mark()
        identb = kb.bf(128)
        kb.v(lambda e: e.tensor_copy(out=identb, in_=ident), ["ident"], ["identb"])
        dwT = r3(kb.f32(8 * 31), 8, 31)
        kb.dma(dwT.rearrange("p a b -> p (a b)"), D["dwT"], w=["dwT"])
        dwb = kb.f32(8); kb.dma(dwb, D["dwbT"], w=["dwb"])
        dgs = [r3(kb.bf(31 * 128), 31, 128) for _ in range(2)]
        cvo = [kb.f32(512) for _ in range(2)]
        ncv = 0
        for cb in range(8):
            if cb == 4 and halo_in is not None:
                halo_in()
            wk = dwT[:, cb, :].unsqueeze(2).broadcast_to([128, 31, 128])
            ib = identb.unsqueeze(1).broadcast_to([128, 31, 128])
            dg = dgs[cb % 2]
            kb.g(lambda e, wk=wk, dg=dg: e.tensor_tensor(out=dg, in0=ib, in1=wk, op=ALU.mult), ["identb", "dwT"], ["dg%d" % (cb % 2)])
            dgk = ["dg%d" % (cb % 2)]
            for t4 in range(4):
                pst = kb.ps[t4]; kps = "ps%d" % t4
                for k in range(31):
                    if cb < 4:
                        rhs = Hrow[:, cb, t4 * 8:(t4 + 1) * 8, k:k + 64]
                    else:
                        r0 = t4 * 8 + k + 1
                        rhs = Hcol[:, cb - 4, r0:r0 + 8, :]
                    MM(kb, pst[:, 0:512], dg[:, k, :], rhs, k == 0, k == 30, dgk + (["Hrow"] if cb < 4 else ["Hcol", "HcolT", "HcolB"]), [kps])
                c = cvo[ncv % 2]; kc = "cvo%d" % (ncv % 2); ncv += 1
                ACT(kb, c, pst[:, 0:512], AF.Identity, [kps, "dwb"], [kc], bias=dwb[:, cb:cb + 1])
                kb.dma(CV[cb][:, t4 * 512:(t4 + 1) * 512], c, r=[kc], w=["CV"])
        kb.P.barrier()
        kb.release(mC)
        kb.pb = 0
        wout = r3(kb.bf(8 * 1024), 8, 1024)
        kb.dma(wout, D["w_out"].rearrange("(k p) c -> p k c", p=128), w=["wout"], q="gpsimd")
        kb.v(lambda e: e.tensor_tensor(out=wout, in0=wout, in1=gate_rep.unsqueeze(1).broadcast_to([128, 8, 1024]), op=ALU.mult), ["wout"], ["wout"])
        wz = [r3(kb.bf(8 * 512), 8, 512) for _ in range(2)]
        hm = r3(kb.bf(8 * 512), 8, 512)
        act = r3(kb.bf(8 * 512), 8, 512)
        ones = kb.f32(128)
        kb.v(lambda e: e.memset(ones, 1.0), [], ["ones"])
        g_rep = kb.f32(1024); kb.dma(g_rep, D["g_rep"], w=["g_rep"])
        b_rep = kb.f32(1024); kb.dma(b_rep, D["b_rep"], w=["b_rep"])
        clng = kb.f32(8); kb.dma(clng, D["clngT"], w=["clng"])
        clnb = kb.f32(8); kb.dma(clnb, D["clnbT"], w=["clnb"])
        cvt = r3(kb.f32(8 * 512), 8, 512)
        mean = kb.f32(512); rstd = kb.f32(512)
        sqs = [kb.f32(512) for _ in range(2)]
        nrms = [kb.f32(512) for _ in range(2)]
        szs = [kb.f32(512) for _ in range(2)]
        f32b = lambda n: kb.bf(2 * n).bitcast(F32)
        rsets = [(kb.f32(1024), kb.f32(1024), kb.f32(12), kb.f32(2), kb.f32(1)),
                 (f32b(1024), f32b(1024), kb.f32(12), kb.f32(2), kb.f32(1))]
        xtiles = [kb.f32(1024), f32b(1024)]
        wzv = D["w_in"].rearrange("(k p) c -> p k c", p=128)
        for zh in range(2):
            kb.dma(wz[zh], wzv[:, :, 2048 + zh * 512:2048 + (zh + 1) * 512], w=["wz%d" % zh], q="gpsimd")
        def c2_loads(t4):
            kb.dma(hm.rearrange("p a b -> p (a b)"), HM[t4], w=["hm"])
            for cb in range(8):
                kb.dma(cvt[:, cb, :], CV[cb][:, t4 * 512:(t4 + 1) * 512], w=["cvt%d" % cb])
        p1 = kb.ps[6]; p2 = kb.ps[7]

        def c2_stats_mm(cbs):
            for cb in cbs:
                sq = sqs[cb % 2]; ksq = "sq%d" % (cb % 2)
                MM(kb, p1[:, 0:512], ones, cvt[:, cb, :], cb == 0, cb == 7, ["ones", "cvt%d" % cb], ["ps6"])
                ACT(kb, sq, cvt[:, cb, :], AF.Square, ["cvt%d" % cb], [ksq])
                MM(kb, p2[:, 0:512], ones, sq, cb == 0, cb == 7, ["ones", ksq], ["ps7"])
        c2_loads(0)
        c2_stats_mm(range(8))
        for t4 in range(4):
            sq = sqs[0]
            kb.v(lambda e: e.tensor_scalar(out=mean, in0=p1[:, 0:512], scalar1=1.0 / 1024, scalar2=None, op0=ALU.mult), ["ps6"], ["mean"])
            kb.v(lambda e, sq=sq: e.tensor_tensor(out=sq, in0=mean, in1=mean, op=ALU.mult), ["mean"], ["sq0"])
            kb.v(lambda e, sq=sq: e.scalar_tensor_tensor(out=rstd, in0=p2[:, 0:512], scalar=1.0 / 1024, in1=sq, op0=ALU.mult, op1=ALU.subtract), ["ps7", "sq0"], ["rstd"])
            kb.v(lambda e: e.tensor_scalar(out=rstd, in0=rstd, scalar1=LN_EPS, scalar2=None, op0=ALU.add), ["rstd"], ["rstd"])
            ACT(kb, rstd, rstd, AF.Sqrt, ["rstd"], ["rstd"])
            kb.v(lambda e: e.reciprocal(out=rstd, in_=rstd), ["rstd"], ["rstd"])
            for cb in range(8):
                w = wz[cb // 4]; kw = "wz%d" % (cb // 4)
                pst = kb.ps[cb % 4]; kps = "ps%d" % (cb % 4)
                nrm = nrms[cb % 2]; kn = "nrm%d" % (cb % 2)
                sz = szs[cb % 2]; kz = "sz%d" % (cb % 2)
                for k in range(8):
                    MM(kb, pst[:, 0:512], w[:, k, (cb % 4) * 128:(cb % 4 + 1) * 128], hm[:, k, :], k == 0, k == 7, [kw, "hm"], [kps])
                ACT(kb, sz, pst[:, 0:512], AF.Silu, [kps], [kz])
                kb.v(lambda e, cb=cb, nrm=nrm: e.tensor_tensor(out=nrm, in0=cvt[:, cb, :], in1=mean, op=ALU.subtract), ["cvt%d" % cb, "mean"], [kn])
                kb.v(lambda e, nrm=nrm: e.tensor_tensor(out=nrm, in0=nrm, in1=rstd, op=ALU.mult), [kn, "rstd"], [kn])
                ACT(kb, nrm, nrm, AF.Silu, [kn, "clng", "clnb"], [kn], scale=clng[:, cb:cb + 1], bias=clnb[:, cb:cb + 1])
                o = act[:, cb, :]
                kb.g(lambda e, o=o, nrm=nrm, sz=sz: e.tensor_tensor(out=o, in0=nrm, in1=sz, op=ALU.mult), [kn, kz], ["act%d" % cb])
            actk = ["act%d" % i for i in range(8)]
            if t4 < 3:
                c2_loads(t4 + 1)
            kb.dma(xtiles[0], xsrc[xoff + t4 * 512:xoff + t4 * 512 + 128, :], w=["Ox0"])
            for s in range(4):
                ssl = slice(s * 128, (s + 1) * 128)
                row0 = t4 * 512 + s * 128
                bi = s % 2
                xtile = xtiles[bi]
                if s < 3:
                    kb.dma(xtiles[1 - bi], xsrc[xoff + row0 + 128:xoff + row0 + 256, :], w=["Ox%d" % (1 - bi)])
                ys = []
                for h in range(2):
                    pi = 4 + (2 * s + h) % 2 if False else (4 + h if s % 2 == 0 else 2 + h)
                    pst = kb.ps[pi]; kps = "ps%d" % pi
                    for cb in range(8):
                        MM(kb, pst[:, 0:512], act[:, cb, ssl], wout[:, cb, h * 512:(h + 1) * 512], cb == 0, cb == 7, actk + ["wout"], [kps])
                    ys.append((pst, kps))
                if t4 < 3:
                    c2_stats_mm(range(2 * s, 2 * s + 2))
                resid_ln(kb, ys, xtile, gate_rep, g_rep, b_rep, OUT[row0:row0 + 128, :], rsets[bi], "O", sfx=str(bi))


def build3():
    nc = bass.Bass("TRN2", target_bir_lowering=False)
    D = {}

    def inp(name, shape):
        D[name] = nc.dram_tensor(name, shape, F32, kind="ExternalInput").ap()
    inp("xw", [4096, 1024]); inp("maskw", [128, 4096]); inp("ident", [128, 128])
    inp("cT", [128, 16]); inp("modw", [1024, 3072]); inp("modbT", [128, 24]); inp("modb_gate_rep", [128, 1024])
    inp("w_in", [1024, 3072]); inp("w_out", [1024, 1024]); inp("g_rep", [128, 1024]); inp("b_rep", [128, 1024])
    inp("dwT", [128, 8 * 31]); inp("dwbT", [128, 8]); inp("clngT", [128, 8]); inp("clnbT", [128, 8])
    OUT = nc.dram_tensor("OUT", [NTOK, 1024], F32, kind="ExternalOutput").ap()
    HM = nc.dram_tensor("HM3", [4, 128, 8 * 512], BF16, kind="Internal").ap()
    CV = nc.dram_tensor("CV3", [8, 128, NTOK], F32, kind="Internal").ap()
    with contextlib.ExitStack() as es:
        kb = KB(nc, es, F32_ARENA, BF_ARENA)
        kb.boot_clear()
        block = es.enter_context(nc.Block())
        ident = kb.f32(128)
        kb.dma(ident, D["ident"], w=["ident"])
        modT, gate_rep = adaln(kb, D)
        Hrow = r4(kb.bf(4 * 32 * 94), 4, 32, 94)
        Hcol = r4(kb.bf(4 * 64 * 64), 4, 64, 64)
        kb.v(lambda e: e.memset(Hrow, 0.0), [], ["Hrow"])
        mB = kb.mark()
        wv = D["w_in"].rearrange("(k p) c -> p k c", p=128)
        wa = r3(kb.bf(8 * 1024), 8, 1024)
        wg = r3(kb.bf(8 * 1024), 8, 1024)
        kb.dma(wa, wv[:, :, 0:1024], w=["wa"], q="gpsimd")
        kb.dma(wg, wv[:, :, 1024:2048], w=["wg"], q="gpsimd")
        hm = r3(kb.bf(8 * 512), 8, 512)
        xt = r3(kb.f32(4 * 1024), 4, 1024)
        msk = kb.f32(512)
        sig = kb.f32(512)
        for tw in range(8):
            own = 2 <= tw < 6
            make_hmod(kb, D["xw"][tw * 512:(tw + 1) * 512, :], modT, 0, hm, xt, ident)
            if own:
                kb.dma(HM[tw - 2], hm.rearrange("p a b -> p (a b)"), r=["hm"], w=["HMw%d" % tw])
            kb.dma(msk, D["maskw"][:, tw * 512:(tw + 1) * 512], w=["msk"])
            for cb in (range(8) if own else range(4, 8)):
                pa = kb.ps[(cb % 4) * 2]; ka = "ps%d" % ((cb % 4) * 2)
                pg = kb.ps[(cb % 4) * 2 + 1]; kg = "ps%d" % ((cb % 4) * 2 + 1)
                for k in range(8):
                    MM(kb, pa[:, 0:512], wa[:, k, cb * 128:(cb + 1) * 128], hm[:, k, :], k == 0, k == 7, ["wa", "hm"] + ([ka] if k else []), [ka])
                for k in range(8):
                    MM(kb, pg[:, 0:512], wg[:, k, cb * 128:(cb + 1) * 128], hm[:, k, :], k == 0, k == 7, ["wg", "hm"] + ([kg] if k else []), [kg])
                ACT(kb, sig, pg[:, 0:512], AF.Sigmoid, [kg], ["sig"])
                kb.v(lambda e, pa=pa: e.tensor_tensor(out=sig, in0=pa[:, 0:512], in1=sig, op=ALU.mult), [ka, "sig"], ["sig"])
                if cb < 4:
                    o = Hrow[:, cb, (tw - 2) * 8:(tw - 1) * 8, 15:79]
                    kb.g(lambda e, o=o: e.tensor_tensor(out=o, in0=r3(sig, 8, 64), in1=r3(msk, 8, 64), op=ALU.mult), ["sig", "msk"], ["Hrow"])
                else:
                    o = Hcol[:, cb - 4, tw * 8:(tw + 1) * 8, :]
                    kb.g(lambda e, o=o: e.tensor_tensor(out=o, in0=r3(sig, 8, 64), in1=r3(msk, 8, 64), op=ALU.mult), ["sig", "msk"], ["Hcol"])
        kb.P.barrier()
        kb.release(mB)
        layer1_tail(kb, D, D["xw"], 1024, OUT, HM, CV, Hrow, Hcol, ident, gate_rep)
        kb.P.barrier()
        print("stage3 nops", kb.P.nops, "arena", kb.pf, kb.pb)
        kb.P.emit(kb.sems, block)
    return nc


def host_consts():
    selF = np.zeros((128, 8, 240), np.float32)
    for gl in range(8):
        for h in range(16):
            selF[16 * gl + h, gl, 112 + h] = 1.0
    ident = np.eye(128, dtype=np.float32)
    si = np.arange(128) // 16
    maskF = (si[None, :] >= si[:, None]).astype(np.float32)
    maskB = (si[None, :] <= si[:, None]).astype(np.float32)
    return dict(selF=selF.reshape(128, -1), ident=ident, maskF=maskF, maskB=maskB)


def s5_host_layout(lam_re, lam_im, log_dt, b_re, b_im, c_re, c_im, d):
    o = {}
    o["lamre"] = lam_re.transpose(0, 2, 1).reshape(128, 32)
    o["lamim"] = lam_im.transpose(0, 2, 1).reshape(128, 32)
    o["logdt"] = np.repeat(log_dt[:, None, :], 64, axis=1).reshape(128, 32)
    o["bre"] = b_re.transpose(0, 2, 1, 3).reshape(128, 512)
    o["bim"] = b_im.transpose(0, 2, 1, 3).reshape(128, 512)
    o["cre"] = c_re.transpose(0, 3, 1, 2).reshape(128, 512)
    o["cim"] = c_im.transpose(0, 3, 1, 2).reshape(128, 512)
    o["dT"] = np.tile(d.reshape(32, 16).T, (8, 1))
    return {k: np.ascontiguousarray(v, dtype=np.float32) for k, v in o.items()}


def colT(v, n):
    return np.ascontiguousarray(np.asarray(v, np.float32).reshape(n, 128).T)


def rep(v):
    return np.ascontiguousarray(np.tile(np.asarray(v, np.float32)[None, :], (128, 1)))


_NC = {}


def _get(stage):
    if stage not in _NC:
        _NC[stage] = build12(stage) if stage in (1, 2, "F") else build3()
    return _NC[stage]


def kernel(x, c, ctx, c_ctx, mod_w, mod_b, norm_g, norm_b, ev_w_in, ev_w_out,
           s5_lam_re, s5_lam_im, s5_log_dt, s5_b_re, s5_b_im, s5_c_re, s5_c_im, s5_d,
           glu_w, glu_b, sgu_ln_g, sgu_ln_b, sgu_w, sgu_b,
           od_w_in, od_w_out, dw_w, dw_b, conv_ln_g, conv_ln_b):
    A = lambda a: np.asarray(a, dtype=np.float32)
    x, c, ctx, c_ctx, mod_w, mod_b = A(x), A(c), A(ctx), A(c_ctx), A(mod_w), A(mod_b)
    cores = list(range(8))
    base = dict(host_consts())
    base.update(s5_host_layout(A(s5_lam_re)[0], A(s5_lam_im)[0], A(s5_log_dt)[0], A(s5_b_re)[0], A(s5_b_im)[0],
                               A(s5_c_re)[0], A(s5_c_im)[0], A(s5_d)[0]))

    def cT_of(b):
        return np.ascontiguousarray(np.stack([c[b].reshape(8, 128).T, c_ctx.reshape(8, 128).T], axis=2).reshape(128, 16))

    def mod_of(l, sfx=""):
        return {"modw" + sfx: np.ascontiguousarray(mod_w[l]), "modbT" + sfx: colT(mod_b[l], 24),
                "modb_gate_rep" + sfx: rep(mod_b[l][2048:3072])}
    shared = dict(base)
    shared.update(mod_of(0))
    shared.update(mod_of(1, "_1"))
    shared.update(w_in=np.ascontiguousarray(A(ev_w_in)[0]), g_rep=rep(A(norm_g)[0]), b_rep=rep(A(norm_b)[0]),
                  glu_w=np.ascontiguousarray(A(glu_w)[0]), glu_bT=colT(A(glu_b)[0], 4),
                  lng_rep=rep(A(sgu_ln_g)[0]), lnb_rep=rep(A(sgu_ln_b)[0]),
                  wsT=np.ascontiguousarray(A(sgu_w)[0].transpose(2, 0, 1).reshape(128, 1024)),
                  bs_tm=np.ascontiguousarray(A(sgu_b)[0].T), w_out=np.ascontiguousarray(A(ev_w_out)[0]),
                  w_in_1=np.ascontiguousarray(A(od_w_in)[0]), w_out_1=np.ascontiguousarray(A(od_w_out)[0]),
                  g_rep_1=rep(A(norm_g)[1]), b_rep_1=rep(A(norm_b)[1]),
                  dwT_1=np.ascontiguousarray(A(dw_w)[0].T.reshape(8, 128, 31).transpose(1, 0, 2).reshape(128, 8 * 31)),
                  dwbT_1=colT(A(dw_b)[0], 8), clngT_1=colT(A(conv_ln_g)[0], 8), clnbT_1=colT(A(conv_ln_b)[0], 8))
    maps = []
    for core in cores:
        b, half = core // 2, core % 2
        m = dict(shared)
        mh = np.zeros((128, 2), np.float32)
        mh[:, 0] = float(half)
        mh[:, 1] = 1.0 - float(half)
        m.update(x=np.ascontiguousarray(x[b, half * 2048:(half + 1) * 2048]), ctx=np.ascontiguousarray(ctx[b]),
                 cT=cT_of(b), cT_1=cT_of(b), mhalf=mh)
        maps.append(m)
    res = run_bass_kernel_spmd(_get("F"), maps, core_ids=cores).results
    out = np.zeros((4, 4096, 1024), np.float32)
    for core in cores:
        b, half = core // 2, core % 2
        out[b, half * 2048:(half + 1) * 2048] = res[core]["OUT"]
    return out
```
